# Optimizing a Trainium2 kernel written in Bass

```python
import math
import jax, jax.numpy as jnp
from jax import lax
import numpy as np

D_MODEL = 1024
BATCH = 2
SEQ = 8192
DEPTH = 4

CHUNK = 64
QBLK = 128
A_HEADS = 4
A_DQK = 64
A_DV = 2 * A_DQK
B_HEADS = 4
B_DH = 128
C_HEADS = 4
C_DK = 128
C_DV = 256
ROPE_BASE = 10000.0
N_BRANCH = 3
D_FF = 2816
CONV_W = 3
LN_EPS = 1e-5
ALPHA = (2 * DEPTH) ** 0.25
BETA = (8 * DEPTH) ** -0.25

SPLITS = (A_HEADS * 2 * A_DQK, A_HEADS * 2 * A_DQK, A_HEADS * A_DV,
          B_HEADS * B_DH, B_HEADS * B_DH, B_HEADS * B_DH, B_HEADS,
          C_HEADS * C_DK, C_HEADS * C_DK, C_HEADS * C_DV, C_HEADS * C_DV,
          N_BRANCH * D_MODEL)
D_IN = int(sum(SPLITS))
SPLIT_IDX = tuple(int(v) for v in np.cumsum(SPLITS)[:-1])

kernel_name = "hybrid_diff_fox_retention_convffn_deepnorm_adaln"


def _layernorm(x, g, b):
    xf = x.astype(jnp.float32)
    mu = jnp.mean(xf, -1, keepdims=True)
    var = jnp.mean(jnp.square(xf - mu), -1, keepdims=True)
    return ((xf - mu) * lax.rsqrt(var + LN_EPS) * g + b).astype(x.dtype)


def _head_rmsnorm(o, g):
    of = o.astype(jnp.float32)
    of = of * lax.rsqrt(jnp.mean(jnp.square(of), -1, keepdims=True) + LN_EPS)
    return of * g.reshape(o.shape[2], o.shape[3])


def _head_groupnorm(o, g):
    of = o.astype(jnp.float32)
    mu = jnp.mean(of, -1, keepdims=True)
    var = jnp.mean(jnp.square(of - mu), -1, keepdims=True)
    return (of - mu) * lax.rsqrt(var + LN_EPS) * g.reshape(o.shape[2], o.shape[3])


def _query_blocks(t):
    t = t.reshape((t.shape[0], t.shape[1] // QBLK, QBLK) + t.shape[2:])
    return jnp.moveaxis(t, 1, 0)


def _merge_blocks(o):
    o = jnp.moveaxis(o, 0, 1)
    return o.reshape((o.shape[0], o.shape[1] * o.shape[2]) + o.shape[3:])


def _diff_attention(q, k, v, lam):
    S_ = q.shape[1]
    scale = A_DQK ** -0.5
    k_chunk = jnp.arange(S_) // CHUNK

    def block(args):
        qb, i = args
        s = jnp.einsum('bqhmd,bkhmd->bhmqk', qb, k).astype(jnp.float32) * scale
        q_chunk = (i * QBLK + jnp.arange(QBLK)) // CHUNK
        mask = k_chunk[None, :] <= q_chunk[:, None]
        p = jax.nn.softmax(jnp.where(mask, s, -jnp.inf), axis=-1)
        a = p[:, :, 0] - lam * p[:, :, 1]
        return jnp.einsum('bhqk,bkhe->bqhe', a.astype(v.dtype), v)

    o = lax.map(block, (_query_blocks(q), jnp.arange(S_ // QBLK)))
    return _merge_blocks(o)


def _forgetting_attention(q, k, v, log_f):
    S_ = q.shape[1]
    scale = B_DH ** -0.5
    F = jnp.cumsum(log_f, axis=1)
    Fk = jnp.transpose(F, (0, 2, 1))[:, :, None, :]
    pos = jnp.arange(S_)

    def block(args):
        qb, Fq, i = args
        s = jnp.einsum('bqhd,bkhd->bhqk', qb, k).astype(jnp.float32) * scale
        s = s + jnp.transpose(Fq, (0, 2, 1))[..., None] - Fk
        q_pos = i * QBLK + jnp.arange(QBLK)
        mask = pos[None, :] <= q_pos[:, None]
        p = jax.nn.softmax(jnp.where(mask, s, -jnp.inf), axis=-1)
        return jnp.einsum('bhqk,bkhd->bqhd', p.astype(v.dtype), v)

    o = lax.map(block, (_query_blocks(q), _query_blocks(F), jnp.arange(S_ // QBLK)))
    return _merge_blocks(o)


def _rotary(t, pos):
    half = t.shape[-1] // 2
    inv = 1.0 / (ROPE_BASE ** jnp.linspace(0.0, 1.0, half, dtype=jnp.float32))
    ang = pos.astype(jnp.float32)[:, None] * inv[None, :]
    cos, sin = jnp.cos(ang)[None, :, None, :], jnp.sin(ang)[None, :, None, :]
    tf = t.astype(jnp.float32)
    t1, t2 = tf[..., :half], tf[..., half:]
    return jnp.concatenate([t1 * cos - t2 * sin, t1 * sin + t2 * cos], axis=-1)


def _retention(q, k, v):
    B_, S_ = q.shape[0], q.shape[1]
    nc = S_ // CHUNK
    log_gamma = jnp.log(1.0 - 2.0 ** (-5.0 - jnp.arange(C_HEADS, dtype=jnp.float32)))
    pos = jnp.arange(S_)
    qc = _rotary(q, pos).reshape(B_, nc, CHUNK, C_HEADS, C_DK)
    kc = (_rotary(k, pos) * C_DK ** -0.5).reshape(B_, nc, CHUNK, C_HEADS, C_DK)
    vc = v.astype(jnp.float32).reshape(B_, nc, CHUNK, C_HEADS, C_DV)
    i = jnp.arange(CHUNK, dtype=jnp.float32)
    diff = i[:, None] - i[None, :]
    dmask = jnp.where(diff >= 0, jnp.exp(log_gamma[:, None, None] * jnp.maximum(diff, 0.0)), 0.0)
    s = jnp.einsum('bnihd,bnjhd->bnhij', qc, kc) * dmask
    inner = jnp.einsum('bnhij,bnjhe->bnihe', s, vc)
    q_dec = qc * jnp.exp(log_gamma[None, :] * (i[:, None] + 1.0))[:, :, None]
    k_dec = kc * jnp.exp(log_gamma[None, :] * (CHUNK - 1.0 - i[:, None]))[:, :, None]
    chunk_decay = jnp.exp(log_gamma * CHUNK)[None, :, None, None]

    def step(R, xs):
        qd, kd, vv = xs
        cross = jnp.einsum('bihd,bhde->bihe', qd, R)
        R = chunk_decay * R + jnp.einsum('bjhd,bjhe->bhde', kd, vv)
        return R, cross

    R0 = jnp.zeros((B_, C_HEADS, C_DK, C_DV), jnp.float32)
    _, cross = lax.scan(step, R0, (jnp.moveaxis(q_dec, 1, 0), jnp.moveaxis(k_dec, 1, 0),
                                   jnp.moveaxis(vc, 1, 0)))
    o = inner + jnp.moveaxis(cross, 0, 1)
    return o.reshape(B_, S_, C_HEADS, C_DV)


def _token_mixer(h, w_in, b_in, lam, lam_init, g_diff, g_ret, w_pa, w_pb, w_pc, w_out):
    B_, S_, _ = h.shape
    z = h @ w_in + b_in
    aq, ak, av, bq, bk, bv, bf, cq, ck, cv, cg, gates = jnp.split(z, SPLIT_IDX, axis=-1)
    ya = _diff_attention(aq.reshape(B_, S_, A_HEADS, 2, A_DQK), ak.reshape(B_, S_, A_HEADS, 2, A_DQK),
                         av.reshape(B_, S_, A_HEADS, A_DV), lam)
    ya = (_head_rmsnorm(ya, g_diff) * (1.0 - lam_init)).astype(h.dtype).reshape(B_, S_, -1)
    yb = _forgetting_attention(bq.reshape(B_, S_, B_HEADS, B_DH), bk.reshape(B_, S_, B_HEADS, B_DH),
                               bv.reshape(B_, S_, B_HEADS, B_DH),
                               jax.nn.log_sigmoid(bf.astype(jnp.float32)))
    yb = yb.reshape(B_, S_, -1)
    yc = _retention(cq.reshape(B_, S_, C_HEADS, C_DK), ck.reshape(B_, S_, C_HEADS, C_DK),
                    cv.reshape(B_, S_, C_HEADS, C_DV))
    yc = _head_groupnorm(yc, g_ret).astype(h.dtype).reshape(B_, S_, -1) * jax.nn.silu(cg)
    ga, gb, gc = jnp.split(jax.nn.sigmoid(gates), N_BRANCH, axis=-1)
    m = ga * (ya @ w_pa) + gb * (yb @ w_pb) + gc * (yc @ w_pc)
    return m @ w_out


def _conv_ffn(h, w_up, w_conv, b_conv, w_down):
    S_ = h.shape[1]
    u, g = jnp.split(h @ w_up, 2, axis=-1)
    up = jnp.pad(u, ((0, 0), (CONV_W - 1, 0), (0, 0)))
    conv = b_conv + up[:, 0:S_] * w_conv[0]
    for j in range(1, CONV_W):
        conv = conv + up[:, j:j + S_] * w_conv[j]
    return (jax.nn.gelu(conv, approximate=False) * g) @ w_down


def setup_inputs(seed: int = 0) -> dict:
    key = jax.random.key(seed)
    ks = jax.random.split(key, 24)
    nrm = lambda k, shape, s: jax.random.normal(k, shape, jnp.float32) * s
    D = D_MODEL
    return {
        "x": nrm(ks[0], (BATCH, SEQ, D), 1.0),
        "c": nrm(ks[1], (BATCH, D), 1.0),
        "w_ada": nrm(ks[2], (DEPTH, D, 6 * D), D ** -0.5),
        "b_ada": nrm(ks[3], (DEPTH, 6 * D), 0.02),
        "w_in": nrm(ks[4], (DEPTH, D, D_IN), D ** -0.5),
        "b_in": nrm(ks[5], (DEPTH, D_IN), 0.02),
        "lam_q1": nrm(ks[6], (DEPTH, A_DQK), 0.1),
        "lam_k1": nrm(ks[7], (DEPTH, A_DQK), 0.1),
        "lam_q2": nrm(ks[8], (DEPTH, A_DQK), 0.1),
        "lam_k2": nrm(ks[9], (DEPTH, A_DQK), 0.1),
        "g_diff": 1.0 + nrm(ks[10], (DEPTH, A_HEADS * A_DV), 0.02),
        "g_ret": 1.0 + nrm(ks[11], (DEPTH, C_HEADS * C_DV), 0.02),
        "w_pa": nrm(ks[12], (DEPTH, A_HEADS * A_DV, D), (A_HEADS * A_DV) ** -0.5),
        "w_pb": nrm(ks[13], (DEPTH, B_HEADS * B_DH, D), (B_HEADS * B_DH) ** -0.5),
        "w_pc": nrm(ks[14], (DEPTH, C_HEADS * C_DV, D), (C_HEADS * C_DV) ** -0.5),
        "w_out": nrm(ks[15], (DEPTH, D, D), BETA * D ** -0.5),
        "ln_g": 1.0 + nrm(ks[16], (DEPTH, 2, D), 0.02),
        "ln_b": nrm(ks[17], (DEPTH, 2, D), 0.02),
        "w_up": nrm(ks[18], (DEPTH, D, 2 * D_FF), D ** -0.5),
        "w_conv": nrm(ks[19], (DEPTH, CONV_W, D_FF), CONV_W ** -0.5),
        "b_conv": nrm(ks[20], (DEPTH, D_FF), 0.02),
        "w_down": nrm(ks[21], (DEPTH, D_FF, D), BETA * D_FF ** -0.5),
    }


def reference(x, c, w_ada, b_ada, w_in, b_in, lam_q1, lam_k1, lam_q2, lam_k2, g_diff, g_ret,
              w_pa, w_pb, w_pc, w_out, ln_g, ln_b, w_up, w_conv, b_conv, w_down):
    c_act = jax.nn.silu(c)
    for l in range(DEPTH):
        mod = c_act @ w_ada[l] + b_ada[l]
        sh1, sc1, gt1, sh2, sc2, gt2 = [m[:, None, :] for m in jnp.split(mod, 6, axis=-1)]
        lam_init = 0.8 - 0.6 * math.exp(-0.3 * l)
        lam = (jnp.exp(jnp.sum(lam_q1[l].astype(jnp.float32) * lam_k1[l].astype(jnp.float32)))
               - jnp.exp(jnp.sum(lam_q2[l].astype(jnp.float32) * lam_k2[l].astype(jnp.float32)))
               + lam_init)
        h = x * (1.0 + sc1) + sh1
        y = _token_mixer(h, w_in[l], b_in[l], lam, lam_init, g_diff[l], g_ret[l],
                         w_pa[l], w_pb[l], w_pc[l], w_out[l])
        x = _layernorm(ALPHA * x + gt1 * y, ln_g[l, 0], ln_b[l, 0])
        h = x * (1.0 + sc2) + sh2
        y = _conv_ffn(h, w_up[l], w_conv[l], b_conv[l], w_down[l])
        x = _layernorm(ALPHA * x + gt2 * y, ln_g[l, 1], ln_b[l, 1])
    return x
```

```python
import math
from contextlib import ExitStack

import numpy as np
import ml_dtypes

import concourse.bass as bass
import concourse.mybir as mybir
from concourse.bass_utils import run_bass_kernel_spmd

F32 = mybir.dt.float32
BF16 = mybir.dt.bfloat16
AF = mybir.ActivationFunctionType
ALU = mybir.AluOpType
AX = mybir.AxisListType

P = 128
D = 1024
KC = 8
DEPTH = 4
NT = 2048
TS = 512
NSEG = 4
DFF = 2816
NFC = 22
LN_EPS = 1e-5
ALPHA = (2 * DEPTH) ** 0.25
AQ, AK, AV, BQ, BK, BV, BFc, CQ, CK, CV, CG, GT, DIN = 0, 512, 1024, 1536, 2048, 2560, 3072, 3076, 3588, 4100, 5124, 6148, 9220
SCALE_A = 64 ** -0.5
SCALE_B = 128 ** -0.5
NEG = -30000.0
GAM = [1.0 - 2.0 ** (-5.0 - h) for h in range(4)]
G128 = [g ** 128 for g in GAM]
G512 = [g ** 512 for g in GAM]
KVW = 2048 + 2048 + 2080 + 2080
SMW = 64 + 4096


class Prog:
    ENGS = ("pe", "act", "dve", "pool", "sp")

    def __init__(self):
        self.ops = []

    def add(self, eng, fn, rd=(), wr=(), dma=None, inc=16):
        self.ops.append(dict(eng=eng, fn=fn, rd=list(rd), wr=list(wr), dma=dma, inc=inc))

    def barrier(self):
        self.ops.append(dict(barrier=True))

    @staticmethod
    def _split(key):
        if isinstance(key, tuple):
            return key[0], key[1]
        return key, None

    def resolve(self):
        state = {}
        ops = self.ops
        last_on_eng = {}
        last_dma = {}
        pend = {e: set() for e in self.ENGS}

        def touch(key):
            name, idx = self._split(key)
            d = state.setdefault(name, {})
            if idx is None:
                d.setdefault(None, dict(w=None, r={}))
                return list(d.values())
            d.setdefault(idx, dict(w=None, r={}))
            res = [d[idx]]
            if None in d:
                res.append(d[None])
            return res

        for i, op in enumerate(ops):
            if op.get("barrier"):
                allp = set(last_on_eng.values()) | set(last_dma.values())
                for e in self.ENGS:
                    pend[e] |= allp
                continue
            deps = set()
            for k in op["rd"]:
                for st in touch(k):
                    if st["w"] is not None:
                        deps.add(st["w"])
            for k in op["wr"]:
                for st in touch(k):
                    if st["w"] is not None:
                        deps.add(st["w"])
                    deps.update(st["r"].values())
            for k in op["rd"]:
                for st in touch(k):
                    st["r"][op["eng"] if not op["dma"] else ("dma", op["dma"])] = i
            for k in op["wr"]:
                name, idx = self._split(k)
                for st in touch(k):
                    st["w"] = i
                    st["r"] = {}
            deps |= pend[op["eng"]]
            pend[op["eng"]] = set()
            deps.discard(i)
            op["deps"] = deps
            if op["dma"]:
                last_dma[op["dma"]] = i
            else:
                last_on_eng[op["eng"]] = i
        for op in ops:
            if op.get("barrier"):
                continue
            op["signal"] = bool(op["dma"])
        for op in ops:
            if op.get("barrier"):
                continue
            for p in op["deps"]:
                po = ops[p]
                if not po["dma"] and po["eng"] == "pe" and op["eng"] == "pe" and not op["dma"]:
                    continue
                po["signal"] = True
        cnt = {}
        dma_hist = {}
        for i, op in enumerate(ops):
            if op.get("barrier"):
                continue
            if op["dma"]:
                k = ("dma", op["dma"])
                cnt[k] = cnt.get(k, 0) + op["inc"]
                op["sigval"] = cnt[k]
                dma_hist.setdefault(op["dma"], []).append((i, cnt[k]))
            elif op["signal"]:
                k = ("eng", op["eng"])
                cnt[k] = cnt.get(k, 0) + 1
                op["sigval"] = cnt[k]
        self.dma_hist = dma_hist
        self.final = {k[1]: v for k, v in cnt.items() if k[0] == "dma"}

    def emit(self, nc, block, sems):
        ops = self.ops
        import bisect
        hist_idx = {k: [h[0] for h in v] for k, v in self.dma_hist.items()}

        def run(engname, e):
            waited = {}
            for i, op in enumerate(ops):
                if op.get("barrier") or op["eng"] != engname:
                    continue
                need = {}
                for p in op["deps"]:
                    po = ops[p]
                    if po["dma"]:
                        key = ("dma", po["dma"])
                        hi = hist_idx[po["dma"]]
                        pos = bisect.bisect_left(hi, i) - 1
                        val = self.dma_hist[po["dma"]][pos][1]
                    else:
                        if po["eng"] == "pe" and engname == "pe" and not op["dma"]:
                            continue
                        key = ("eng", po["eng"])
                        val = po["sigval"]
                    if val > need.get(key, 0):
                        need[key] = val
                for key, val in need.items():
                    if waited.get(key, 0) >= val:
                        continue
                    e.wait_ge(sems[key], val)
                    waited[key] = val
                ins = op["fn"](e)
                if op["dma"]:
                    ins.then_inc(sems[("dma", op["dma"])], op["inc"])
                elif op["signal"]:
                    ins.then_inc(sems[("eng", engname)], 1)
            if engname == "sp":
                for k, v in self.final.items():
                    if waited.get(("dma", k), 0) < v:
                        e.wait_ge(sems[("dma", k)], v)

        block.tensor(lambda e: run("pe", e))
        block.scalar(lambda e: run("act", e))
        block.vector(lambda e: run("dve", e))
        block.gpsimd(lambda e: run("pool", e))
        block.sync(lambda e: run("sp", e))


def pack_cols(w, blocks):
    K = w.shape[0]
    kc = K // P
    parts = []
    for c0, wd in blocks:
        blk = w[:, c0:c0 + wd].reshape(kc, P, wd).transpose(1, 0, 2).reshape(P, kc * wd)
        parts.append(blk)
    return np.ascontiguousarray(np.concatenate(parts, axis=1))


def win_blocks():
    blocks = []
    index = {}
    off = 0

    def addg(name, c0, width, n):
        nonlocal off
        index[name] = (off, width, n)
        for i in range(n):
            blocks.append((c0 + i * width, width))
            off += KC * width

    addg("ak", AK, 128, 4)
    addg("bk", BK, 128, 4)
    addg("av", AV, 512, 1)
    addg("bv", BV, 512, 1)
    addg("ck", CK, 512, 1)
    addg("cv", CV, 512, 2)
    addg("aq", AQ, 128, 4)
    addg("bq", BQ, 128, 4)
    addg("cq", CQ, 512, 1)
    addg("cg", CG, 128, 8)
    index["gt"] = (off, 128, 24)
    for dc in range(8):
        for wh in range(3):
            blocks.append((GT + wh * 1024 + dc * 128, 128))
            off += KC * 128
    index["bf"] = (off, 4, 1)
    blocks.append((BFc, 4))
    off += KC * 4
    return blocks, index, off


WIN_BLOCKS, WIN_IDX, WIN_TOT = win_blocks()
ROW_OFF = {}
_o = 0
for _n, _c0, _w in (("av", AV, 512), ("bv", BV, 512), ("bf", BFc, 4), ("ck", CK, 512), ("cv", CV, 1024), ("cq", CQ, 512)):
    ROW_OFF[_n] = (_o, _c0, _w)
    _o += _w
ROW_TOT = _o + (_o % 2)
VEC = {}
_o = 0
for _n, _w in (("b_ada", 48), ("b_fm", 48), ("ln_g", 16), ("ln_b", 16), ("w_conv", 66), ("b_conv", 22), ("g_diff", 4), ("g_ret", 8)):
    VEC[_n] = (_o, _w)
    _o += _w
VEC_TOT = _o
BRO_TOT = 256
CST = {}
_o = 0
for _n, _w in (("ident", 128), ("tri", 128), ("ones", 128), ("causal", 128), ("cos", 16 * 64), ("sin", 16 * 64), ("kdec", 4), ("gpow", 4),
               ("alpha", 4), ("beta", 4), ("sel", 4), ("selA", 4), ("selB", 4), ("sel4", 512)):
    CST[_n] = (_o, _w)
    _o += _w
CST_TOT = _o
CSTB = {"ident": (0, 128), "diagA": (128, 2048), "diagB": (2176, 2048), "ones": (4224, 128)}
CSTB_TOT = 4352


def host_constants(j):
    cst = np.zeros((P, CST_TOT), np.float32)
    cstb = np.zeros((P, CSTB_TOT), np.float32)

    def put(name, arr):
        o, w = CST[name]
        cst[:, o:o + w] = arr.reshape(P, w)

    i = np.arange(P)
    put("ident", np.eye(P, dtype=np.float32))
    put("tri", (i[:, None] <= i[None, :]).astype(np.float32))
    put("ones", np.ones((P, P), np.float32))
    put("causal", (i[:, None] <= i[None, :]).astype(np.float32))
    half = 64
    inv = (1.0 / (np.float32(10000.0) ** np.linspace(0.0, 1.0, half, dtype=np.float32))).astype(np.float32)
    cos = np.zeros((P, 16, half), np.float32)
    sin = np.zeros((P, 16, half), np.float32)
    for m in range(NSEG):
        g = j + 4 * m
        for tb in range(4):
            pos = (g * TS + tb * P + i).astype(np.float32)
            ang = (pos[:, None] * inv[None, :]).astype(np.float32)
            cos[:, 4 * m + tb, :] = np.cos(ang)
            sin[:, 4 * m + tb, :] = np.sin(ang)
    put("cos", cos)
    put("sin", sin)
    kdec = np.stack([(128.0 ** -0.5) * (GAM[h] ** (-(i + 1.0))) for h in range(4)], 1)
    gpow = np.stack([GAM[h] ** (i + 1.0) for h in range(4)], 1)
    put("kdec", kdec.astype(np.float32))
    put("gpow", gpow.astype(np.float32))
    r = np.arange(4)
    put("alpha", np.broadcast_to((r >= j).astype(np.float32), (P, 4)))
    put("beta", np.broadcast_to(np.where(r > j, NEG, 0.0).astype(np.float32), (P, 4)))
    put("sel", np.broadcast_to((r == j).astype(np.float32), (P, 4)))
    put("selA", np.broadcast_to((r == j - 1).astype(np.float32), (P, 4)))
    put("selB", np.broadcast_to(((r == 3) & (j == 0)).astype(np.float32), (P, 4)))
    s4 = np.zeros((P, 4, P), np.float32)
    for h in range(4):
        s4[h, h, :] = 1.0
    put("sel4", s4)
    o, w = CSTB["ident"]
    cstb[:, o:o + w] = np.eye(P)
    kk = i[:, None]
    q = np.arange(TS)[None, :]
    dA = np.zeros((P, 4, TS), np.float32)
    dB = np.zeros((P, 4, TS), np.float32)
    for kb in range(4):
        k = kb * P + kk
        dA[:, kb, :] = np.where((k // 64) <= (q // 64), 0.0, NEG)
        dB[:, kb, :] = np.where(k <= q, 0.0, NEG)
    o, w = CSTB["diagA"]
    cstb[:, o:o + w] = dA.reshape(P, w)
    o, w = CSTB["diagB"]
    cstb[:, o:o + w] = dB.reshape(P, w)
    o, w = CSTB["ones"]
    cstb[:, o:o + w] = 1.0
    return cst, cstb.astype(ml_dtypes.bfloat16)


_HOSTW_CACHE = {}


def host_layout(inputs, L=DEPTH):
    out = {}
    w_ada = inputs["w_ada"]
    out["w_ada"] = np.stack([pack_cols(w_ada[l], [(c * 128, 128) for c in range(48)]) for l in range(L)])
    w_in = inputs["w_in"]
    out["w_in"] = np.stack([pack_cols(w_in[l], WIN_BLOCKS) for l in range(L)])
    out["w_pa"] = np.stack([pack_cols(inputs["w_pa"][l], [(c * 128, 128) for c in range(8)]) for l in range(L)])
    out["w_pb"] = np.stack([pack_cols(inputs["w_pb"][l], [(c * 128, 128) for c in range(8)]) for l in range(L)])
    out["w_pc"] = np.stack([pack_cols(inputs["w_pc"][l], [(c * 128, 128) for c in range(8)]) for l in range(L)])
    out["w_out"] = np.stack([pack_cols(inputs["w_out"][l], [(c * 128, 128) for c in range(8)]) for l in range(L)])
    ub = []
    for fc in range(NFC):
        ub.append((fc * 128, 128))
        ub.append((DFF + fc * 128, 128))
    out["w_up"] = np.stack([pack_cols(inputs["w_up"][l], ub) for l in range(L)])
    out["w_down"] = np.stack([pack_cols(inputs["w_down"][l], [(c * 128, 128) for c in range(8)]) for l in range(L)])
    vec = np.zeros((L, P, VEC_TOT), np.float32)
    for l in range(L):
        def putv(name, arr):
            o, w = VEC[name]
            vec[l, :, o:o + w] = arr
        putv("b_ada", inputs["b_ada"][l].reshape(48, P).T)
        b_in = inputs["b_in"][l]
        fm = []
        for c0 in (AK, BK, AQ, BQ):
            for h in range(4):
                fm.append(b_in[c0 + h * 128:c0 + (h + 1) * 128])
        for dc in range(8):
            for wh in range(3):
                c0 = GT + wh * 1024 + dc * 128
                fm.append(b_in[c0:c0 + 128])
        for ec in range(8):
            fm.append(b_in[CG + ec * 128:CG + (ec + 1) * 128])
        putv("b_fm", np.stack(fm, 1))
        putv("g_diff", inputs["g_diff"][l].reshape(4, P).T)
        putv("g_ret", inputs["g_ret"][l].reshape(8, P).T)
        putv("ln_g", inputs["ln_g"][l].reshape(2, 8, P).transpose(2, 0, 1).reshape(P, 16))
        putv("ln_b", inputs["ln_b"][l].reshape(2, 8, P).transpose(2, 0, 1).reshape(P, 16))
        putv("w_conv", inputs["w_conv"][l].reshape(3, NFC, P).transpose(2, 0, 1).reshape(P, 66))
        putv("b_conv", inputs["b_conv"][l].reshape(NFC, P).T)
    out["vec"] = vec
    rows = np.zeros((L, 1, ROW_TOT), np.float32)
    for l in range(L):
        for n, (o, c0, w) in ROW_OFF.items():
            rows[l, 0, o:o + w] = inputs["b_in"][l][c0:c0 + w]
    out["rows"] = rows
    bro = np.zeros((L, 1, BRO_TOT), np.float32)
    for l in range(L):
        bro[l, 0, 0:64] = inputs["lam_q1"][l]
        bro[l, 0, 64:128] = inputs["lam_k1"][l]
        bro[l, 0, 128:192] = inputs["lam_q2"][l]
        bro[l, 0, 192:256] = inputs["lam_k2"][l]
    out["bro"] = bro
    return out


WSHAPES = {
    "w_ada": 48 * KC * 128, "w_in": WIN_TOT, "w_pa": 8 * 4 * 128, "w_pb": 8 * 4 * 128, "w_pc": 8 * 8 * 128,
    "w_out": 8 * 8 * 128, "w_up": 44 * 8 * 128, "w_down": 8 * NFC * 128,
}


class StopBuild(Exception):
    pass


class Builder:
    def __init__(self, nlayers=DEPTH, stage="full", dbg=()):
        self.nlayers = nlayers
        self.stage = stage
        self.dbg = set(dbg)
        self.nc = bass.Bass("TRN2", target_bir_lowering=False)
        self.pg = Prog()
        self.es = ExitStack()
        self.dbg_out = {}
        self.dma_keys = []

    def sb(self, name, nfree, dt):
        t = self.es.enter_context(self.nc.sbuf_tensor("sb_" + name, [P, nfree], dt))
        return t

    def dma(self, out, in_, rd, wr, key, q="sp"):
        if key not in self.dma_keys:
            self.dma_keys.append(key)
        self.pg.add(q, lambda e, o=out, i=in_: e.dma_start(out=o, in_=i), rd=rd, wr=wr, dma=key)

    def op(self, eng, fn, rd, wr):
        self.pg.add(eng, fn, rd=rd, wr=wr)

    def debug_dump(self, name, ap_sb, shape, dt, rd):
        if name not in self.dbg:
            return
        t = self.nc.dram_tensor("dbg_" + name, list(shape), dt, kind="ExternalOutput").ap()
        self.dbg_out[name] = t
        self.dma(t, ap_sb, rd=rd, wr=["dbg_" + name], key="dbg_" + name)

    def build(self):
        nc = self.nc
        pg = self.pg
        L = self.nlayers
        dram = nc.dram_tensor
        x_in = dram("x", [NT, D], F32, kind="ExternalInput").ap()
        cT_in = dram("cT", [P, KC], F32, kind="ExternalInput").ap()
        wsrc = {n: dram(n, [L, P, WSHAPES[n]], F32, kind="ExternalInput").ap() for n in WSHAPES}
        vec_in = dram("vec", [L, P, VEC_TOT], F32, kind="ExternalInput").ap()
        rows_in = dram("rows", [L, 1, ROW_TOT], F32, kind="ExternalInput").ap()
        bro_in = dram("bro", [L, 1, BRO_TOT], F32, kind="ExternalInput").ap()
        cst_in = dram("cst", [P, CST_TOT], F32, kind="ExternalInput").ap()
        cstb_in = dram("cstb", [P, CSTB_TOT], BF16, kind="ExternalInput").ap()
        y_out = dram("y", [NT, D], F32, kind="ExternalOutput").ap()
        wb = {n: dram("wb_" + n, [L, P, WSHAPES[n]], BF16).ap() for n in WSHAPES}
        kK = [dram("kK%d" % m, [P, 4096], BF16) for m in range(NSEG)]
        kV = [dram("kV%d" % m, [P, 4096], BF16) for m in range(NSEG)]
        sL = [dram("sL%d" % m, [P, 1024], F32) for m in range(NSEG)]
        sG = dram("sG", [P, 64], F32)
        gK = [dram("gK%d" % m, [4 * P, 4096], BF16) for m in range(NSEG)]
        gV = [dram("gV%d" % m, [4 * P, 4096], BF16) for m in range(NSEG)]
        gL = [dram("gL%d" % m, [4 * P, 1024], F32) for m in range(NSEG)]
        gG = dram("gG", [4 * P, 64], F32)
        hls = dram("hls", [P, 64], BF16)
        hlg = dram("hlg", [4 * P, 64], BF16)
        self.wb = wb

        sb = self.sb
        xs = sb("xs", KC * NT, F32)
        xsv = xs[:, :].rearrange("p (k t) -> p k t", k=KC)
        cst = sb("cst", CST_TOT, F32)
        cstb = sb("cstb", CSTB_TOT, BF16)
        vec = sb("vec", VEC_TOT, F32)
        modT = sb("modT", 48, F32)
        hT = sb("hT", KC * TS, BF16)
        hTv = hT[:, :].rearrange("p (k t) -> p k t", k=KC)
        W = [sb("W0", 5120, BF16), sb("W1", 5120, BF16)]
        Wbf = sb("Wbf", KC * 4, BF16)
        brow = sb("brow", ROW_TOT, BF16)
        Gloc = sb("Gloc", 64, F32)
        cT = sb("cTs", KC, F32)
        cact = sb("cact", KC, BF16)
        lamv = sb("lamv", 8, F32)
        gdf = sb("gdf", 4, F32)
        epsc = sb("epsc", 2, F32)
        Gk = sb("Gk", 256, F32)
        off_rm = sb("off_rm", 64, F32)
        offq = sb("offq", 16, F32)
        Rin = sb("Rin", 4096, F32)
        halo = sb("halo", 64, BF16)
        PSb = [self.es.enter_context(nc.psum_tensor(f"ps{i}", [P, 512], F32)) for i in range(8)]

        def cs(name):
            o, w = CST[name]
            return cst[:, o:o + w]

        def csb(name):
            o, w = CSTB[name]
            return cstb[:, o:o + w]

        ident_f = cs("ident")
        ident_b = csb("ident")
        ones_b = csb("ones")

        CH = 8192

        def cast(l, names, gi):
            if l >= L:
                return
            for n in names:
                tot = WSHAPES[n]
                for c0 in range(0, tot, CH):
                    c1 = min(tot, c0 + CH)
                    self.dma(wb[n][l, :, c0:c1], wsrc[n][l, :, c0:c1], rd=[], wr=[("wb_" + n, l)], key="cast%d_%d" % (l % 2, gi), q="pool")

        CG0, CG1, CG2 = ["w_ada", "w_in"], ["w_pa", "w_pb", "w_pc", "w_out"], ["w_up", "w_down"]
        self.cast = cast
        cast(0, CG0, 0)
        cast(0, CG1, 1)
        cast(0, CG2, 2)

        self.dma(cst[:, :], cst_in[:, :], rd=[], wr=["cst"], key="cst")
        self.dma(cstb[:, :], cstb_in[:, :], rd=[], wr=["cstb"], key="cstb")
        self.dma(cT[:, :], cT_in[:, :], rd=[], wr=["cT"], key="cst")
        self.op("act", lambda e: e.activation(out=cact[:, :], in_=cT[:, :], func=AF.Silu), rd=["cT"], wr=["cact"])
        self.op("pool", lambda e: e.memset(epsc[:, :], LN_EPS), rd=[], wr=["epsc"])
        self.op("pool", lambda e: e.memset(halo[:, :], 0.0), rd=[], wr=["halo"])

        wstate = dict(n=0, queue=[], loaded=0)

        def wplan(lst):
            wstate["queue"].extend(lst)

        def _wload(idx):
            slot = idx % 2
            pieces = wstate["queue"][idx]
            o = 0
            for pi, (src, n, rdk) in enumerate(pieces):
                self.dma(W[slot][:, o:o + n], src, rd=[rdk], wr=[("W%d" % slot, pi)] if len(pieces) > 1 else ["W%d" % slot], key="W%d" % slot)
                o += n

        def wtake():
            idx = wstate["n"]
            while wstate["loaded"] <= min(idx + 1, len(wstate["queue"]) - 1):
                _wload(wstate["loaded"])
                wstate["loaded"] += 1
            wstate["n"] += 1
            return W[idx % 2], "W%d" % (idx % 2)

        def win_piece(l, name, i=0, n=1):
            off, width, nb = WIN_IDX[name]
            o = off + i * KC * width
            return (wb["w_in"][l, :, o:o + n * KC * width], n * KC * width, ("wb_w_in", l))

        def wpiece(wn, l, o, n):
            return (wb[wn][l, :, o:o + n], n, ("wb_" + wn, l))

        ARENA_ELEMS = 33 * 1024
        arena_t = sb("arena", ARENA_ELEMS, BF16)

        class Arena:
            def __init__(s):
                s.off = 0
                s.peak = 0

            def get(s, n, dt):
                nb = n * (2 if dt == F32 else 1)
                nb = (nb + 31) // 32 * 32
                assert s.off + nb <= ARENA_ELEMS, ("arena overflow", s.off, nb)
                v = arena_t[:, s.off:s.off + nb]
                s.off += nb
                s.peak = max(s.peak, s.off)
                if dt == F32:
                    return v.bitcast(F32)[:, 0:n]
                return v[:, 0:n]

            def mark(s):
                return s.off

            def reset(s, mk=0):
                s.off = mk

        ar = Arena()
        self.arena = ar
        RG = [[0, 1, 2, 3], [4, 5, 6, 7]]
        tri = cs("tri")
        onesf = cs("ones")
        causal = cs("causal")
        cosv = cs("cos").rearrange("p (t d) -> p t d", t=16)
        sinv = cs("sin").rearrange("p (t d) -> p t d", t=16)
        kdec = cs("kdec")
        gpow = cs("gpow")
        alpha_c = cs("alpha")
        beta_c = cs("beta")
        sel_c = cs("sel")
        selA_c = cs("selA")
        selB_c = cs("selB")
        sel4 = cs("sel4")
        diagA = csb("diagA").rearrange("p (k q) -> p k q", k=4)
        diagB = csb("diagB").rearrange("p (k q) -> p k q", k=4)

        def agather(src, dst, skey, dkey):
            if "cc" not in self.dma_keys:
                self.dma_keys.append("cc")
            pg.add("pool", lambda e, src=src, dst=dst: e.collective_compute("AllGather", ALU.bypass, replica_groups=RG, ins=[src.ap().opt()], outs=[dst.ap().opt()]),
                   rd=[skey], wr=[dkey], dma="cc", inc=1)

        def vcol(name, j=0, n=1):
            o, w = VEC[name]
            return vec[:, o + j:o + j + n]

        ar.reset()
        xin_ts = [ar.get(D, F32), ar.get(D, F32)]
        for tb in range(NT // P):
            xin_t = xin_ts[tb % 2]
            xk = "xin_t%d" % (tb % 2)
            self.dma(xin_t, x_in[tb * P:(tb + 1) * P, :], rd=[], wr=[xk], key=xk)
            for k2 in range(2):
                bi = (tb * 2 + k2) % 4
                ps = PSb[bi]
                pkey = ("ps", bi)
                for c in range(4):
                    kc = k2 * 4 + c
                    self.op("pe", lambda e, ps=ps, c=c, kc=kc, xin_t=xin_t: e.transpose(out=ps[:, c * P:(c + 1) * P], in_=xin_t[:, kc * P:(kc + 1) * P], identity=ident_f),
                            rd=[xk, "cst"], wr=[pkey])
                outv = xsv[:, k2 * 4:(k2 + 1) * 4, tb * P:(tb + 1) * P]
                inv = ps[:, :].rearrange("p (c t) -> p c t", c=4)
                xkeys = [("xs", (tb // 4) * 8 + k2 * 4 + c) for c in range(4)]
                if k2 == 0:
                    self.op("dve", lambda e, o=outv, i=inv: e.tensor_copy(out=o, in_=i), rd=[pkey], wr=xkeys)
                else:
                    self.op("act", lambda e, o=outv, i=inv: e.copy(out=o, in_=i), rd=[pkey], wr=xkeys)
        pg.barrier()

        def layer_prelude(l):
            ar.reset()
            brf = ar.get(1024, F32)
            brb = ar.get(1024, BF16)
            lamt = ar.get(256, F32)
            lamp = ar.get(128, F32)
            self.dma(vec[:, :], vec_in[l, :, :], rd=[], wr=["vec"], key="vec")
            self.op("pool", lambda e: e.memset(brow[0:64, :], 0.0), rd=[], wr=["brow"])
            for c0 in range(0, ROW_TOT, 1024):
                c1 = min(ROW_TOT, c0 + 1024)
                n = c1 - c0
                self.dma(brf[0:1, 0:n], rows_in[l, :, c0:c1], rd=[], wr=["brf"], key="brf")
                self.dma(brf[32:33, 0:n], rows_in[l, :, c0:c1], rd=[], wr=["brf"], key="brf")
                self.op("dve", lambda e, c0=c0, c1=c1, n=n: e.tensor_copy(out=brow[0:1, c0:c1], in_=brf[0:1, 0:n]), rd=["brf"], wr=["brow"])
                self.op("dve", lambda e, n=n: e.tensor_copy(out=brb[32:33, 0:n], in_=brf[32:33, 0:n]), rd=["brf"], wr=["brb"])
                self.op("dve", lambda e, c0=c0, c1=c1, n=n: e.tensor_tensor(out=brow[32:33, c0:c1], in0=brf[32:33, 0:n], in1=brb[32:33, 0:n], op=ALU.subtract), rd=["brf", "brb"], wr=["brow"])
            lam_init = 0.8 - 0.6 * math.exp(-0.3 * l)
            self.dma(lamt[:, :], bro_in[l, :, :].partition_broadcast(P)[:, 0, :], rd=[], wr=["lamt"], key="lamt")
            lt3 = lamt[:, :].rearrange("p (a b d) -> p a b d", a=2, b=2)
            lp3 = lamp[:, :].rearrange("p (a d) -> p a d", a=2)
            self.op("dve", lambda e: e.tensor_tensor(out=lp3, in0=lt3[:, :, 0, :], in1=lt3[:, :, 1, :], op=ALU.mult), rd=["lamt"], wr=["lamp"])
            self.op("dve", lambda e: e.reduce_sum(out=lamv[:, 0:2], in_=lp3, axis=AX.X), rd=["lamp"], wr=["lamv"])
            self.op("act", lambda e: e.activation(out=lamv[:, 2:4], in_=lamv[:, 0:2], func=AF.Exp), rd=["lamv"], wr=["lamv"])
            self.op("dve", lambda e: e.tensor_tensor(out=lamv[:, 4:5], in0=lamv[:, 3:4], in1=lamv[:, 2:3], op=ALU.subtract), rd=["lamv"], wr=["lamv"])
            self.op("dve", lambda e: e.tensor_scalar(out=lamv[:, 5:6], in0=lamv[:, 4:5], scalar1=-lam_init, scalar2=None, op0=ALU.add), rd=["lamv"], wr=["lamv"])
            self.op("dve", lambda e: e.tensor_scalar(out=gdf[:, :], in0=vcol("g_diff", 0, 4), scalar1=1.0 - lam_init, scalar2=None, op0=ALU.mult), rd=["vec"], wr=["gdf"])
            ps = PSb[7]
            wplan([[wpiece("w_ada", l, g * 4096, 4096)] for g in range(12)])
            for g in range(12):
                Wt, wk = wtake()
                for jj in range(4):
                    j = g * 4 + jj
                    for kc in range(KC):
                        self.op("pe", lambda e, Wt=Wt, jj=jj, kc=kc, j=j: e.matmul(ps[:, j:j + 1], lhsT=Wt[:, (jj * KC + kc) * 128:(jj * KC + kc + 1) * 128],
                                                                                   rhs=cact[:, kc:kc + 1], start=(kc == 0), stop=(kc == KC - 1)),
                                rd=[wk, "cact"], wr=[("ps", 7)])
            o, w = VEC["b_ada"]
            self.op("dve", lambda e: e.tensor_tensor(out=modT[:, :], in0=ps[:, 0:48], in1=vec[:, o:o + w], op=ALU.add), rd=[("ps", 7), "vec"], wr=["modT"])
            self.op("dve", lambda e: e.tensor_scalar(out=modT[:, 8:16], in0=modT[:, 8:16], scalar1=1.0, scalar2=None, op0=ALU.add), rd=["modT"], wr=["modT"])
            self.op("dve", lambda e: e.tensor_scalar(out=modT[:, 32:40], in0=modT[:, 32:40], scalar1=1.0, scalar2=None, op0=ALU.add), rd=["modT"], wr=["modT"])
            self.debug_dump("modT%d" % l, modT[:, :], [P, 48], F32, rd=["modT"])
            pg.barrier()

        def make_hT(t, which):
            sh0 = 0 if which == 0 else 24
            sc0 = 8 if which == 0 else 32
            for kc in range(KC):
                src = xsv[:, kc, t * TS:(t + 1) * TS]
                dst = hTv[:, kc, :]
                if kc % 2 == 0:
                    self.op("dve", lambda e, s=src, d=dst, kc=kc: e.tensor_scalar(out=d, in0=s, scalar1=modT[:, sc0 + kc:sc0 + kc + 1], scalar2=modT[:, sh0 + kc:sh0 + kc + 1],
                                                                                  op0=ALU.mult, op1=ALU.add), rd=[("xs", t * 8 + kc), "modT"], wr=[("hT", kc)])
                else:
                    self.op("act", lambda e, s=src, d=dst, kc=kc: e.activation(out=d, in_=s, func=AF.Identity, bias=modT[:, sh0 + kc:sh0 + kc + 1], scale=modT[:, sc0 + kc:sc0 + kc + 1]),
                            rd=[("xs", t * 8 + kc), "modT"], wr=[("hT", kc)])

        psrr = dict(i=0, lo=0, hi=4)

        def next_ps():
            i = psrr["lo"] + psrr["i"] % (psrr["hi"] - psrr["lo"])
            psrr["i"] += 1
            return PSb[i], ("ps", i)

        def proj_fm(Wt, wk, woff, bias_col, dst, dst_key, func=None, rhsv=None, rhs_key="hT", nk=KC, rhs_c0=0):
            if rhsv is None:
                rhsv = hTv
            ps, pk = next_ps()
            for kc in range(nk):
                self.op("pe", lambda e, kc=kc, ps=ps: e.matmul(ps[:, :], lhsT=Wt[:, woff + kc * 128:woff + (kc + 1) * 128], rhs=rhsv[:, rhs_c0 + kc, :],
                                                                start=(kc == 0), stop=(kc == nk - 1)), rd=[wk, rhs_key], wr=[pk])
            if dst is None:
                return ps, pk
            f = AF.Identity if func is None else func
            bcol = vcol("b_fm", bias_col)
            self.op("act", lambda e, ps=ps: e.activation(out=dst, in_=ps[:, :], func=f, bias=bcol, scale=1.0), rd=[pk, "vec"], wr=[dst_key])

        def proj_tm(Wt, wk, woff, ncols, tb, rowname, rowoff, ps, pk, pcol=0):
            ro = ROW_OFF[rowname][0] + rowoff
            for kc in range(KC):
                self.op("pe", lambda e, kc=kc: e.matmul(ps[:, pcol:pcol + ncols], lhsT=hTv[:, kc, tb * P:(tb + 1) * P], rhs=Wt[:, woff + kc * ncols:woff + (kc + 1) * ncols],
                                                         start=(kc == 0), stop=False), rd=[wk, "hT"], wr=[pk])
            self.op("pe", lambda e: e.matmul(ps[:, pcol:pcol + ncols], lhsT=ones_b[0:33, :], rhs=brow[0:33, ro:ro + ncols], start=False, stop=True),
                    rd=["brow", "cstb"], wr=[pk])

        def rotary(ps, pk, tbg, dst, dst_key, dec, rt1, rt2, krot):
            pv = ps[:, :].rearrange("p (h s d) -> p h s d", h=4, s=2)
            t1 = pv[:, :, 0, :]
            t2 = pv[:, :, 1, :]
            cb = cosv[:, tbg, :].unsqueeze(1).broadcast_to([P, 4, 64])
            sbv = sinv[:, tbg, :].unsqueeze(1).broadcast_to([P, 4, 64])
            a = rt1[:, :].rearrange("p (h d) -> p h d", h=4)
            b = rt2[:, :].rearrange("p (h d) -> p h d", h=4)
            kr = krot[:, :].rearrange("p (h s d) -> p h s d", h=4, s=2)
            self.op("dve", lambda e: e.tensor_tensor(out=a, in0=t1, in1=cb, op=ALU.mult), rd=[pk, "cst"], wr=["rt1"])
            self.op("dve", lambda e: e.tensor_tensor(out=b, in0=t2, in1=sbv, op=ALU.mult), rd=[pk, "cst"], wr=["rt2"])
            self.op("pool", lambda e: e.tensor_tensor(out=kr[:, :, 0, :], in0=a, in1=b, op=ALU.subtract), rd=["rt1", "rt2"], wr=[("krot", 0)])
            self.op("dve", lambda e: e.tensor_tensor(out=a, in0=t1, in1=sbv, op=ALU.mult), rd=[pk, "cst"], wr=["rt1"])
            self.op("dve", lambda e: e.tensor_tensor(out=b, in0=t2, in1=cb, op=ALU.mult), rd=[pk, "cst"], wr=["rt2"])
            self.op("pool", lambda e: e.tensor_tensor(out=kr[:, :, 1, :], in0=a, in1=b, op=ALU.add), rd=["rt1", "rt2"], wr=[("krot", 1)])
            kv3 = krot[:, :].rearrange("p (h d) -> p h d", h=4)
            d3 = dst.rearrange("p (h d) -> p h d", h=4)
            if dec is not None:
                db = dec.unsqueeze(2).broadcast_to([P, 4, 128])
                self.op("pool", lambda e: e.tensor_tensor(out=d3, in0=kv3, in1=db, op=ALU.mult), rd=["krot", "cst"], wr=[dst_key])
            else:
                self.op("pool", lambda e: e.tensor_copy(out=d3, in_=kv3), rd=["krot"], wr=[dst_key])

        def phaseA(l, t):
            m = t
            ar.reset()
            kTt = ar.get(4 * TS, BF16)
            kTv = kTt[:, :].rearrange("p (h t) -> p h t", h=4)
            kT2 = ar.get(4 * TS, BF16)
            kT2v = kT2[:, :].rearrange("p (h t) -> p h t", h=4)
            Vaug = [ar.get(2048, BF16), ar.get(2048, BF16)]
            Vaugv = [v[:, :].rearrange("p (h t e) -> p h t e", h=4, t=4) for v in Vaug]
            nl = ar.get(16, F32)
            ex = ar.get(16, F32)
            rt1 = ar.get(256, F32)
            rt2 = ar.get(256, F32)
            krot = ar.get(512, F32)
            kinv = ar.get(4 * 512, BF16)
            kinvv = kinv[:, :].rearrange("p (t c) -> p t c", t=4)
            vtm = ar.get(4 * 1024, BF16)
            vtmv = vtm[:, :].rearrange("p (t c) -> p t c", t=4)
            Qst = ar.get(1024, F32)
            Qstv = Qst[:, :].rearrange("p (h e) -> p h e", h=4)
            Lst = ar.get(1024, F32)
            Lstv = Lst[:, :].rearrange("p (h e) -> p h e", h=4)
            vkeys = ["VaugA", "VaugB"]
            psrr["lo"], psrr["hi"] = 0, 4
            make_hT(t, 0)
            if ("hT%d_%d" % (l, t)) in self.dbg:
                self.debug_dump("hT%d_%d" % (l, t), hT[:, :], [P, KC * TS], BF16, rd=["hT"])
            wplan([[win_piece(l, "ak", 0, 4)], [win_piece(l, "bk", 0, 4)], [win_piece(l, "av")], [win_piece(l, "bv")],
                   [win_piece(l, "ck")], [win_piece(l, "cv", 0, 1)], [win_piece(l, "cv", 1, 1)]])
            if t == 0:
                self.dma(Wbf[:, :], win_piece(l, "bf")[0], rd=[("wb_w_in", l)], wr=["Wbf"], key="Wbf")
            Wt, wk = wtake()
            for h in range(4):
                proj_fm(Wt, wk, h * KC * 128, 0 + h, kTv[:, h, :], ("kTt", h))
            self.dma(kK[m][:, 0:2048], kTt[:, :], rd=["kTt"], wr=[("kK", m)], key="kvst")
            Wt, wk = wtake()
            for h in range(4):
                proj_fm(Wt, wk, h * KC * 128, 4 + h, kT2v[:, h, :], ("kT2", h))
            self.dma(kK[m][:, 2048:4096], kT2[:, :], rd=["kT2"], wr=[("kK", m)], key="kvst")
            agather(kK[m], gK[m], ("kK", m), ("gK", m))
            for vi, nm in enumerate(("av", "bv")):
                Wt, wk = wtake()
                vkey = vkeys[vi]
                for tb in range(4):
                    ps, pk = next_ps()
                    proj_tm(Wt, wk, 0, 512, tb, nm, 0, ps, pk)
                    outv = Vaugv[vi][:, :, tb, 0:128]
                    inv = ps[:, :].rearrange("p (h e) -> p h e", h=4)
                    if tb % 2 == 0:
                        self.op("dve", lambda e, o=outv, i=inv: e.tensor_copy(out=o, in_=i), rd=[pk], wr=[(vkey, tb)])
                    else:
                        self.op("act", lambda e, o=outv, i=inv: e.copy(out=o, in_=i), rd=[pk], wr=[(vkey, tb)])
                self.dma(kV[m][:, vi * 2048:(vi + 1) * 2048], Vaug[vi][:, :], rd=[vkey], wr=[("kV", m)], key="kvst")
            agather(kV[m], gV[m], ("kV", m), ("gV", m))
            ps, pk = PSb[4], ("ps", 4)
            ro = ROW_OFF["bf"][0]
            for tb in range(4):
                for kc in range(KC):
                    self.op("pe", lambda e, kc=kc, tb=tb: e.matmul(ps[:, tb * 4:tb * 4 + 4], lhsT=hTv[:, kc, tb * P:(tb + 1) * P], rhs=Wbf[:, kc * 4:kc * 4 + 4],
                                                                    start=(kc == 0), stop=False), rd=["Wbf", "hT"], wr=[pk])
                self.op("pe", lambda e, tb=tb: e.matmul(ps[:, tb * 4:tb * 4 + 4], lhsT=ones_b[0:33, :], rhs=brow[0:33, ro:ro + 4], start=False, stop=True),
                        rd=["brow", "cstb"], wr=[pk])
            self.op("act", lambda e: e.activation(out=ex[:, :], in_=ps[:, 0:16], func=AF.Exp, scale=-1.0), rd=[pk], wr=["ex"])
            self.op("act", lambda e: e.activation(out=nl[:, :], in_=ex[:, :], func=AF.Ln, bias=1.0, scale=1.0), rd=["ex"], wr=["nl"])
            ps2, pk2 = PSb[5], ("ps", 5)
            for tb in range(4):
                for tb2 in range(tb + 1):
                    lh = tri if tb2 == tb else onesf
                    self.op("pe", lambda e, tb=tb, tb2=tb2, lh=lh: e.matmul(ps2[:, tb * 4:tb * 4 + 4], lhsT=lh, rhs=nl[:, tb2 * 4:tb2 * 4 + 4], start=(tb2 == 0), stop=(tb2 == tb)),
                            rd=["nl", "cst"], wr=[pk2])
            self.op("dve", lambda e: e.tensor_copy(out=Gloc[:, m * 16:(m + 1) * 16], in_=ps2[:, 0:16]), rd=[pk2], wr=[("Gloc", m)])
            Wck, wkck = wtake()
            for tb in range(4):
                psk, pkk = next_ps()
                proj_tm(Wck, wkck, 0, 512, tb, "ck", 0, psk, pkk)
                rotary(psk, pkk, 4 * m + tb, kinvv[:, tb, :], ("kinv", tb), kdec, rt1, rt2, krot)
            for half in range(2):
                Wcv, wkcv = wtake()
                for tb in range(4):
                    psv, pkv = next_ps()
                    proj_tm(Wcv, wkcv, 0, 512, tb, "cv", half * 512, psv, pkv)
                    outv = vtmv[:, tb, half * 512:(half + 1) * 512]
                    if tb % 2 == 0:
                        self.op("act", lambda e, o=outv, ps=psv: e.copy(out=o, in_=ps[:, :]), rd=[pkv], wr=[("vtm", tb)])
                    else:
                        self.op("dve", lambda e, o=outv, ps=psv: e.tensor_copy(out=o, in_=ps[:, :]), rd=[pkv], wr=[("vtm", tb)])
            for tb in range(4):
                for hp in range(2):
                    psu, pku = PSb[6 + hp], ("ps", 6 + hp)
                    for hh in range(2):
                        h = hp * 2 + hh
                        self.op("pe", lambda e, tb=tb, h=h, hh=hh, psu=psu: e.matmul(psu[:, hh * 256:(hh + 1) * 256], lhsT=kinvv[:, tb, h * 128:(h + 1) * 128],
                                                                                      rhs=vtmv[:, tb, h * 256:(h + 1) * 256], start=True, stop=True),
                                rd=[("kinv", tb), ("vtm", tb)], wr=[pku])
                        if tb == 0:
                            self.op("dve", lambda e, h=h, hh=hh, psu=psu: e.tensor_copy(out=Qstv[:, h, :], in_=psu[:, hh * 256:(hh + 1) * 256]), rd=[pku], wr=[("Qst", h)])
                        else:
                            self.op("dve", lambda e, h=h, hh=hh, psu=psu: e.scalar_tensor_tensor(out=Qstv[:, h, :], in0=Qstv[:, h, :], scalar=G128[h], in1=psu[:, hh * 256:(hh + 1) * 256],
                                                                                                  op0=ALU.mult, op1=ALU.add), rd=[pku, ("Qst", h)], wr=[("Qst", h)])
            for h in range(4):
                self.op("pool", lambda e, h=h: e.tensor_scalar(out=Lstv[:, h, :], in0=Qstv[:, h, :], scalar1=G128[h], scalar2=None, op0=ALU.mult), rd=[("Qst", h)], wr=["Lst"])
            self.dma(sL[m][:, :], Lst[:, :], rd=["Lst"], wr=[("sL", m)], key="smst")
            agather(sL[m], gL[m], ("sL", m), ("gL", m))
            pg.barrier()

        def exchange1():
            self.dma(sG[:, :], Gloc[:, :], rd=["Gloc"], wr=["sG"], key="smst")
            agather(sG, gG, "sG", "gG")

        def phaseB_prelude(l):
            ar.reset()
            Tb = ar.get(256, F32)
            Tbv = Tb[:, :].rearrange("p (r c) -> p r c", r=4)
            Pg = ar.get(1024, F32)
            Lg = [ar.get(1024, F32), ar.get(1024, F32)]
            Gkv = Gk[:, :].rearrange("p (r c) -> p r c", r=4)
            self.dma(Gkv, gG[:, :].rearrange("(r p) c -> p r c", p=P), rd=["gG"], wr=["Gk"], key="Gk")
            for r in range(4):
                self.dma(Tbv[:, r, :], gG[r * P + 127:r * P + 128, :].partition_broadcast(P)[:, 0, :], rd=["gG"], wr=["Tb"], key="Tb")
            offv = off_rm[:, :].rearrange("p (r m h) -> p r m h", r=4, m=4)
            self.op("dve", lambda e: e.memset(off_rm[:, :], 0.0), rd=[], wr=["off"])
            for g in range(15):
                r, m = g % 4, g // 4
                r2, m2 = (g + 1) % 4, (g + 1) // 4
                self.op("dve", lambda e, r=r, m=m, r2=r2, m2=m2: e.tensor_tensor(out=offv[:, r2, m2, :], in0=offv[:, r, m, :], in1=Tbv[:, r, (4 * m + 3) * 4:(4 * m + 3) * 4 + 4], op=ALU.add),
                        rd=["off", "Tb"], wr=["off"])
            Gk5 = Gk[:, :].rearrange("p (r m t h) -> p r m t h", r=4, m=4, t=4)
            for r in range(4):
                self.op("dve", lambda e, r=r: e.tensor_tensor(out=Gk5[:, r], in0=Gk5[:, r], in1=offv[:, r].unsqueeze(2).broadcast_to([P, 4, 4, 4]), op=ALU.add),
                        rd=["Gk", "off"], wr=["Gk"])
            offqv = offq[:, :].rearrange("p (m h) -> p m h", m=4)
            self.op("dve", lambda e: e.tensor_scalar(out=offqv, in0=offv[:, 0], scalar1=sel_c[:, 0:1], scalar2=None, op0=ALU.mult), rd=["off", "cst"], wr=["offq"])
            for r in range(1, 4):
                self.op("dve", lambda e, r=r: e.scalar_tensor_tensor(out=offqv, in0=offv[:, r], scalar=sel_c[:, r:r + 1], in1=offqv, op0=ALU.mult, op1=ALU.add),
                        rd=["off", "cst", "offq"], wr=["offq"])
            Rinv = Rin[:, :].rearrange("p (m c) -> p m c", m=4)
            Pgv = Pg[:, :].rearrange("p (h e) -> p h e", h=4)
            self.op("pool", lambda e: e.memset(Pg[:, :], 0.0), rd=[], wr=["Pg"])
            for g in range(16):
                r, m = g % 4, g // 4
                if r == 0:
                    self.op("dve", lambda e, m=m, r=r: e.tensor_scalar(out=Rinv[:, m, :], in0=Pg[:, :], scalar1=sel_c[:, r:r + 1], scalar2=None, op0=ALU.mult),
                            rd=["Pg", "cst"], wr=[("Rin", m)])
                else:
                    self.op("dve", lambda e, m=m, r=r: e.scalar_tensor_tensor(out=Rinv[:, m, :], in0=Pg[:, :], scalar=sel_c[:, r:r + 1], in1=Rinv[:, m, :], op0=ALU.mult, op1=ALU.add),
                            rd=["Pg", "cst", ("Rin", m)], wr=[("Rin", m)])
                if g < 15:
                    lg = Lg[g % 2]
                    lk = "Lg%d" % (g % 2)
                    self.dma(lg[:, :], gL[m][r * P:(r + 1) * P, :], rd=[("gL", m)], wr=[lk], key=lk)
                    lgv = lg[:, :].rearrange("p (h e) -> p h e", h=4)
                    for h in range(4):
                        self.op("dve", lambda e, h=h, lgv=lgv: e.scalar_tensor_tensor(out=Pgv[:, h, :], in0=Pgv[:, h, :], scalar=G512[h], in1=lgv[:, h, :], op0=ALU.mult, op1=ALU.add),
                                rd=["Pg", lk], wr=["Pg"])
            pg.barrier()

        def phaseB_tile(l, m):
            t = m
            ar.reset()
            yT = ar.get(16 * TS, BF16)
            yTv = yT[:, :].rearrange("p (c t) -> p c t", c=16)
            mkB = ar.mark()
            make_hT(t, 0)
            qT = ar.get(4 * TS, BF16)
            qTv = qT[:, :].rearrange("p (h t) -> p h t", h=4)
            kiT = ar.get(4 * TS, BF16)
            kiTv = kiT[:, :].rearrange("p (h t) -> p h t", h=4)
            kinv = ar.get(4 * 512, BF16)
            kinvv = kinv[:, :].rearrange("p (t c) -> p t c", t=4)
            vtm = ar.get(4 * 1024, BF16)
            vtmv = vtm[:, :].rearrange("p (t c) -> p t c", t=4)
            qrot = [ar.get(512, BF16), ar.get(512, BF16)]
            yctm = [ar.get(1024, BF16), ar.get(1024, BF16)]
            cgT = [ar.get(TS, BF16), ar.get(TS, BF16)]
            Qst = ar.get(1024, F32)
            Qstv = Qst[:, :].rearrange("p (h e) -> p h e", h=4)
            Rb = ar.get(1024, BF16)
            Rbv = Rb[:, :].rearrange("p (h e) -> p h e", h=4)
            ycr = ar.get(1024, F32)
            ycrv = ycr[:, :].rearrange("p (h e) -> p h e", h=4)
            sqb = ar.get(1024, F32)
            sqbv = sqb[:, :].rearrange("p (h e) -> p h e", h=4)
            Pm = ar.get(512, BF16)
            Pmv = Pm[:, :].rearrange("p (h i) -> p h i", h=4)
            st = ar.get(32, F32)
            rt1 = ar.get(256, F32)
            rt2 = ar.get(256, F32)
            krot = ar.get(512, F32)
            psrr["lo"], psrr["hi"] = 0, 2
            wplan([[win_piece(l, "cq")], [win_piece(l, "ck")], [win_piece(l, "cv", 0, 1)], [win_piece(l, "cv", 1, 1)]])
            psT = PSb[7][:, :].bitcast(BF16)
            pkT = ("ps", 7)
            Wt, wk = wtake()
            for tb in range(4):
                ps, pk = next_ps()
                proj_tm(Wt, wk, 0, 512, tb, "cq", 0, ps, pk)
                qr = qrot[tb % 2]
                qk = "qrot%d" % (tb % 2)
                rotary(ps, pk, 4 * m + tb, qr[:, :], qk, None, rt1, rt2, krot)
                for h in range(4):
                    self.op("pe", lambda e, h=h, qr=qr: e.transpose(out=psT[:, h * 128:(h + 1) * 128], in_=qr[:, h * 128:(h + 1) * 128], identity=ident_b),
                            rd=[qk, "cstb"], wr=[pkT])
                self.op("act", lambda e, tb=tb: e.copy(out=qTv[:, :, tb * P:(tb + 1) * P], in_=psT[:, 0:512].rearrange("p (h t) -> p h t", h=4)), rd=[pkT], wr=[("qT", tb)])
            Wt, wk = wtake()
            for tb in range(4):
                ps, pk = next_ps()
                proj_tm(Wt, wk, 0, 512, tb, "ck", 0, ps, pk)
                rotary(ps, pk, 4 * m + tb, kinvv[:, tb, :], ("kinv", tb), kdec, rt1, rt2, krot)
                for h in range(4):
                    self.op("pe", lambda e, h=h, tb=tb: e.transpose(out=psT[:, 512 + h * 128:512 + (h + 1) * 128], in_=kinvv[:, tb, h * 128:(h + 1) * 128], identity=ident_b),
                            rd=[("kinv", tb), "cstb"], wr=[pkT])
                self.op("act", lambda e, tb=tb: e.copy(out=kiTv[:, :, tb * P:(tb + 1) * P], in_=psT[:, 512:1024].rearrange("p (h t) -> p h t", h=4)), rd=[pkT], wr=[("kiT", tb)])
            for half in range(2):
                Wcv, wkcv = wtake()
                for tb in range(4):
                    psv, pkv = next_ps()
                    proj_tm(Wcv, wkcv, 0, 512, tb, "cv", half * 512, psv, pkv)
                    outv = vtmv[:, tb, half * 512:(half + 1) * 512]
                    if tb % 2 == 0:
                        self.op("act", lambda e, o=outv, ps=psv: e.copy(out=o, in_=ps[:, :]), rd=[pkv], wr=[("vtm", tb)])
                    else:
                        self.op("dve", lambda e, o=outv, ps=psv: e.tensor_copy(out=o, in_=ps[:, :]), rd=[pkv], wr=[("vtm", tb)])
            Rinv = Rin[:, :].rearrange("p (m c) -> p m c", m=4)
            self.op("pool", lambda e: e.tensor_copy(out=Rb[:, :], in_=Rinv[:, m, :]), rd=[("Rin", m)], wr=["Rb"])
            psS, pkS = PSb[4], ("ps", 4)
            psU = [PSb[2], PSb[3]]
            psO = [PSb[5], PSb[6]]
            for tb in range(4):
                tsl = slice(tb * P, (tb + 1) * P)
                for h in range(4):
                    self.op("pe", lambda e, h=h, tsl=tsl: e.matmul(psS[:, h * 128:(h + 1) * 128], lhsT=kiTv[:, h, tsl], rhs=qTv[:, h, tsl], start=True, stop=True),
                            rd=[("kiT", tb), ("qT", tb)], wr=[pkS])
                self.op("dve", lambda e: e.tensor_tensor(out=Pmv, in0=psS[:, :].rearrange("p (h i) -> p h i", h=4), in1=causal.unsqueeze(1).broadcast_to([P, 4, 128]), op=ALU.mult),
                        rd=[pkS, "cst"], wr=["Pm"])
                for h in range(4):
                    po = psO[h // 2]
                    pko = ("ps", 5 + h // 2)
                    osl = slice((h % 2) * 256, (h % 2 + 1) * 256)
                    self.op("pe", lambda e, h=h, po=po, osl=osl, tb=tb: e.matmul(po[:, osl], lhsT=Pmv[:, h, :], rhs=vtmv[:, tb, h * 256:(h + 1) * 256], start=True, stop=False),
                            rd=["Pm", ("vtm", tb)], wr=[pko])
                    self.op("pe", lambda e, h=h, po=po, osl=osl, tsl=tsl: e.matmul(po[:, osl], lhsT=qTv[:, h, tsl], rhs=Rbv[:, h, :], start=False, stop=True),
                            rd=[("qT", tb), "Rb"], wr=[pko])
                    self.op("act", lambda e, h=h, po=po, osl=osl: e.mul(out=ycrv[:, h, :], in_=po[:, osl], mul=gpow[:, h:h + 1]), rd=[pko, "cst"], wr=[("ycr", h)])
                if tb < 3:
                    for h in range(4):
                        pu = psU[h // 2]
                        pku = ("ps", 2 + h // 2)
                        usl = slice((h % 2) * 256, (h % 2 + 1) * 256)
                        self.op("pe", lambda e, h=h, pu=pu, usl=usl, tb=tb: e.matmul(pu[:, usl], lhsT=kinvv[:, tb, h * 128:(h + 1) * 128], rhs=vtmv[:, tb, h * 256:(h + 1) * 256], start=True, stop=True),
                                rd=[("kinv", tb), ("vtm", tb)], wr=[pku])
                        if tb == 0:
                            src = Rinv[:, m, h * 256:(h + 1) * 256]
                            self.op("dve", lambda e, h=h, pu=pu, usl=usl, src=src: e.tensor_tensor(out=Qstv[:, h, :], in0=pu[:, usl], in1=src, op=ALU.add), rd=[pku, ("Rin", m)], wr=[("Qst", h)])
                        else:
                            self.op("dve", lambda e, h=h, pu=pu, usl=usl: e.scalar_tensor_tensor(out=Qstv[:, h, :], in0=Qstv[:, h, :], scalar=G128[h], in1=pu[:, usl], op0=ALU.mult, op1=ALU.add),
                                    rd=[pku, ("Qst", h)], wr=[("Qst", h)])
                        self.op("pool", lambda e, h=h: e.tensor_scalar(out=Rbv[:, h, :], in0=Qstv[:, h, :], scalar1=G128[h], scalar2=None, op0=ALU.mult), rd=[("Qst", h)], wr=["Rb"])
                self.op("dve", lambda e: e.reduce_sum(out=st[:, 0:4], in_=ycrv, axis=AX.X), rd=["ycr"], wr=["st"])
                self.op("pool", lambda e: e.tensor_tensor(out=sqb[:, :], in0=ycr[:, :], in1=ycr[:, :], op=ALU.mult), rd=["ycr"], wr=["sqb"])
                self.op("dve", lambda e: e.reduce_sum(out=st[:, 4:8], in_=sqbv, axis=AX.X), rd=["sqb"], wr=["st"])
                self.op("dve", lambda e: e.tensor_scalar(out=st[:, 8:12], in0=st[:, 0:4], scalar1=1.0 / 256, scalar2=None, op0=ALU.mult), rd=["st"], wr=["st"])
                self.op("dve", lambda e: e.tensor_tensor(out=st[:, 12:16], in0=st[:, 8:12], in1=st[:, 8:12], op=ALU.mult), rd=["st"], wr=["st"])
                self.op("dve", lambda e: e.scalar_tensor_tensor(out=st[:, 16:20], in0=st[:, 4:8], scalar=1.0 / 256, in1=st[:, 12:16], op0=ALU.mult, op1=ALU.subtract), rd=["st"], wr=["st"])
                self.op("act", lambda e: e.activation(out=st[:, 20:24], in_=st[:, 16:20], func=AF.Sqrt, bias=epsc[:, 0:1], scale=1.0), rd=["st", "epsc"], wr=["st"])
                self.op("dve", lambda e: e.reciprocal(out=st[:, 24:28], in_=st[:, 20:24]), rd=["st"], wr=["st"])
                self.op("dve", lambda e: e.tensor_tensor(out=ycrv, in0=ycrv, in1=st[:, 8:12].unsqueeze(2).broadcast_to([P, 4, 256]), op=ALU.subtract), rd=["ycr", "st"], wr=["ycr"])
                yc = yctm[tb % 2]
                yk = "yctm%d" % (tb % 2)
                self.op("pool", lambda e, yc=yc: e.tensor_tensor(out=yc[:, :].rearrange("p (h e) -> p h e", h=4), in0=ycrv, in1=st[:, 24:28].unsqueeze(2).broadcast_to([P, 4, 256]), op=ALU.mult),
                        rd=["ycr", "st"], wr=[yk])
                for ec in range(8):
                    self.op("pe", lambda e, ec=ec, yc=yc: e.transpose(out=psT[:, ec * 128:(ec + 1) * 128], in_=yc[:, ec * 128:(ec + 1) * 128], identity=ident_b), rd=[yk, "cstb"], wr=[pkT])
                self.op("act", lambda e, tsl=tsl: e.copy(out=yTv[:, 8:16, tsl], in_=psT[:, :].rearrange("p (c t) -> p c t", c=8)), rd=[pkT], wr=[("yT", 8 + tb)])
            wplan([[win_piece(l, "cg", 0, 4)], [win_piece(l, "cg", 4, 4)]])
            for half in range(2):
                Wt, wk = wtake()
                for e4 in range(4):
                    ec = half * 4 + e4
                    cg_t = cgT[ec % 2]
                    ck_ = "cgT%d" % (ec % 2)
                    proj_fm(Wt, wk, e4 * KC * 128, 40 + ec, cg_t[:, :], ck_, func=AF.Silu)
                    self.op("dve", lambda e, ec=ec, cg_t=cg_t: e.scalar_tensor_tensor(out=yTv[:, 8 + ec, :], in0=yTv[:, 8 + ec, :], scalar=vcol("g_ret", ec), in1=cg_t[:, :], op0=ALU.mult, op1=ALU.mult),
                            rd=["yT", ck_, "vec"], wr=[("yT", 100 + ec)])
            if ("yc%d_%d" % (l, m)) in self.dbg:
                self.debug_dump("yc%d_%d" % (l, m), yT[:, 8 * TS:16 * TS], [P, 8 * TS], BF16, rd=["yT"])
            pg.barrier()
            if self.stage == "B1":
                raise StopBuild()

            ar.reset(mkB)
            QA = ar.get(4 * TS, BF16)
            QAv = QA[:, :].rearrange("p (h t) -> p h t", h=4)
            QB = ar.get(4 * TS, BF16)
            QBv = QB[:, :].rearrange("p (h t) -> p h t", h=4)
            Kj = [ar.get(512, BF16), ar.get(512, BF16)]
            Vj = [ar.get(520, BF16), ar.get(520, BF16)]
            Pt = [ar.get(512, BF16) for _ in range(3)]
            tS = [ar.get(512, F32), ar.get(512, F32)]
            tS2 = [ar.get(512, F32), ar.get(512, F32)]
            Mq = [ar.get(512, F32), ar.get(512, F32)]
            Grow = ar.get(512, F32)
            yat = ar.get(512, F32)
            yat3 = yat[:, :].rearrange("p (q e) -> p q e", q=4)
            yat2 = ar.get(512, F32)
            yat23 = yat2[:, :].rearrange("p (q e) -> p q e", q=4)
            yab = ar.get(4 * 512, BF16)
            yabv = yab[:, :].rearrange("p (q c) -> p q c", q=4)
            ybb = ar.get(4 * 512, BF16)
            ybbv = ybb[:, :].rearrange("p (q c) -> p q c", q=4)
            bK = [ar.get(64, F32), ar.get(64, F32)]
            bKb = [ar.get(16, F32), ar.get(16, F32)]
            rr = ar.get(16, F32)
            for ks in range(2):
                self.op("pool", lambda e, ks=ks: e.memset(Vj[ks][:, :], 1.0), rd=[], wr=["Vj%d" % ks])
            psrr["lo"], psrr["hi"] = 0, 3
            wplan([[win_piece(l, "aq", 0, 4)], [win_piece(l, "bq", 0, 4)]])
            Wt, wk = wtake()
            for h in range(4):
                proj_fm(Wt, wk, h * KC * 128, 8 + h, QAv[:, h, :], ("QA", h))
            Wt, wk = wtake()
            for h in range(4):
                proj_fm(Wt, wk, h * KC * 128, 12 + h, QBv[:, h, :], ("QB", h))
            psG, pkG = PSb[7], ("ps", 7)
            for qb in range(4):
                self.op("pe", lambda e, qb=qb: e.transpose(out=psG[0:4, qb * 128:(qb + 1) * 128], in_=Gloc[:, (4 * m + qb) * 4:(4 * m + qb) * 4 + 4], identity=ident_f),
                        rd=[("Gloc", m), "cst"], wr=[pkG])
            self.op("dve", lambda e: e.tensor_scalar(out=Grow[0:4, :], in0=psG[0:4, :], scalar1=-1.0 / SCALE_B, scalar2=None, op0=ALU.mult), rd=[pkG], wr=["Grow"])

            jobs = [("A", h, mp) for h in range(4) for mp in range(2)] + [("B", h, 0) for h in range(4)]
            segs = [(m2, r) for m2 in range(m + 1) for r in range(4)]
            cnt = dict(job=0, kv=0, s=0, p=0, t=0)
            for (kind, h, mp) in jobs:
                jset = cnt["job"] % 2
                cnt["job"] += 1
                X, Y = PSb[3 + 2 * jset], PSb[4 + 2 * jset]
                kX, kY = ("ps", 3 + 2 * jset), ("ps", 4 + 2 * jset)
                X3 = X[:, 0:390].rearrange("p (q e) -> p q e", q=3)
                if kind == "A":
                    prs = slice(64 * mp, 64 * mp + 64)
                    Qv = QAv
                    qkey = ("QA", h)
                    scale = SCALE_A
                else:
                    prs = slice(0, 128)
                    Qv = QBv
                    qkey = ("QB", h)
                    scale = SCALE_B
                    mq = Mq[h % 2]
                    mqk = "Mq%d" % (h % 2)
                    self.op("pe", lambda e, h=h: e.matmul(psG[:, :], lhsT=sel4[0:4, h * 128:(h + 1) * 128], rhs=Grow[0:4, :], start=True, stop=True), rd=["Grow", "cst"], wr=[pkG])
                    self.op("act", lambda e, mq=mq: e.copy(out=mq[:, :], in_=psG[:, :]), rd=[pkG], wr=[mqk])
                    bk_ = bK[h % 2]
                    bkk = "bK%d" % (h % 2)
                    bkb = bKb[h % 2]
                    bkbk = "bKb%d" % (h % 2)
                    Gk4 = Gk[:, :].rearrange("p (c h) -> p c h", h=4)
                    self.op("dve", lambda e, h=h, bk_=bk_: e.tensor_scalar(out=bk_[:, :], in0=Gk4[:, :, h], scalar1=offq[:, m * 4 + h:m * 4 + h + 1], scalar2=None, op0=ALU.subtract),
                            rd=["Gk", "offq"], wr=[bkk])
                    bk3 = bk_[:, :].rearrange("p (r c) -> p r c", r=4)
                    bkb3 = bkb[:, :].rearrange("p (r c) -> p r c", r=4)
                    self.op("dve", lambda e, bk3=bk3, bkb3=bkb3: e.tensor_tensor(out=bkb3, in0=bk3[:, :, 4 * m:4 * m + 4], in1=beta_c.unsqueeze(2).broadcast_to([P, 4, 4]), op=ALU.add),
                            rd=[bkk, "cst"], wr=[bkbk])
                nsteps = len(segs) * 4
                si = 0
                for (m2, r) in segs:
                    row0 = 128 * r
                    ks = cnt["kv"] % 2
                    cnt["kv"] += 1
                    kj, vj = Kj[ks], Vj[ks]
                    kjk, vjk = "Kj%d" % ks, "Vj%d" % ks
                    vj3 = vj[:, :].rearrange("p (t e) -> p t e", t=4)[:, :, 0:128]
                    if kind == "A":
                        self.dma(kj[prs, :], gK[m2][row0 + 64 * mp:row0 + 64 * mp + 64, h * 512:(h + 1) * 512], rd=[("gK", m2)], wr=[kjk], key=kjk)
                        self.dma(vj3, gV[m2][row0:row0 + 128, h * 512:(h + 1) * 512].rearrange("p (t e) -> p t e", t=4), rd=[("gV", m2)], wr=[vjk], key=vjk)
                    else:
                        self.dma(kj[:, :], gK[m2][row0:row0 + 128, 2048 + h * 512:2048 + (h + 1) * 512], rd=[("gK", m2)], wr=[kjk], key=kjk)
                        self.dma(vj3, gV[m2][row0:row0 + 128, 2048 + h * 512:2048 + (h + 1) * 512].rearrange("p (t e) -> p t e", t=4), rd=[("gV", m2)], wr=[vjk], key=vjk)
                    diag = (m2 == m)
                    for kb in range(4):
                        psS_, pkS_ = next_ps()
                        self.op("pe", lambda e, psS_=psS_, kj=kj, kb=kb, Qv=Qv, h=h, prs=prs: e.matmul(psS_[:, :], lhsT=kj[prs, kb * 128:(kb + 1) * 128], rhs=Qv[prs, h, :], start=True, stop=True),
                                rd=[kjk, qkey], wr=[pkS_])
                        pt = Pt[cnt["p"] % 3]
                        ptk = "Pt%d" % (cnt["p"] % 3)
                        cnt["p"] += 1
                        if kind == "A" and not diag:
                            self.op("act", lambda e, pt=pt, psS_=psS_, sc_=scale: e.activation(out=pt[:, :], in_=psS_[:, :], func=AF.Exp, scale=sc_), rd=[pkS_], wr=[ptk])
                        elif kind == "A":
                            ts_ = tS[cnt["t"] % 2]
                            tsk = "tS%d" % (cnt["t"] % 2)
                            cnt["t"] += 1
                            self.op("dve", lambda e, ts_=ts_, psS_=psS_, kb=kb, r=r: e.scalar_tensor_tensor(out=ts_[:, :], in0=diagA[:, kb, :], scalar=alpha_c[:, r:r + 1], in1=psS_[:, :], op0=ALU.mult, op1=ALU.add),
                                    rd=[pkS_, "cstb", "cst"], wr=[tsk])
                            self.op("act", lambda e, pt=pt, ts_=ts_, r=r, sc_=scale: e.activation(out=pt[:, :], in_=ts_[:, :], func=AF.Exp, bias=beta_c[:, r:r + 1], scale=sc_), rd=[tsk, "cst"], wr=[ptk])
                        elif not diag:
                            ts_ = tS[cnt["t"] % 2]
                            tsk = "tS%d" % (cnt["t"] % 2)
                            cnt["t"] += 1
                            self.op("dve", lambda e, ts_=ts_, psS_=psS_, mq=mq: e.tensor_tensor(out=ts_[:, :], in0=psS_[:, :], in1=mq[:, :], op=ALU.add), rd=[pkS_, mqk], wr=[tsk])
                            bcol = bk_[:, r * 16 + m2 * 4 + kb:r * 16 + m2 * 4 + kb + 1]
                            self.op("act", lambda e, pt=pt, ts_=ts_, bcol=bcol, sc_=scale: e.activation(out=pt[:, :], in_=ts_[:, :], func=AF.Exp, bias=bcol, scale=sc_), rd=[tsk, bkk], wr=[ptk])
                        else:
                            ts_ = tS[cnt["t"] % 2]
                            tsk = "tS%d" % (cnt["t"] % 2)
                            ts2 = tS2[cnt["t"] % 2]
                            ts2k = "tS2%d" % (cnt["t"] % 2)
                            cnt["t"] += 1
                            self.op("dve", lambda e, ts_=ts_, psS_=psS_, kb=kb, r=r: e.scalar_tensor_tensor(out=ts_[:, :], in0=diagB[:, kb, :], scalar=alpha_c[:, r:r + 1], in1=psS_[:, :], op0=ALU.mult, op1=ALU.add),
                                    rd=[pkS_, "cstb", "cst"], wr=[tsk])
                            self.op("pool", lambda e, ts_=ts_, ts2=ts2, mq=mq: e.tensor_tensor(out=ts2[:, :], in0=ts_[:, :], in1=mq[:, :], op=ALU.add), rd=[tsk, mqk], wr=[ts2k])
                            bcol = bkb[:, r * 4 + kb:r * 4 + kb + 1]
                            self.op("act", lambda e, pt=pt, ts2=ts2, bcol=bcol, sc_=scale: e.activation(out=pt[:, :], in_=ts2[:, :], func=AF.Exp, bias=bcol, scale=sc_), rd=[ts2k, bkbk], wr=[ptk])
                        first = (si == 0)
                        last = (si == nsteps - 1)
                        for qb in range(4):
                            if qb < 3:
                                oreg = X[:, qb * 130:qb * 130 + 129]
                                ok = kX
                                st_ = first and qb == 0
                            else:
                                oreg = Y[:, 0:129]
                                ok = kY
                                st_ = first
                            self.op("pe", lambda e, oreg=oreg, pt=pt, qb=qb, vj=vj, kb=kb, st_=st_, last=last: e.matmul(oreg, lhsT=pt[:, qb * 128:(qb + 1) * 128], rhs=vj[:, kb * 130:kb * 130 + 129],
                                                                                                                      start=st_, stop=last, skip_group_check=True),
                                    rd=[ptk, vjk], wr=[ok])
                        si += 1
                self.op("dve", lambda e, X3=X3: e.reciprocal(out=rr[:, 0:3].unsqueeze(2), in_=X3[:, :, 128:129]), rd=[kX], wr=["rr"])
                self.op("dve", lambda e, Y=Y: e.reciprocal(out=rr[:, 3:4], in_=Y[:, 128:129]), rd=[kY], wr=["rr"])
                if kind == "A" and mp == 0:
                    self.op("dve", lambda e, X3=X3: e.tensor_tensor(out=yat3[:, 0:3, :], in0=X3[:, :, 0:128], in1=rr[:, 0:3].unsqueeze(2).broadcast_to([P, 3, 128]), op=ALU.mult), rd=[kX, "rr"], wr=["yat"])
                    self.op("dve", lambda e, Y=Y: e.tensor_scalar(out=yat3[:, 3, :], in0=Y[:, 0:128], scalar1=rr[:, 3:4], scalar2=None, op0=ALU.mult), rd=[kY, "rr"], wr=["yat"])
                elif kind == "A":
                    self.op("dve", lambda e: e.tensor_scalar(out=rr[:, 4:8], in0=rr[:, 0:4], scalar1=lamv[:, 5:6], scalar2=None, op0=ALU.mult), rd=["rr", "lamv"], wr=["rr"])
                    self.op("dve", lambda e, X3=X3: e.tensor_tensor(out=yat23[:, 0:3, :], in0=X3[:, :, 0:128], in1=rr[:, 4:7].unsqueeze(2).broadcast_to([P, 3, 128]), op=ALU.mult), rd=[kX, "rr"], wr=["yat2"])
                    self.op("dve", lambda e, Y=Y: e.tensor_scalar(out=yat23[:, 3, :], in0=Y[:, 0:128], scalar1=rr[:, 7:8], scalar2=None, op0=ALU.mult), rd=[kY, "rr"], wr=["yat2"])
                    self.op("pool", lambda e: e.tensor_tensor(out=yat[:, :], in0=yat[:, :], in1=yat2[:, :], op=ALU.add), rd=["yat", "yat2"], wr=["yat"])
                    self.op("pool", lambda e: e.tensor_tensor(out=yat2[:, :], in0=yat[:, :], in1=yat[:, :], op=ALU.mult), rd=["yat"], wr=["yat2"])
                    self.op("dve", lambda e: e.reduce_sum(out=rr[:, 8:12], in_=yat23, axis=AX.X), rd=["yat2"], wr=["rr"])
                    self.op("act", lambda e: e.activation(out=rr[:, 8:12], in_=rr[:, 8:12], func=AF.Sqrt, bias=epsc[:, 0:1], scale=1.0 / 128), rd=["rr", "epsc"], wr=["rr"])
                    self.op("dve", lambda e: e.reciprocal(out=rr[:, 12:16], in_=rr[:, 8:12]), rd=["rr"], wr=["rr"])
                    self.op("dve", lambda e, h=h: e.tensor_tensor(out=yabv[:, :, h * 128:(h + 1) * 128], in0=yat3, in1=rr[:, 12:16].unsqueeze(2).broadcast_to([P, 4, 128]), op=ALU.mult), rd=["yat", "rr"], wr=[("yab", h)])
                else:
                    self.op("dve", lambda e, X3=X3, h=h: e.tensor_tensor(out=ybbv[:, 0:3, h * 128:(h + 1) * 128], in0=X3[:, :, 0:128], in1=rr[:, 0:3].unsqueeze(2).broadcast_to([P, 3, 128]), op=ALU.mult),
                            rd=[kX, "rr"], wr=[("ybb", h)])
                    self.op("dve", lambda e, Y=Y, h=h: e.tensor_scalar(out=ybbv[:, 3, h * 128:(h + 1) * 128], in0=Y[:, 0:128], scalar1=rr[:, 3:4], scalar2=None, op0=ALU.mult), rd=[kY, "rr"], wr=[("ybb", h)])
            psTa = [PSb[0][:, :].bitcast(BF16), PSb[1][:, :].bitcast(BF16)]
            for c in range(8):
                src = yabv if c < 4 else ybbv
                skey = "yab" if c < 4 else "ybb"
                h = c % 4
                pT = psTa[c % 2]
                pkT2 = ("ps", c % 2)
                for qb in range(4):
                    self.op("pe", lambda e, src=src, h=h, qb=qb, pT=pT: e.transpose(out=pT[:, qb * 128:(qb + 1) * 128], in_=src[:, qb, h * 128:(h + 1) * 128], identity=ident_b), rd=[skey, "cstb"], wr=[pkT2])
                if c < 4:
                    self.op("dve", lambda e, c=c, pT=pT, h=h: e.tensor_scalar(out=yTv[:, c, :], in0=pT[:, 0:512], scalar1=gdf[:, h:h + 1], scalar2=None, op0=ALU.mult), rd=[pkT2, "gdf"], wr=[("yT", c)])
                else:
                    self.op("act", lambda e, c=c, pT=pT: e.copy(out=yTv[:, c, :], in_=pT[:, 0:512]), rd=[pkT2], wr=[("yT", c)])
            if ("yab%d_%d" % (l, m)) in self.dbg:
                self.debug_dump("yab%d_%d" % (l, m), yT[:, 0:8 * TS], [P, 8 * TS], BF16, rd=["yT"])
            pg.barrier()
            if self.stage == "B2":
                raise StopBuild()

            ar.reset(mkB)
            mT = ar.get(8 * TS, BF16)
            mTv = mT[:, :].rearrange("p (c t) -> p c t", c=8)
            g3 = [ar.get(TS, BF16) for _ in range(3)]
            mt1 = ar.get(TS, F32)
            mt2 = ar.get(TS, F32)
            psrr["lo"], psrr["hi"] = 0, 6
            pieces = []
            o_gt = WIN_IDX["gt"][0]
            for dc in range(8):
                pieces.append([(wb["w_in"][l, :, o_gt + dc * 3072:o_gt + (dc + 1) * 3072], 3072, ("wb_w_in", l)),
                               wpiece("w_pa", l, dc * 512, 512), wpiece("w_pb", l, dc * 512, 512), wpiece("w_pc", l, dc * 1024, 1024)])
            wplan(pieces)
            wplan([[wpiece("w_out", l, 0, 4096)], [wpiece("w_out", l, 4096, 4096)]])
            for dc in range(8):
                Wt, wk = wtake()
                for wh in range(3):
                    proj_fm(Wt, wk, wh * 1024, 16 + dc * 3 + wh, g3[wh][:, :], "g3_%d" % wh, func=AF.Sigmoid)
                psa, pka = proj_fm(Wt, wk, 3072, 0, None, None, rhsv=yTv, rhs_key="yT", nk=4, rhs_c0=0)
                self.op("dve", lambda e, psa=psa: e.tensor_tensor(out=mt1[:, :], in0=g3[0][:, :], in1=psa[:, :], op=ALU.mult), rd=[pka, "g3_0"], wr=["mt1"])
                psb_, pkb = proj_fm(Wt, wk, 3584, 0, None, None, rhsv=yTv, rhs_key="yT", nk=4, rhs_c0=4)
                self.op("dve", lambda e, psb_=psb_: e.tensor_tensor(out=mt2[:, :], in0=g3[1][:, :], in1=psb_[:, :], op=ALU.mult), rd=[pkb, "g3_1"], wr=["mt2"])
                self.op("pool", lambda e: e.tensor_tensor(out=mt1[:, :], in0=mt1[:, :], in1=mt2[:, :], op=ALU.add), rd=["mt1", "mt2"], wr=["mt1"])
                psc, pkc = proj_fm(Wt, wk, 4096, 0, None, None, rhsv=yTv, rhs_key="yT", nk=8, rhs_c0=8)
                self.op("dve", lambda e, psc=psc: e.tensor_tensor(out=mt2[:, :], in0=g3[2][:, :], in1=psc[:, :], op=ALU.mult), rd=[pkc, "g3_2"], wr=["mt2"])
                self.op("pool", lambda e, dc=dc: e.tensor_tensor(out=mTv[:, dc, :], in0=mt1[:, :], in1=mt2[:, :], op=ALU.add), rd=["mt1", "mt2"], wr=[("mT", dc)])
            residual_ln(l, t, mTv, "mT", 8, "w_out_planned", 16, 0)
            pg.barrier()

        def residual_ln(l, t, rhsv, rhs_key, nk, wname, gt0, lnidx):
            tsl = slice(t * TS, (t + 1) * TS)
            xt = xsv[:, :, tsl]
            xall = [("xs", t * 8 + c) for c in range(8)]
            self.op("pool", lambda e: e.tensor_scalar(out=xt, in0=xt, scalar1=ALPHA, scalar2=None, op0=ALU.mult), rd=xall, wr=xall)
            pbanks = [6, 7]
            Wt = wk = None
            for dc in range(8):
                if nk == 8:
                    if dc % 4 == 0:
                        Wt, wk = wtake()
                    woff = (dc % 4) * nk * 128
                else:
                    Wt, wk = wtake()
                    woff = 0
                bi = pbanks[dc % 2]
                ps, pk = PSb[bi], ("ps", bi)
                for kc in range(nk):
                    self.op("pe", lambda e, kc=kc, ps=ps, Wt=Wt, woff=woff: e.matmul(ps[:, :], lhsT=Wt[:, woff + kc * 128:woff + (kc + 1) * 128], rhs=rhsv[:, kc, :], start=(kc == 0), stop=(kc == nk - 1)),
                            rd=[wk, rhs_key], wr=[pk])
                self.op("dve", lambda e, dc=dc, ps=ps: e.scalar_tensor_tensor(out=xsv[:, dc, tsl], in0=ps[:, :], scalar=modT[:, gt0 + dc:gt0 + dc + 1], in1=xsv[:, dc, tsl], op0=ALU.mult, op1=ALU.add),
                        rd=[pk, "modT", ("xs", t * 8 + dc)], wr=[("xs", t * 8 + dc)])
            mk = ar.mark()
            xb = ar.get(8 * TS, BF16)
            xbv = xb[:, :].rearrange("p (c t) -> p c t", c=8)
            sq = ar.get(8 * TS, BF16)
            sqv = sq[:, :].rearrange("p (c t) -> p c t", c=8)
            mean = ar.get(TS, F32)
            msq = ar.get(TS, F32)
            rstd = ar.get(TS, F32)
            tl = [ar.get(TS, F32), ar.get(TS, F32)]
            for kc in range(KC):
                self.op("act", lambda e, kc=kc: e.copy(out=xbv[:, kc, :], in_=xsv[:, kc, tsl]), rd=[("xs", t * 8 + kc)], wr=[("xb", kc)])
                self.op("act", lambda e, kc=kc: e.activation(out=sqv[:, kc, :], in_=xsv[:, kc, tsl], func=AF.Square), rd=[("xs", t * 8 + kc)], wr=[("sq", kc)])
            ps1, pk1 = PSb[4], ("ps", 4)
            ps2, pk2 = PSb[5], ("ps", 5)
            for kc in range(KC):
                self.op("pe", lambda e, kc=kc: e.matmul(ps1[:, :], lhsT=ones_b, rhs=xbv[:, kc, :], start=(kc == 0), stop=(kc == KC - 1)), rd=[("xb", kc), "cstb"], wr=[pk1])
            for kc in range(KC):
                self.op("pe", lambda e, kc=kc: e.matmul(ps2[:, :], lhsT=ones_b, rhs=sqv[:, kc, :], start=(kc == 0), stop=(kc == KC - 1)), rd=[("sq", kc), "cstb"], wr=[pk2])
            self.op("act", lambda e: e.mul(out=mean[:, :], in_=ps1[:, :], mul=1.0 / D), rd=[pk1], wr=["mean"])
            self.op("pool", lambda e: e.tensor_tensor(out=msq[:, :], in0=mean[:, :], in1=mean[:, :], op=ALU.mult), rd=["mean"], wr=["msq"])
            self.op("dve", lambda e: e.scalar_tensor_tensor(out=msq[:, :], in0=ps2[:, :], scalar=1.0 / D, in1=msq[:, :], op0=ALU.mult, op1=ALU.subtract), rd=[pk2, "msq"], wr=["msq"])
            self.op("act", lambda e: e.activation(out=rstd[:, :], in_=msq[:, :], func=AF.Sqrt, bias=epsc[:, 0:1], scale=1.0), rd=["msq", "epsc"], wr=["rstd"])
            self.op("dve", lambda e: e.reciprocal(out=rstd[:, :], in_=rstd[:, :]), rd=["rstd"], wr=["rstd"])
            og, _ = VEC["ln_g"]
            ob, _ = VEC["ln_b"]
            for kc in range(KC):
                tt = tl[kc % 2]
                tk = "tl%d" % (kc % 2)
                self.op("dve", lambda e, kc=kc, tt=tt: e.tensor_tensor(out=tt[:, :], in0=xsv[:, kc, tsl], in1=mean[:, :], op=ALU.subtract), rd=[("xs", t * 8 + kc), "mean"], wr=[tk])
                self.op("pool", lambda e, tt=tt: e.tensor_tensor(out=tt[:, :], in0=tt[:, :], in1=rstd[:, :], op=ALU.mult), rd=[tk, "rstd"], wr=[tk])
                gcol = vec[:, og + lnidx * 8 + kc:og + lnidx * 8 + kc + 1]
                bcol = vec[:, ob + lnidx * 8 + kc:ob + lnidx * 8 + kc + 1]
                self.op("dve", lambda e, kc=kc, tt=tt, gcol=gcol, bcol=bcol: e.tensor_scalar(out=xsv[:, kc, tsl], in0=tt[:, :], scalar1=gcol, scalar2=bcol, op0=ALU.mult, op1=ALU.add),
                        rd=[tk, "vec"], wr=[("xs", t * 8 + kc)])
            ar.reset(mk)

        def exchange2(l):
            ar.reset()
            hh = ar.get(64, BF16)
            hhv = hh[:, :].rearrange("p (c m t) -> p c m t", c=8, m=4)
            for m in range(4):
                for kc in range(KC):
                    src = xsv[:, kc, m * TS + TS - 2:m * TS + TS]
                    self.op("dve", lambda e, src=src, kc=kc, m=m: e.tensor_scalar(out=hhv[:, kc, m, :], in0=src, scalar1=modT[:, 32 + kc:33 + kc], scalar2=modT[:, 24 + kc:25 + kc], op0=ALU.mult, op1=ALU.add),
                            rd=[("xs", m * 8 + kc), "modT"], wr=["hh"])
            self.dma(hls[:, :], hh[:, :], rd=["hh"], wr=["hls"], key="hls")
            agather(hls, hlg, "hls", "hlg")
            hall = ar.get(256, BF16)
            hallv = hall[:, :].rearrange("p (r c m t) -> p r c m t", r=4, c=8, m=4)
            self.dma(hall[:, :].rearrange("p (r x) -> p r x", r=4), hlg[:, :].rearrange("(r p) x -> p r x", p=P), rd=["hlg"], wr=["hall"], key="hall")
            hsf = ar.get(64, F32)
            hsfv = hsf[:, :].rearrange("p (c m t) -> p c m t", c=8, m=4)
            self.op("dve", lambda e: e.tensor_scalar(out=hsfv, in0=hallv[:, 0], scalar1=selA_c[:, 0:1], scalar2=None, op0=ALU.mult), rd=["hall", "cst"], wr=["hsf"])
            for r in range(1, 3):
                self.op("dve", lambda e, r=r: e.scalar_tensor_tensor(out=hsfv, in0=hallv[:, r], scalar=selA_c[:, r:r + 1], in1=hsfv, op0=ALU.mult, op1=ALU.add), rd=["hall", "cst", "hsf"], wr=["hsf"])
            self.op("dve", lambda e: e.scalar_tensor_tensor(out=hsfv[:, :, 1:4, :], in0=hallv[:, 3, :, 0:3, :], scalar=selB_c[:, 3:4], in1=hsfv[:, :, 1:4, :], op0=ALU.mult, op1=ALU.add),
                    rd=["hall", "cst", "hsf"], wr=["hsf"])
            self.op("dve", lambda e: e.tensor_copy(out=halo[:, :], in_=hsf[:, :]), rd=["hsf"], wr=["halo"])
            pg.barrier()

        def phaseC_tile(l, t):
            m = t
            ar.reset()
            aT = ar.get(NFC * TS, BF16)
            aTv = aT[:, :].rearrange("p (c t) -> p c t", c=NFC)
            ub = [ar.get(TS + 2, F32), ar.get(TS + 2, F32)]
            cv_ = [ar.get(TS, F32), ar.get(TS, F32)]
            ge = [ar.get(TS, F32), ar.get(TS, F32)]
            halov = halo[:, :].rearrange("p (c m t) -> p c m t", c=8, m=4)
            make_hT(t, 1)
            psrr["lo"], psrr["hi"] = 0, 4
            wplan([[wpiece("w_up", l, fc * 2048, 4096)] for fc in range(0, NFC, 2)])
            wplan([[wpiece("w_down", l, dc * NFC * 128, NFC * 128)] for dc in range(8)])
            ow, _ = VEC["w_conv"]
            obc, _ = VEC["b_conv"]
            psH, pkH = PSb[4], ("ps", 4)
            Wt = wk = None
            for fc in range(NFC):
                if fc % 2 == 0:
                    Wt, wk = wtake()
                wo = (fc % 2) * 2048
                psu, pku = next_ps()
                psg, pkg = next_ps()
                for kc in range(KC):
                    self.op("pe", lambda e, kc=kc, psu=psu, Wt=Wt, wo=wo: e.matmul(psu[:, :], lhsT=Wt[:, wo + kc * 128:wo + (kc + 1) * 128], rhs=hTv[:, kc, :], start=(kc == 0), stop=(kc == KC - 1)),
                            rd=[wk, "hT"], wr=[pku])
                for kc in range(KC):
                    self.op("pe", lambda e, kc=kc, Wt=Wt, wo=wo, fc=fc: e.matmul(psH[:, fc * 2:fc * 2 + 2], lhsT=Wt[:, wo + kc * 128:wo + (kc + 1) * 128], rhs=halov[:, kc, m, :], start=(kc == 0), stop=(kc == KC - 1)),
                            rd=[wk, "halo"], wr=[pkH])
                for kc in range(KC):
                    self.op("pe", lambda e, kc=kc, psg=psg, Wt=Wt, wo=wo: e.matmul(psg[:, :], lhsT=Wt[:, wo + 1024 + kc * 128:wo + 1024 + (kc + 1) * 128], rhs=hTv[:, kc, :], start=(kc == 0), stop=(kc == KC - 1)),
                            rd=[wk, "hT"], wr=[pkg])
                u = ub[fc % 2]
                uk = "ub%d" % (fc % 2)
                c_ = cv_[fc % 2]
                ck_ = "cv%d" % (fc % 2)
                g_ = ge[fc % 2]
                gk_ = "ge%d" % (fc % 2)
                self.op("act", lambda e, u=u, psu=psu: e.copy(out=u[:, 2:TS + 2], in_=psu[:, :]), rd=[pku], wr=[(uk, 1)])
                self.op("dve", lambda e, u=u, fc=fc: e.tensor_copy(out=u[:, 0:2], in_=psH[:, fc * 2:fc * 2 + 2]), rd=[pkH], wr=[(uk, 0)])
                w0 = vec[:, ow + 0 * NFC + fc:ow + 0 * NFC + fc + 1]
                w1 = vec[:, ow + 1 * NFC + fc:ow + 1 * NFC + fc + 1]
                w2 = vec[:, ow + 2 * NFC + fc:ow + 2 * NFC + fc + 1]
                bc = vec[:, obc + fc:obc + fc + 1]
                self.op("dve", lambda e, u=u, c_=c_, w0=w0, bc=bc: e.tensor_scalar(out=c_[:, :], in0=u[:, 0:TS], scalar1=w0, scalar2=bc, op0=ALU.mult, op1=ALU.add), rd=[uk, "vec"], wr=[ck_])
                self.op("dve", lambda e, u=u, c_=c_, w1=w1: e.scalar_tensor_tensor(out=c_[:, :], in0=u[:, 1:TS + 1], scalar=w1, in1=c_[:, :], op0=ALU.mult, op1=ALU.add), rd=[uk, "vec", ck_], wr=[ck_])
                self.op("dve", lambda e, u=u, c_=c_, w2=w2: e.scalar_tensor_tensor(out=c_[:, :], in0=u[:, 2:TS + 2], scalar=w2, in1=c_[:, :], op0=ALU.mult, op1=ALU.add), rd=[uk, "vec", ck_], wr=[ck_])
                self.op("act", lambda e, c_=c_, g_=g_: e.activation(out=g_[:, :], in_=c_[:, :], func=AF.Gelu), rd=[ck_], wr=[gk_])
                self.op("dve", lambda e, g_=g_, psg=psg, fc=fc: e.tensor_tensor(out=aTv[:, fc, :], in0=g_[:, :], in1=psg[:, :], op=ALU.mult), rd=[gk_, pkg], wr=[("aT", fc)])
            residual_ln(l, t, aTv, "aT", NFC, "w_down", 40, 1)
            pg.barrier()

        def stop(tag):
            if self.stage == tag:
                raise StopBuild()

        try:
            for l in range(L):
                layer_prelude(l)
                for t in range(NSEG):
                    phaseA(l, t)
                stop("A")
                exchange1()
                stop("X1")
                self.cast(l + 1, CG0, 0)
                phaseB_prelude(l)
                stop("P")
                for t in range(NSEG):
                    phaseB_tile(l, t)
                    if t == 1:
                        self.cast(l + 1, CG1, 1)
                self.cast(l + 1, CG2, 2)
                stop("B")
                exchange2(l)
                stop("X2")
                for t in range(NSEG):
                    phaseC_tile(l, t)
        except StopBuild:
            pg.barrier()

        ar.reset()
        yo = [ar.get(D, F32), ar.get(D, F32)]
        for tb in range(NT // P):
            yt = yo[tb % 2]
            yk = "yo%d" % (tb % 2)
            for k2 in range(2):
                bi = (tb * 2 + k2) % 4
                ps = PSb[bi]
                pkey = ("ps", bi)
                for c in range(4):
                    kc = k2 * 4 + c
                    self.op("pe", lambda e, ps=ps, c=c, kc=kc, tb=tb: e.transpose(out=ps[:, c * P:(c + 1) * P], in_=xsv[:, kc, tb * P:(tb + 1) * P], identity=ident_f),
                            rd=[("xs", (tb // 4) * 8 + kc), "cst"], wr=[pkey])
                if k2 == 0:
                    self.op("dve", lambda e, yt=yt, ps=ps: e.tensor_copy(out=yt[:, 0:512], in_=ps[:, :]), rd=[pkey], wr=[(yk, 0)])
                else:
                    self.op("act", lambda e, yt=yt, ps=ps: e.copy(out=yt[:, 512:1024], in_=ps[:, :]), rd=[pkey], wr=[(yk, 1)])
            self.dma(y_out[tb * P:(tb + 1) * P, :], yt[:, :], rd=[yk], wr=["y"], key="yout")

        pg.resolve()
        sems = {}
        for e in Prog.ENGS:
            sems[("eng", e)] = self.es.enter_context(nc.semaphore("s_" + e))
        for k in self.dma_keys:
            sems[("dma", k)] = self.es.enter_context(nc.semaphore("d_" + k))
        with nc.Block() as block:
            pg.emit(nc, block, sems)
        self.es.close()
        return nc


def make_in_maps(inputs, L=DEPTH):
    hl = host_layout(inputs, L)
    x = np.asarray(inputs["x"], np.float32)
    c = np.asarray(inputs["c"], np.float32)
    maps = []
    for r in range(8):
        b, j = r // 4, r % 4
        toks = np.concatenate([np.arange((j + 4 * m) * TS, (j + 4 * m + 1) * TS) for m in range(NSEG)])
        cst, cstb = host_constants(j)
        m = {"x": np.ascontiguousarray(x[b][toks]), "cT": np.ascontiguousarray(c[b].reshape(KC, P).T), "cst": cst, "cstb": cstb,
             "vec": hl["vec"], "rows": hl["rows"], "bro": hl["bro"]}
        for n in WSHAPES:
            m[n] = hl[n]
        maps.append(m)
    return maps


def kernel(**inputs):
    inputs = {k: np.asarray(v) for k, v in inputs.items()}
    b = Builder()
    nc = b.build()
    maps = make_in_maps(inputs)
    res = run_bass_kernel_spmd(nc, maps, core_ids=list(range(8)))
    out = np.zeros((2, 8192, D), np.float32)
    for r in range(8):
        bb, j = r // 4, r % 4
        y = res.results[r]["y"]
        for m in range(NSEG):
            g = j + 4 * m
            out[bb, g * TS:(g + 1) * TS] = y[m * TS:(m + 1) * TS]
    return out
```

```python
import math
from contextlib import ExitStack

import numpy as np
import ml_dtypes

import concourse.bass as bass
import concourse.mybir as mybir
from concourse.bass_utils import run_bass_kernel_spmd

F32 = mybir.dt.float32
BF16 = mybir.dt.bfloat16
AF = mybir.ActivationFunctionType
ALU = mybir.AluOpType
AX = mybir.AxisListType

P = 128
D = 1024
KC = 8
DEPTH = 4
NT = 2048
TS = 512
NSEG = 4
DFF = 2816
NFC = 22
LN_EPS = 1e-5
ALPHA = (2 * DEPTH) ** 0.25
AQ, AK, AV, BQ, BK, BV, BFc, CQ, CK, CV, CG, GT, DIN = 0, 512, 1024, 1536, 2048, 2560, 3072, 3076, 3588, 4100, 5124, 6148, 9220
SCALE_A = 64 ** -0.5
SCALE_B = 128 ** -0.5
NEG = -30000.0
GAM = [1.0 - 2.0 ** (-5.0 - h) for h in range(4)]
G128 = [g ** 128 for g in GAM]
G512 = [g ** 512 for g in GAM]
KVW = 2048 + 2048 + 2080 + 2080
SMW = 64 + 4096


class Prog:
    ENGS = ("pe", "act", "dve", "pool", "sp")

    def __init__(self):
        self.ops = []

    def add(self, eng, fn, rd=(), wr=(), dma=None, inc=16):
        self.ops.append(dict(eng=eng, fn=fn, rd=list(rd), wr=list(wr), dma=dma, inc=inc))

    def barrier(self):
        self.ops.append(dict(barrier=True))

    @staticmethod
    def _split(key):
        if isinstance(key, tuple):
            return key[0], key[1]
        return key, None

    def resolve(self):
        state = {}
        ops = self.ops
        last_on_eng = {}
        last_dma = {}
        pend = {e: set() for e in self.ENGS}

        def touch(key):
            name, idx = self._split(key)
            d = state.setdefault(name, {})
            if idx is None:
                d.setdefault(None, dict(w=None, r={}))
                return list(d.values())
            d.setdefault(idx, dict(w=None, r={}))
            res = [d[idx]]
            if None in d:
                res.append(d[None])
            return res

        for i, op in enumerate(ops):
            if op.get("barrier"):
                allp = set(last_on_eng.values()) | set(last_dma.values())
                for e in self.ENGS:
                    pend[e] |= allp
                continue
            deps = set()
            for k in op["rd"]:
                for st in touch(k):
                    if st["w"] is not None:
                        deps.add(st["w"])
            for k in op["wr"]:
                for st in touch(k):
                    if st["w"] is not None:
                        deps.add(st["w"])
                    deps.update(st["r"].values())
            for k in op["rd"]:
                for st in touch(k):
                    st["r"][op["eng"] if not op["dma"] else ("dma", op["dma"])] = i
            for k in op["wr"]:
                name, idx = self._split(k)
                for st in touch(k):
                    st["w"] = i
                    st["r"] = {}
            deps |= pend[op["eng"]]
            pend[op["eng"]] = set()
            deps.discard(i)
            op["deps"] = deps
            if op["dma"]:
                last_dma[op["dma"]] = i
            else:
                last_on_eng[op["eng"]] = i
        for op in ops:
            if op.get("barrier"):
                continue
            op["signal"] = bool(op["dma"])
        for op in ops:
            if op.get("barrier"):
                continue
            for p in op["deps"]:
                po = ops[p]
                if not po["dma"] and po["eng"] == "pe" and op["eng"] == "pe" and not op["dma"]:
                    continue
                po["signal"] = True
        cnt = {}
        dma_hist = {}
        for i, op in enumerate(ops):
            if op.get("barrier"):
                continue
            if op["dma"]:
                k = ("dma", op["dma"])
                cnt[k] = cnt.get(k, 0) + op["inc"]
                op["sigval"] = cnt[k]
                dma_hist.setdefault(op["dma"], []).append((i, cnt[k]))
            elif op["signal"]:
                k = ("eng", op["eng"])
                cnt[k] = cnt.get(k, 0) + 1
                op["sigval"] = cnt[k]
        self.dma_hist = dma_hist
        self.final = {k[1]: v for k, v in cnt.items() if k[0] == "dma"}

    def emit(self, nc, block, sems):
        ops = self.ops
        import bisect
        hist_idx = {k: [h[0] for h in v] for k, v in self.dma_hist.items()}

        def run(engname, e):
            waited = {}
            for i, op in enumerate(ops):
                if op.get("barrier") or op["eng"] != engname:
                    continue
                need = {}
                for p in op["deps"]:
                    po = ops[p]
                    if po["dma"]:
                        key = ("dma", po["dma"])
                        hi = hist_idx[po["dma"]]
                        pos = bisect.bisect_left(hi, i) - 1
                        val = self.dma_hist[po["dma"]][pos][1]
                    else:
                        if po["eng"] == "pe" and engname == "pe" and not op["dma"]:
                            continue
                        key = ("eng", po["eng"])
                        val = po["sigval"]
                    if val > need.get(key, 0):
                        need[key] = val
                for key, val in need.items():
                    if waited.get(key, 0) >= val:
                        continue
                    e.wait_ge(sems[key], val)
                    waited[key] = val
                ins = op["fn"](e)
                if op["dma"]:
                    ins.then_inc(sems[("dma", op["dma"])], op["inc"])
                elif op["signal"]:
                    ins.then_inc(sems[("eng", engname)], 1)
            if engname == "sp":
                for k, v in self.final.items():
                    if waited.get(("dma", k), 0) < v:
                        e.wait_ge(sems[("dma", k)], v)

        block.tensor(lambda e: run("pe", e))
        block.scalar(lambda e: run("act", e))
        block.vector(lambda e: run("dve", e))
        block.gpsimd(lambda e: run("pool", e))
        block.sync(lambda e: run("sp", e))


def pack_cols(w, blocks):
    K = w.shape[0]
    kc = K // P
    parts = []
    for c0, wd in blocks:
        blk = w[:, c0:c0 + wd].reshape(kc, P, wd).transpose(1, 0, 2).reshape(P, kc * wd)
        parts.append(blk)
    return np.ascontiguousarray(np.concatenate(parts, axis=1))


def win_blocks():
    blocks = []
    index = {}
    off = 0

    def addg(name, c0, width, n):
        nonlocal off
        index[name] = (off, width, n)
        for i in range(n):
            blocks.append((c0 + i * width, width))
            off += KC * width

    addg("ak", AK, 128, 4)
    addg("bk", BK, 128, 4)
    addg("av", AV, 512, 1)
    addg("bv", BV, 512, 1)
    addg("ck", CK, 512, 1)
    addg("cv", CV, 512, 2)
    addg("aq", AQ, 128, 4)
    addg("bq", BQ, 128, 4)
    addg("cq", CQ, 512, 1)
    addg("cg", CG, 128, 8)
    index["gt"] = (off, 128, 24)
    for dc in range(8):
        for wh in range(3):
            blocks.append((GT + wh * 1024 + dc * 128, 128))
            off += KC * 128
    index["bf"] = (off, 4, 1)
    blocks.append((BFc, 4))
    off += KC * 4
    return blocks, index, off


WIN_BLOCKS, WIN_IDX, WIN_TOT = win_blocks()
ROW_OFF = {}
_o = 0
for _n, _c0, _w in (("av", AV, 512), ("bv", BV, 512), ("bf", BFc, 4), ("ck", CK, 512), ("cv", CV, 1024), ("cq", CQ, 512)):
    ROW_OFF[_n] = (_o, _c0, _w)
    _o += _w
ROW_TOT = _o + (_o % 2)
VEC = {}
_o = 0
for _n, _w in (("b_ada", 48), ("b_fm", 48), ("ln_g", 16), ("ln_b", 16), ("w_conv", 66), ("b_conv", 22), ("g_diff", 4), ("g_ret", 8)):
    VEC[_n] = (_o, _w)
    _o += _w
VEC_TOT = _o
BRO_TOT = 256
CST = {}
_o = 0
for _n, _w in (("ident", 128), ("tri", 128), ("ones", 128), ("causal", 128), ("cos", 16 * 64), ("sin", 16 * 64), ("kdec", 4), ("gpow", 4),
               ("alpha", 4), ("beta", 4), ("sel", 4), ("selA", 4), ("selB", 4), ("sel4", 512)):
    CST[_n] = (_o, _w)
    _o += _w
CST_TOT = _o
CSTB = {"ident": (0, 128), "diagA": (128, 2048), "diagB": (2176, 2048), "ones": (4224, 128)}
CSTB_TOT = 4352


def host_constants(j):
    cst = np.zeros((P, CST_TOT), np.float32)
    cstb = np.zeros((P, CSTB_TOT), np.float32)

    def put(name, arr):
        o, w = CST[name]
        cst[:, o:o + w] = arr.reshape(P, w)

    i = np.arange(P)
    put("ident", np.eye(P, dtype=np.float32))
    put("tri", (i[:, None] <= i[None, :]).astype(np.float32))
    put("ones", np.ones((P, P), np.float32))
    put("causal", (i[:, None] <= i[None, :]).astype(np.float32))
    half = 64
    inv = (1.0 / (np.float32(10000.0) ** np.linspace(0.0, 1.0, half, dtype=np.float32))).astype(np.float32)
    cos = np.zeros((P, 16, half), np.float32)
    sin = np.zeros((P, 16, half), np.float32)
    for m in range(NSEG):
        g = j + 4 * m
        for tb in range(4):
            pos = (g * TS + tb * P + i).astype(np.float32)
            ang = (pos[:, None] * inv[None, :]).astype(np.float32)
            cos[:, 4 * m + tb, :] = np.cos(ang)
            sin[:, 4 * m + tb, :] = np.sin(ang)
    put("cos", cos)
    put("sin", sin)
    kdec = np.stack([(128.0 ** -0.5) * (GAM[h] ** (-(i + 1.0))) for h in range(4)], 1)
    gpow = np.stack([GAM[h] ** (i + 1.0) for h in range(4)], 1)
    put("kdec", kdec.astype(np.float32))
    put("gpow", gpow.astype(np.float32))
    r = np.arange(4)
    put("alpha", np.broadcast_to((r >= j).astype(np.float32), (P, 4)))
    put("beta", np.broadcast_to(np.where(r > j, NEG, 0.0).astype(np.float32), (P, 4)))
    put("sel", np.broadcast_to((r == j).astype(np.float32), (P, 4)))
    put("selA", np.broadcast_to((r == j - 1).astype(np.float32), (P, 4)))
    put("selB", np.broadcast_to(((r == 3) & (j == 0)).astype(np.float32), (P, 4)))
    s4 = np.zeros((P, 4, P), np.float32)
    for h in range(4):
        s4[h, h, :] = 1.0
    put("sel4", s4)
    o, w = CSTB["ident"]
    cstb[:, o:o + w] = np.eye(P)
    kk = i[:, None]
    q = np.arange(TS)[None, :]
    dA = np.zeros((P, 4, TS), np.float32)
    dB = np.zeros((P, 4, TS), np.float32)
    for kb in range(4):
        k = kb * P + kk
        dA[:, kb, :] = np.where((k // 64) <= (q // 64), 0.0, NEG)
        dB[:, kb, :] = np.where(k <= q, 0.0, NEG)
    o, w = CSTB["diagA"]
    cstb[:, o:o + w] = dA.reshape(P, w)
    o, w = CSTB["diagB"]
    cstb[:, o:o + w] = dB.reshape(P, w)
    o, w = CSTB["ones"]
    cstb[:, o:o + w] = 1.0
    return cst, cstb.astype(ml_dtypes.bfloat16)


_HOSTW_CACHE = {}


def host_layout(inputs, L=DEPTH):
    out = {}
    w_ada = inputs["w_ada"]
    out["w_ada"] = np.stack([pack_cols(w_ada[l], [(c * 128, 128) for c in range(48)]) for l in range(L)])
    w_in = inputs["w_in"]
    out["w_in"] = np.stack([pack_cols(w_in[l], WIN_BLOCKS) for l in range(L)])
    out["w_pa"] = np.stack([pack_cols(inputs["w_pa"][l], [(c * 128, 128) for c in range(8)]) for l in range(L)])
    out["w_pb"] = np.stack([pack_cols(inputs["w_pb"][l], [(c * 128, 128) for c in range(8)]) for l in range(L)])
    out["w_pc"] = np.stack([pack_cols(inputs["w_pc"][l], [(c * 128, 128) for c in range(8)]) for l in range(L)])
    out["w_out"] = np.stack([pack_cols(inputs["w_out"][l], [(c * 128, 128) for c in range(8)]) for l in range(L)])
    ub = []
    for fc in range(NFC):
        ub.append((fc * 128, 128))
        ub.append((DFF + fc * 128, 128))
    out["w_up"] = np.stack([pack_cols(inputs["w_up"][l], ub) for l in range(L)])
    out["w_down"] = np.stack([pack_cols(inputs["w_down"][l], [(c * 128, 128) for c in range(8)]) for l in range(L)])
    vec = np.zeros((L, P, VEC_TOT), np.float32)
    for l in range(L):
        def putv(name, arr):
            o, w = VEC[name]
            vec[l, :, o:o + w] = arr
        putv("b_ada", inputs["b_ada"][l].reshape(48, P).T)
        b_in = inputs["b_in"][l]
        fm = []
        for c0 in (AK, BK, AQ, BQ):
            for h in range(4):
                fm.append(b_in[c0 + h * 128:c0 + (h + 1) * 128])
        for dc in range(8):
            for wh in range(3):
                c0 = GT + wh * 1024 + dc * 128
                fm.append(b_in[c0:c0 + 128])
        for ec in range(8):
            fm.append(b_in[CG + ec * 128:CG + (ec + 1) * 128])
        putv("b_fm", np.stack(fm, 1))
        putv("g_diff", inputs["g_diff"][l].reshape(4, P).T)
        putv("g_ret", inputs["g_ret"][l].reshape(8, P).T)
        putv("ln_g", inputs["ln_g"][l].reshape(2, 8, P).transpose(2, 0, 1).reshape(P, 16))
        putv("ln_b", inputs["ln_b"][l].reshape(2, 8, P).transpose(2, 0, 1).reshape(P, 16))
        putv("w_conv", inputs["w_conv"][l].reshape(3, NFC, P).transpose(2, 0, 1).reshape(P, 66))
        putv("b_conv", inputs["b_conv"][l].reshape(NFC, P).T)
    out["vec"] = vec
    rows = np.zeros((L, 1, ROW_TOT), np.float32)
    for l in range(L):
        for n, (o, c0, w) in ROW_OFF.items():
            rows[l, 0, o:o + w] = inputs["b_in"][l][c0:c0 + w]
    out["rows"] = rows
    bro = np.zeros((L, 1, BRO_TOT), np.float32)
    for l in range(L):
        bro[l, 0, 0:64] = inputs["lam_q1"][l]
        bro[l, 0, 64:128] = inputs["lam_k1"][l]
        bro[l, 0, 128:192] = inputs["lam_q2"][l]
        bro[l, 0, 192:256] = inputs["lam_k2"][l]
    out["bro"] = bro
    return out


WSHAPES = {
    "w_ada": 48 * KC * 128, "w_in": WIN_TOT, "w_pa": 8 * 4 * 128, "w_pb": 8 * 4 * 128, "w_pc": 8 * 8 * 128,
    "w_out": 8 * 8 * 128, "w_up": 44 * 8 * 128, "w_down": 8 * NFC * 128,
}


class StopBuild(Exception):
    pass


class Builder:
    def __init__(self, nlayers=DEPTH, stage="full", dbg=()):
        self.nlayers = nlayers
        self.stage = stage
        self.dbg = set(dbg)
        self.nc = bass.Bass("TRN2", target_bir_lowering=False)
        self.pg = Prog()
        self.es = ExitStack()
        self.dbg_out = {}
        self.dma_keys = []

    def sb(self, name, nfree, dt):
        t = self.es.enter_context(self.nc.sbuf_tensor("sb_" + name, [P, nfree], dt))
        return t

    def dma(self, out, in_, rd, wr, key, q="sp"):
        if key not in self.dma_keys:
            self.dma_keys.append(key)
        self.pg.add(q, lambda e, o=out, i=in_: e.dma_start(out=o, in_=i), rd=rd, wr=wr, dma=key)

    def op(self, eng, fn, rd, wr):
        self.pg.add(eng, fn, rd=rd, wr=wr)

    def debug_dump(self, name, ap_sb, shape, dt, rd):
        if name not in self.dbg:
            return
        t = self.nc.dram_tensor("dbg_" + name, list(shape), dt, kind="ExternalOutput").ap()
        self.dbg_out[name] = t
        self.dma(t, ap_sb, rd=rd, wr=["dbg_" + name], key="dbg_" + name)

    def build(self):
        nc = self.nc
        pg = self.pg
        L = self.nlayers
        dram = nc.dram_tensor
        x_in = dram("x", [NT, D], F32, kind="ExternalInput").ap()
        cT_in = dram("cT", [P, KC], F32, kind="ExternalInput").ap()
        wsrc = {n: dram(n, [L, P, WSHAPES[n]], F32, kind="ExternalInput").ap() for n in WSHAPES}
        vec_in = dram("vec", [L, P, VEC_TOT], F32, kind="ExternalInput").ap()
        rows_in = dram("rows", [L, 1, ROW_TOT], F32, kind="ExternalInput").ap()
        bro_in = dram("bro", [L, 1, BRO_TOT], F32, kind="ExternalInput").ap()
        cst_in = dram("cst", [P, CST_TOT], F32, kind="ExternalInput").ap()
        cstb_in = dram("cstb", [P, CSTB_TOT], BF16, kind="ExternalInput").ap()
        y_out = dram("y", [NT, D], F32, kind="ExternalOutput").ap()
        wb = {n: dram("wb_" + n, [L, P, WSHAPES[n]], BF16).ap() for n in WSHAPES}
        kK = [dram("kK%d" % m, [P, 4096], BF16) for m in range(NSEG)]
        kV = [dram("kV%d" % m, [P, 4096], BF16) for m in range(NSEG)]
        sL = [dram("sL%d" % m, [P, 1024], F32) for m in range(NSEG)]
        sG = dram("sG", [P, 64], F32)
        gK = [dram("gK%d" % m, [4 * P, 4096], BF16) for m in range(NSEG)]
        gV = [dram("gV%d" % m, [4 * P, 4096], BF16) for m in range(NSEG)]
        gL = [dram("gL%d" % m, [4 * P, 1024], F32) for m in range(NSEG)]
        gG = dram("gG", [4 * P, 64], F32)
        hls = dram("hls", [P, 64], BF16)
        hlg = dram("hlg", [4 * P, 64], BF16)
        self.wb = wb

        sb = self.sb
        xs = sb("xs", KC * NT, F32)
        xsv = xs[:, :].rearrange("p (k t) -> p k t", k=KC)
        cst = sb("cst", CST_TOT, F32)
        cstb = sb("cstb", CSTB_TOT, BF16)
        vec = sb("vec", VEC_TOT, F32)
        modT = sb("modT", 48, F32)
        hT = sb("hT", KC * TS, BF16)
        hTv = hT[:, :].rearrange("p (k t) -> p k t", k=KC)
        W = [sb("W0", 5120, BF16), sb("W1", 5120, BF16)]
        Wbf = sb("Wbf", KC * 4, BF16)
        brow = sb("brow", ROW_TOT, BF16)
        Gloc = sb("Gloc", 64, F32)
        cT = sb("cTs", KC, F32)
        cact = sb("cact", KC, BF16)
        lamv = sb("lamv", 8, F32)
        gdf = sb("gdf", 4, F32)
        epsc = sb("epsc", 2, F32)
        Gk = sb("Gk", 256, F32)
        off_rm = sb("off_rm", 64, F32)
        offq = sb("offq", 16, F32)
        Rin = sb("Rin", 4096, F32)
        halo = sb("halo", 64, BF16)
        PSb = [self.es.enter_context(nc.psum_tensor(f"ps{i}", [P, 512], F32)) for i in range(8)]

        def cs(name):
            o, w = CST[name]
            return cst[:, o:o + w]

        def csb(name):
            o, w = CSTB[name]
            return cstb[:, o:o + w]

        ident_f = cs("ident")
        ident_b = csb("ident")
        ones_b = csb("ones")

        CH = 8192

        def cast(l, names, gi):
            if l >= L:
                return
            for n in names:
                tot = WSHAPES[n]
                for c0 in range(0, tot, CH):
                    c1 = min(tot, c0 + CH)
                    self.dma(wb[n][l, :, c0:c1], wsrc[n][l, :, c0:c1], rd=[], wr=[("wb_" + n, l)], key="cast%d_%d" % (l % 2, gi), q="pool")

        CG0, CG1, CG2 = ["w_ada", "w_in"], ["w_pa", "w_pb", "w_pc", "w_out"], ["w_up", "w_down"]
        self.cast = cast
        cast(0, CG0, 0)
        cast(0, CG1, 1)
        cast(0, CG2, 2)

        self.dma(cst[:, :], cst_in[:, :], rd=[], wr=["cst"], key="cst")
        self.dma(cstb[:, :], cstb_in[:, :], rd=[], wr=["cstb"], key="cstb")
        self.dma(cT[:, :], cT_in[:, :], rd=[], wr=["cT"], key="cst")
        self.op("act", lambda e: e.activation(out=cact[:, :], in_=cT[:, :], func=AF.Silu), rd=["cT"], wr=["cact"])
        self.op("pool", lambda e: e.memset(epsc[:, :], LN_EPS), rd=[], wr=["epsc"])
        self.op("pool", lambda e: e.memset(halo[:, :], 0.0), rd=[], wr=["halo"])

        wstate = dict(n=0, queue=[], loaded=0)

        def wplan(lst):
            wstate["queue"].extend(lst)

        def _wload(idx):
            slot = idx % 2
            pieces = wstate["queue"][idx]
            o = 0
            for pi, (src, n, rdk) in enumerate(pieces):
                self.dma(W[slot][:, o:o + n], src, rd=[rdk], wr=[("W%d" % slot, pi)] if len(pieces) > 1 else ["W%d" % slot], key="W%d" % slot)
                o += n

        def wtake():
            idx = wstate["n"]
            while wstate["loaded"] <= min(idx + 1, len(wstate["queue"]) - 1):
                _wload(wstate["loaded"])
                wstate["loaded"] += 1
            wstate["n"] += 1
            return W[idx % 2], "W%d" % (idx % 2)

        def win_piece(l, name, i=0, n=1):
            off, width, nb = WIN_IDX[name]
            o = off + i * KC * width
            return (wb["w_in"][l, :, o:o + n * KC * width], n * KC * width, ("wb_w_in", l))

        def wpiece(wn, l, o, n):
            return (wb[wn][l, :, o:o + n], n, ("wb_" + wn, l))

        ARENA_ELEMS = 33 * 1024
        arena_t = sb("arena", ARENA_ELEMS, BF16)

        class Arena:
            def __init__(s):
                s.off = 0
                s.peak = 0

            def get(s, n, dt):
                nb = n * (2 if dt == F32 else 1)
                nb = (nb + 31) // 32 * 32
                assert s.off + nb <= ARENA_ELEMS, ("arena overflow", s.off, nb)
                v = arena_t[:, s.off:s.off + nb]
                s.off += nb
                s.peak = max(s.peak, s.off)
                if dt == F32:
                    return v.bitcast(F32)[:, 0:n]
                return v[:, 0:n]

            def mark(s):
                return s.off

            def reset(s, mk=0):
                s.off = mk

        ar = Arena()
        self.arena = ar
        RG = [[0, 1, 2, 3], [4, 5, 6, 7]]
        tri = cs("tri")
        onesf = cs("ones")
        causal = cs("causal")
        cosv = cs("cos").rearrange("p (t d) -> p t d", t=16)
        sinv = cs("sin").rearrange("p (t d) -> p t d", t=16)
        kdec = cs("kdec")
        gpow = cs("gpow")
        alpha_c = cs("alpha")
        beta_c = cs("beta")
        sel_c = cs("sel")
        selA_c = cs("selA")
        selB_c = cs("selB")
        sel4 = cs("sel4")
        diagA = csb("diagA").rearrange("p (k q) -> p k q", k=4)
        diagB = csb("diagB").rearrange("p (k q) -> p k q", k=4)

        def agather(src, dst, skey, dkey):
            if "cc" not in self.dma_keys:
                self.dma_keys.append("cc")
            pg.add("pool", lambda e, src=src, dst=dst: e.collective_compute("AllGather", ALU.bypass, replica_groups=RG, ins=[src.ap().opt()], outs=[dst.ap().opt()]),
                   rd=[skey], wr=[dkey], dma="cc", inc=1)

        def vcol(name, j=0, n=1):
            o, w = VEC[name]
            return vec[:, o + j:o + j + n]

        ar.reset()
        xin_ts = [ar.get(D, F32), ar.get(D, F32)]
        for tb in range(NT // P):
            xin_t = xin_ts[tb % 2]
            xk = "xin_t%d" % (tb % 2)
            self.dma(xin_t, x_in[tb * P:(tb + 1) * P, :], rd=[], wr=[xk], key=xk)
            for k2 in range(2):
                bi = (tb * 2 + k2) % 4
                ps = PSb[bi]
                pkey = ("ps", bi)
                for c in range(4):
                    kc = k2 * 4 + c
                    self.op("pe", lambda e, ps=ps, c=c, kc=kc, xin_t=xin_t: e.transpose(out=ps[:, c * P:(c + 1) * P], in_=xin_t[:, kc * P:(kc + 1) * P], identity=ident_f),
                            rd=[xk, "cst"], wr=[pkey])
                outv = xsv[:, k2 * 4:(k2 + 1) * 4, tb * P:(tb + 1) * P]
                inv = ps[:, :].rearrange("p (c t) -> p c t", c=4)
                xkeys = [("xs", (tb // 4) * 8 + k2 * 4 + c) for c in range(4)]
                if k2 == 0:
                    self.op("dve", lambda e, o=outv, i=inv: e.tensor_copy(out=o, in_=i), rd=[pkey], wr=xkeys)
                else:
                    self.op("act", lambda e, o=outv, i=inv: e.copy(out=o, in_=i), rd=[pkey], wr=xkeys)
        pg.barrier()

        def layer_prelude(l):
            ar.reset()
            brf = ar.get(1024, F32)
            brb = ar.get(1024, BF16)
            lamt = ar.get(256, F32)
            lamp = ar.get(128, F32)
            self.dma(vec[:, :], vec_in[l, :, :], rd=[], wr=["vec"], key="vec")
            self.op("pool", lambda e: e.memset(brow[0:64, :], 0.0), rd=[], wr=["brow"])
            for c0 in range(0, ROW_TOT, 1024):
                c1 = min(ROW_TOT, c0 + 1024)
                n = c1 - c0
                self.dma(brf[0:1, 0:n], rows_in[l, :, c0:c1], rd=[], wr=["brf"], key="brf")
                self.dma(brf[32:33, 0:n], rows_in[l, :, c0:c1], rd=[], wr=["brf"], key="brf")
                self.op("dve", lambda e, c0=c0, c1=c1, n=n: e.tensor_copy(out=brow[0:1, c0:c1], in_=brf[0:1, 0:n]), rd=["brf"], wr=["brow"])
                self.op("dve", lambda e, n=n: e.tensor_copy(out=brb[32:33, 0:n], in_=brf[32:33, 0:n]), rd=["brf"], wr=["brb"])
                self.op("dve", lambda e, c0=c0, c1=c1, n=n: e.tensor_tensor(out=brow[32:33, c0:c1], in0=brf[32:33, 0:n], in1=brb[32:33, 0:n], op=ALU.subtract), rd=["brf", "brb"], wr=["brow"])
            lam_init = 0.8 - 0.6 * math.exp(-0.3 * l)
            self.dma(lamt[:, :], bro_in[l, :, :].partition_broadcast(P)[:, 0, :], rd=[], wr=["lamt"], key="lamt")
            lt3 = lamt[:, :].rearrange("p (a b d) -> p a b d", a=2, b=2)
            lp3 = lamp[:, :].rearrange("p (a d) -> p a d", a=2)
            self.op("dve", lambda e: e.tensor_tensor(out=lp3, in0=lt3[:, :, 0, :], in1=lt3[:, :, 1, :], op=ALU.mult), rd=["lamt"], wr=["lamp"])
            self.op("dve", lambda e: e.reduce_sum(out=lamv[:, 0:2], in_=lp3, axis=AX.X), rd=["lamp"], wr=["lamv"])
            self.op("act", lambda e: e.activation(out=lamv[:, 2:4], in_=lamv[:, 0:2], func=AF.Exp), rd=["lamv"], wr=["lamv"])
            self.op("dve", lambda e: e.tensor_tensor(out=lamv[:, 4:5], in0=lamv[:, 3:4], in1=lamv[:, 2:3], op=ALU.subtract), rd=["lamv"], wr=["lamv"])
            self.op("dve", lambda e: e.tensor_scalar(out=lamv[:, 5:6], in0=lamv[:, 4:5], scalar1=-lam_init, scalar2=None, op0=ALU.add), rd=["lamv"], wr=["lamv"])
            self.op("dve", lambda e: e.tensor_scalar(out=gdf[:, :], in0=vcol("g_diff", 0, 4), scalar1=1.0 - lam_init, scalar2=None, op0=ALU.mult), rd=["vec"], wr=["gdf"])
            ps = PSb[7]
            wplan([[wpiece("w_ada", l, g * 4096, 4096)] for g in range(12)])
            for g in range(12):
                Wt, wk = wtake()
                for jj in range(4):
                    j = g * 4 + jj
                    for kc in range(KC):
                        self.op("pe", lambda e, Wt=Wt, jj=jj, kc=kc, j=j: e.matmul(ps[:, j:j + 1], lhsT=Wt[:, (jj * KC + kc) * 128:(jj * KC + kc + 1) * 128],
                                                                                   rhs=cact[:, kc:kc + 1], start=(kc == 0), stop=(kc == KC - 1)),
                                rd=[wk, "cact"], wr=[("ps", 7)])
            o, w = VEC["b_ada"]
            self.op("dve", lambda e: e.tensor_tensor(out=modT[:, :], in0=ps[:, 0:48], in1=vec[:, o:o + w], op=ALU.add), rd=[("ps", 7), "vec"], wr=["modT"])
            self.op("dve", lambda e: e.tensor_scalar(out=modT[:, 8:16], in0=modT[:, 8:16], scalar1=1.0, scalar2=None, op0=ALU.add), rd=["modT"], wr=["modT"])
            self.op("dve", lambda e: e.tensor_scalar(out=modT[:, 32:40], in0=modT[:, 32:40], scalar1=1.0, scalar2=None, op0=ALU.add), rd=["modT"], wr=["modT"])
            self.debug_dump("modT%d" % l, modT[:, :], [P, 48], F32, rd=["modT"])
            pg.barrier()

        def make_hT(t, which):
            sh0 = 0 if which == 0 else 24
            sc0 = 8 if which == 0 else 32
            for kc in range(KC):
                src = xsv[:, kc, t * TS:(t + 1) * TS]
                dst = hTv[:, kc, :]
                if kc % 2 == 0:
                    self.op("dve", lambda e, s=src, d=dst, kc=kc: e.tensor_scalar(out=d, in0=s, scalar1=modT[:, sc0 + kc:sc0 + kc + 1], scalar2=modT[:, sh0 + kc:sh0 + kc + 1],
                                                                                  op0=ALU.mult, op1=ALU.add), rd=[("xs", t * 8 + kc), "modT"], wr=[("hT", kc)])
                else:
                    self.op("act", lambda e, s=src, d=dst, kc=kc: e.activation(out=d, in_=s, func=AF.Identity, bias=modT[:, sh0 + kc:sh0 + kc + 1], scale=modT[:, sc0 + kc:sc0 + kc + 1]),
                            rd=[("xs", t * 8 + kc), "modT"], wr=[("hT", kc)])

        psrr = dict(i=0, lo=0, hi=4)

        def next_ps():
            i = psrr["lo"] + psrr["i"] % (psrr["hi"] - psrr["lo"])
            psrr["i"] += 1
            return PSb[i], ("ps", i)

        def proj_fm(Wt, wk, woff, bias_col, dst, dst_key, func=None, rhsv=None, rhs_key="hT", nk=KC, rhs_c0=0):
            if rhsv is None:
                rhsv = hTv
            ps, pk = next_ps()
            for kc in range(nk):
                self.op("pe", lambda e, kc=kc, ps=ps: e.matmul(ps[:, :], lhsT=Wt[:, woff + kc * 128:woff + (kc + 1) * 128], rhs=rhsv[:, rhs_c0 + kc, :],
                                                                start=(kc == 0), stop=(kc == nk - 1)), rd=[wk, rhs_key], wr=[pk])
            if dst is None:
                return ps, pk
            f = AF.Identity if func is None else func
            bcol = vcol("b_fm", bias_col)
            self.op("act", lambda e, ps=ps: e.activation(out=dst, in_=ps[:, :], func=f, bias=bcol, scale=1.0), rd=[pk, "vec"], wr=[dst_key])

        def proj_tm(Wt, wk, woff, ncols, tb, rowname, rowoff, ps, pk, pcol=0):
            ro = ROW_OFF[rowname][0] + rowoff
            for kc in range(KC):
                self.op("pe", lambda e, kc=kc: e.matmul(ps[:, pcol:pcol + ncols], lhsT=hTv[:, kc, tb * P:(tb + 1) * P], rhs=Wt[:, woff + kc * ncols:woff + (kc + 1) * ncols],
                                                         start=(kc == 0), stop=False), rd=[wk, "hT"], wr=[pk])
            self.op("pe", lambda e: e.matmul(ps[:, pcol:pcol + ncols], lhsT=ones_b[0:33, :], rhs=brow[0:33, ro:ro + ncols], start=False, stop=True),
                    rd=["brow", "cstb"], wr=[pk])

        def rotary(ps, pk, tbg, dst, dst_key, dec, rt1, rt2, krot):
            pv = ps[:, :].rearrange("p (h s d) -> p h s d", h=4, s=2)
            t1 = pv[:, :, 0, :]
            t2 = pv[:, :, 1, :]
            cb = cosv[:, tbg, :].unsqueeze(1).broadcast_to([P, 4, 64])
            sbv = sinv[:, tbg, :].unsqueeze(1).broadcast_to([P, 4, 64])
            a = rt1[:, :].rearrange("p (h d) -> p h d", h=4)
            b = rt2[:, :].rearrange("p (h d) -> p h d", h=4)
            kr = krot[:, :].rearrange("p (h s d) -> p h s d", h=4, s=2)
            self.op("dve", lambda e: e.tensor_tensor(out=a, in0=t1, in1=cb, op=ALU.mult), rd=[pk, "cst"], wr=["rt1"])
            self.op("dve", lambda e: e.tensor_tensor(out=b, in0=t2, in1=sbv, op=ALU.mult), rd=[pk, "cst"], wr=["rt2"])
            self.op("pool", lambda e: e.tensor_tensor(out=kr[:, :, 0, :], in0=a, in1=b, op=ALU.subtract), rd=["rt1", "rt2"], wr=[("krot", 0)])
            self.op("dve", lambda e: e.tensor_tensor(out=a, in0=t1, in1=sbv, op=ALU.mult), rd=[pk, "cst"], wr=["rt1"])
            self.op("dve", lambda e: e.tensor_tensor(out=b, in0=t2, in1=cb, op=ALU.mult), rd=[pk, "cst"], wr=["rt2"])
            self.op("pool", lambda e: e.tensor_tensor(out=kr[:, :, 1, :], in0=a, in1=b, op=ALU.add), rd=["rt1", "rt2"], wr=[("krot", 1)])
            kv3 = krot[:, :].rearrange("p (h d) -> p h d", h=4)
            d3 = dst.rearrange("p (h d) -> p h d", h=4)
            if dec is not None:
                db = dec.unsqueeze(2).broadcast_to([P, 4, 128])
                self.op("pool", lambda e: e.tensor_tensor(out=d3, in0=kv3, in1=db, op=ALU.mult), rd=["krot", "cst"], wr=[dst_key])
            else:
                self.op("pool", lambda e: e.tensor_copy(out=d3, in_=kv3), rd=["krot"], wr=[dst_key])

        def phaseA(l, t):
            m = t
            ar.reset()
            kTt = ar.get(4 * TS, BF16)
            kTv = kTt[:, :].rearrange("p (h t) -> p h t", h=4)
            kT2 = ar.get(4 * TS, BF16)
            kT2v = kT2[:, :].rearrange("p (h t) -> p h t", h=4)
            Vaug = [ar.get(2048, BF16), ar.get(2048, BF16)]
            Vaugv = [v[:, :].rearrange("p (h t e) -> p h t e", h=4, t=4) for v in Vaug]
            nl = ar.get(16, F32)
            ex = ar.get(16, F32)
            rt1 = ar.get(256, F32)
            rt2 = ar.get(256, F32)
            krot = ar.get(512, F32)
            kinv = ar.get(4 * 512, BF16)
            kinvv = kinv[:, :].rearrange("p (t c) -> p t c", t=4)
            vtm = ar.get(4 * 1024, BF16)
            vtmv = vtm[:, :].rearrange("p (t c) -> p t c", t=4)
            Qst = ar.get(1024, F32)
            Qstv = Qst[:, :].rearrange("p (h e) -> p h e", h=4)
            Lst = ar.get(1024, F32)
            Lstv = Lst[:, :].rearrange("p (h e) -> p h e", h=4)
            vkeys = ["VaugA", "VaugB"]
            psrr["lo"], psrr["hi"] = 0, 4
            make_hT(t, 0)
            if ("hT%d_%d" % (l, t)) in self.dbg:
                self.debug_dump("hT%d_%d" % (l, t), hT[:, :], [P, KC * TS], BF16, rd=["hT"])
            wplan([[win_piece(l, "ak", 0, 4)], [win_piece(l, "bk", 0, 4)], [win_piece(l, "av")], [win_piece(l, "bv")],
                   [win_piece(l, "ck")], [win_piece(l, "cv", 0, 1)], [win_piece(l, "cv", 1, 1)]])
            if t == 0:
                self.dma(Wbf[:, :], win_piece(l, "bf")[0], rd=[("wb_w_in", l)], wr=["Wbf"], key="Wbf")
            Wt, wk = wtake()
            for h in range(4):
                proj_fm(Wt, wk, h * KC * 128, 0 + h, kTv[:, h, :], ("kTt", h))
            self.dma(kK[m][:, 0:2048], kTt[:, :], rd=["kTt"], wr=[("kK", m)], key="kvst")
            Wt, wk = wtake()
            for h in range(4):
                proj_fm(Wt, wk, h * KC * 128, 4 + h, kT2v[:, h, :], ("kT2", h))
            self.dma(kK[m][:, 2048:4096], kT2[:, :], rd=["kT2"], wr=[("kK", m)], key="kvst")
            agather(kK[m], gK[m], ("kK", m), ("gK", m))
            for vi, nm in enumerate(("av", "bv")):
                Wt, wk = wtake()
                vkey = vkeys[vi]
                for tb in range(4):
                    ps, pk = next_ps()
                    proj_tm(Wt, wk, 0, 512, tb, nm, 0, ps, pk)
                    outv = Vaugv[vi][:, :, tb, 0:128]
                    inv = ps[:, :].rearrange("p (h e) -> p h e", h=4)
                    if tb % 2 == 0:
                        self.op("dve", lambda e, o=outv, i=inv: e.tensor_copy(out=o, in_=i), rd=[pk], wr=[(vkey, tb)])
                    else:
                        self.op("act", lambda e, o=outv, i=inv: e.copy(out=o, in_=i), rd=[pk], wr=[(vkey, tb)])
                self.dma(kV[m][:, vi * 2048:(vi + 1) * 2048], Vaug[vi][:, :], rd=[vkey], wr=[("kV", m)], key="kvst")
            agather(kV[m], gV[m], ("kV", m), ("gV", m))
            ps, pk = PSb[4], ("ps", 4)
            ro = ROW_OFF["bf"][0]
            for tb in range(4):
                for kc in range(KC):
                    self.op("pe", lambda e, kc=kc, tb=tb: e.matmul(ps[:, tb * 4:tb * 4 + 4], lhsT=hTv[:, kc, tb * P:(tb + 1) * P], rhs=Wbf[:, kc * 4:kc * 4 + 4],
                                                                    start=(kc == 0), stop=False), rd=["Wbf", "hT"], wr=[pk])
                self.op("pe", lambda e, tb=tb: e.matmul(ps[:, tb * 4:tb * 4 + 4], lhsT=ones_b[0:33, :], rhs=brow[0:33, ro:ro + 4], start=False, stop=True),
                        rd=["brow", "cstb"], wr=[pk])
            self.op("act", lambda e: e.activation(out=ex[:, :], in_=ps[:, 0:16], func=AF.Exp, scale=-1.0), rd=[pk], wr=["ex"])
            self.op("act", lambda e: e.activation(out=nl[:, :], in_=ex[:, :], func=AF.Ln, bias=1.0, scale=1.0), rd=["ex"], wr=["nl"])
            ps2, pk2 = PSb[5], ("ps", 5)
            for tb in range(4):
                for tb2 in range(tb + 1):
                    lh = tri if tb2 == tb else onesf
                    self.op("pe", lambda e, tb=tb, tb2=tb2, lh=lh: e.matmul(ps2[:, tb * 4:tb * 4 + 4], lhsT=lh, rhs=nl[:, tb2 * 4:tb2 * 4 + 4], start=(tb2 == 0), stop=(tb2 == tb)),
                            rd=["nl", "cst"], wr=[pk2])
            self.op("dve", lambda e: e.tensor_copy(out=Gloc[:, m * 16:(m + 1) * 16], in_=ps2[:, 0:16]), rd=[pk2], wr=[("Gloc", m)])
            Wck, wkck = wtake()
            for tb in range(4):
                psk, pkk = next_ps()
                proj_tm(Wck, wkck, 0, 512, tb, "ck", 0, psk, pkk)
                rotary(psk, pkk, 4 * m + tb, kinvv[:, tb, :], ("kinv", tb), kdec, rt1, rt2, krot)
            for half in range(2):
                Wcv, wkcv = wtake()
                for tb in range(4):
                    psv, pkv = next_ps()
                    proj_tm(Wcv, wkcv, 0, 512, tb, "cv", half * 512, psv, pkv)
                    outv = vtmv[:, tb, half * 512:(half + 1) * 512]
                    if tb % 2 == 0:
                        self.op("act", lambda e, o=outv, ps=psv: e.copy(out=o, in_=ps[:, :]), rd=[pkv], wr=[("vtm", tb)])
                    else:
                        self.op("dve", lambda e, o=outv, ps=psv: e.tensor_copy(out=o, in_=ps[:, :]), rd=[pkv], wr=[("vtm", tb)])
            for tb in range(4):
                for hp in range(2):
                    psu, pku = PSb[6 + hp], ("ps", 6 + hp)
                    for hh in range(2):
                        h = hp * 2 + hh
                        self.op("pe", lambda e, tb=tb, h=h, hh=hh, psu=psu: e.matmul(psu[:, hh * 256:(hh + 1) * 256], lhsT=kinvv[:, tb, h * 128:(h + 1) * 128],
                                                                                      rhs=vtmv[:, tb, h * 256:(h + 1) * 256], start=True, stop=True),
                                rd=[("kinv", tb), ("vtm", tb)], wr=[pku])
                        if tb == 0:
                            self.op("dve", lambda e, h=h, hh=hh, psu=psu: e.tensor_copy(out=Qstv[:, h, :], in_=psu[:, hh * 256:(hh + 1) * 256]), rd=[pku], wr=[("Qst", h)])
                        else:
                            self.op("dve", lambda e, h=h, hh=hh, psu=psu: e.scalar_tensor_tensor(out=Qstv[:, h, :], in0=Qstv[:, h, :], scalar=G128[h], in1=psu[:, hh * 256:(hh + 1) * 256],
                                                                                                  op0=ALU.mult, op1=ALU.add), rd=[pku, ("Qst", h)], wr=[("Qst", h)])
            for h in range(4):
                self.op("pool", lambda e, h=h: e.tensor_scalar(out=Lstv[:, h, :], in0=Qstv[:, h, :], scalar1=G128[h], scalar2=None, op0=ALU.mult), rd=[("Qst", h)], wr=["Lst"])
            self.dma(sL[m][:, :], Lst[:, :], rd=["Lst"], wr=[("sL", m)], key="smst")
            agather(sL[m], gL[m], ("sL", m), ("gL", m))
            pg.barrier()

        def exchange1():
            self.dma(sG[:, :], Gloc[:, :], rd=["Gloc"], wr=["sG"], key="smst")
            agather(sG, gG, "sG", "gG")

        def phaseB_prelude(l):
            ar.reset()
            Tb = ar.get(256, F32)
            Tbv = Tb[:, :].rearrange("p (r c) -> p r c", r=4)
            Pg = ar.get(1024, F32)
            Lg = [ar.get(1024, F32), ar.get(1024, F32)]
            Gkv = Gk[:, :].rearrange("p (r c) -> p r c", r=4)
            self.dma(Gkv, gG[:, :].rearrange("(r p) c -> p r c", p=P), rd=["gG"], wr=["Gk"], key="Gk")
            for r in range(4):
                self.dma(Tbv[:, r, :], gG[r * P + 127:r * P + 128, :].partition_broadcast(P)[:, 0, :], rd=["gG"], wr=["Tb"], key="Tb")
            offv = off_rm[:, :].rearrange("p (r m h) -> p r m h", r=4, m=4)
            self.op("dve", lambda e: e.memset(off_rm[:, :], 0.0), rd=[], wr=["off"])
            for g in range(15):
                r, m = g % 4, g // 4
                r2, m2 = (g + 1) % 4, (g + 1) // 4
                self.op("dve", lambda e, r=r, m=m, r2=r2, m2=m2: e.tensor_tensor(out=offv[:, r2, m2, :], in0=offv[:, r, m, :], in1=Tbv[:, r, (4 * m + 3) * 4:(4 * m + 3) * 4 + 4], op=ALU.add),
                        rd=["off", "Tb"], wr=["off"])
            Gk5 = Gk[:, :].rearrange("p (r m t h) -> p r m t h", r=4, m=4, t=4)
            for r in range(4):
                self.op("dve", lambda e, r=r: e.tensor_tensor(out=Gk5[:, r], in0=Gk5[:, r], in1=offv[:, r].unsqueeze(2).broadcast_to([P, 4, 4, 4]), op=ALU.add),
                        rd=["Gk", "off"], wr=["Gk"])
            offqv = offq[:, :].rearrange("p (m h) -> p m h", m=4)
            self.op("dve", lambda e: e.tensor_scalar(out=offqv, in0=offv[:, 0], scalar1=sel_c[:, 0:1], scalar2=None, op0=ALU.mult), rd=["off", "cst"], wr=["offq"])
            for r in range(1, 4):
                self.op("dve", lambda e, r=r: e.scalar_tensor_tensor(out=offqv, in0=offv[:, r], scalar=sel_c[:, r:r + 1], in1=offqv, op0=ALU.mult, op1=ALU.add),
                        rd=["off", "cst", "offq"], wr=["offq"])
            Rinv = Rin[:, :].rearrange("p (m c) -> p m c", m=4)
            Pgv = Pg[:, :].rearrange("p (h e) -> p h e", h=4)
            self.op("pool", lambda e: e.memset(Pg[:, :], 0.0), rd=[], wr=["Pg"])
            for g in range(16):
                r, m = g % 4, g // 4
                if r == 0:
                    self.op("dve", lambda e, m=m, r=r: e.tensor_scalar(out=Rinv[:, m, :], in0=Pg[:, :], scalar1=sel_c[:, r:r + 1], scalar2=None, op0=ALU.mult),
                            rd=["Pg", "cst"], wr=[("Rin", m)])
                else:
                    self.op("dve", lambda e, m=m, r=r: e.scalar_tensor_tensor(out=Rinv[:, m, :], in0=Pg[:, :], scalar=sel_c[:, r:r + 1], in1=Rinv[:, m, :], op0=ALU.mult, op1=ALU.add),
                            rd=["Pg", "cst", ("Rin", m)], wr=[("Rin", m)])
                if g < 15:
                    lg = Lg[g % 2]
                    lk = "Lg%d" % (g % 2)
                    self.dma(lg[:, :], gL[m][r * P:(r + 1) * P, :], rd=[("gL", m)], wr=[lk], key=lk)
                    lgv = lg[:, :].rearrange("p (h e) -> p h e", h=4)
                    for h in range(4):
                        self.op("dve", lambda e, h=h, lgv=lgv: e.scalar_tensor_tensor(out=Pgv[:, h, :], in0=Pgv[:, h, :], scalar=G512[h], in1=lgv[:, h, :], op0=ALU.mult, op1=ALU.add),
                                rd=["Pg", lk], wr=["Pg"])
            pg.barrier()

        def phaseB_tile(l, m):
            t = m
            ar.reset()
            yT = ar.get(16 * TS, BF16)
            yTv = yT[:, :].rearrange("p (c t) -> p c t", c=16)
            mkB = ar.mark()
            make_hT(t, 0)
            qT = ar.get(4 * TS, BF16)
            qTv = qT[:, :].rearrange("p (h t) -> p h t", h=4)
            kiT = ar.get(4 * TS, BF16)
            kiTv = kiT[:, :].rearrange("p (h t) -> p h t", h=4)
            kinv = ar.get(4 * 512, BF16)
            kinvv = kinv[:, :].rearrange("p (t c) -> p t c", t=4)
            vtm = ar.get(4 * 1024, BF16)
            vtmv = vtm[:, :].rearrange("p (t c) -> p t c", t=4)
            qrot = [ar.get(512, BF16), ar.get(512, BF16)]
            yctm = [ar.get(1024, BF16), ar.get(1024, BF16)]
            cgT = [ar.get(TS, BF16), ar.get(TS, BF16)]
            Qst = ar.get(1024, F32)
            Qstv = Qst[:, :].rearrange("p (h e) -> p h e", h=4)
            Rb = ar.get(1024, BF16)
            Rbv = Rb[:, :].rearrange("p (h e) -> p h e", h=4)
            ycr = ar.get(1024, F32)
            ycrv = ycr[:, :].rearrange("p (h e) -> p h e", h=4)
            sqb = ar.get(1024, F32)
            sqbv = sqb[:, :].rearrange("p (h e) -> p h e", h=4)
            Pm = ar.get(512, BF16)
            Pmv = Pm[:, :].rearrange("p (h i) -> p h i", h=4)
            st = ar.get(32, F32)
            rt1 = ar.get(256, F32)
            rt2 = ar.get(256, F32)
            krot = ar.get(512, F32)
            psrr["lo"], psrr["hi"] = 0, 2
            wplan([[win_piece(l, "cq")], [win_piece(l, "ck")], [win_piece(l, "cv", 0, 1)], [win_piece(l, "cv", 1, 1)]])
            psT = PSb[7][:, :].bitcast(BF16)
            pkT = ("ps", 7)
            Wt, wk = wtake()
            for tb in range(4):
                ps, pk = next_ps()
                proj_tm(Wt, wk, 0, 512, tb, "cq", 0, ps, pk)
                qr = qrot[tb % 2]
                qk = "qrot%d" % (tb % 2)
                rotary(ps, pk, 4 * m + tb, qr[:, :], qk, None, rt1, rt2, krot)
                for h in range(4):
                    self.op("pe", lambda e, h=h, qr=qr: e.transpose(out=psT[:, h * 128:(h + 1) * 128], in_=qr[:, h * 128:(h + 1) * 128], identity=ident_b),
                            rd=[qk, "cstb"], wr=[pkT])
                self.op("act", lambda e, tb=tb: e.copy(out=qTv[:, :, tb * P:(tb + 1) * P], in_=psT[:, 0:512].rearrange("p (h t) -> p h t", h=4)), rd=[pkT], wr=[("qT", tb)])
            Wt, wk = wtake()
            for tb in range(4):
                ps, pk = next_ps()
                proj_tm(Wt, wk, 0, 512, tb, "ck", 0, ps, pk)
                rotary(ps, pk, 4 * m + tb, kinvv[:, tb, :], ("kinv", tb), kdec, rt1, rt2, krot)
                for h in range(4):
                    self.op("pe", lambda e, h=h, tb=tb: e.transpose(out=psT[:, 512 + h * 128:512 + (h + 1) * 128], in_=kinvv[:, tb, h * 128:(h + 1) * 128], identity=ident_b),
                            rd=[("kinv", tb), "cstb"], wr=[pkT])
                self.op("act", lambda e, tb=tb: e.copy(out=kiTv[:, :, tb * P:(tb + 1) * P], in_=psT[:, 512:1024].rearrange("p (h t) -> p h t", h=4)), rd=[pkT], wr=[("kiT", tb)])
            for half in range(2):
                Wcv, wkcv = wtake()
                for tb in range(4):
                    psv, pkv = next_ps()
                    proj_tm(Wcv, wkcv, 0, 512, tb, "cv", half * 512, psv, pkv)
                    outv = vtmv[:, tb, half * 512:(half + 1) * 512]
                    if tb % 2 == 0:
                        self.op("act", lambda e, o=outv, ps=psv: e.copy(out=o, in_=ps[:, :]), rd=[pkv], wr=[("vtm", tb)])
                    else:
                        self.op("dve", lambda e, o=outv, ps=psv: e.tensor_copy(out=o, in_=ps[:, :]), rd=[pkv], wr=[("vtm", tb)])
            Rinv = Rin[:, :].rearrange("p (m c) -> p m c", m=4)
            self.op("pool", lambda e: e.tensor_copy(out=Rb[:, :], in_=Rinv[:, m, :]), rd=[("Rin", m)], wr=["Rb"])
            psS, pkS = PSb[4], ("ps", 4)
            psU = [PSb[2], PSb[3]]
            psO = [PSb[5], PSb[6]]
            for tb in range(4):
                tsl = slice(tb * P, (tb + 1) * P)
                for h in range(4):
                    self.op("pe", lambda e, h=h, tsl=tsl: e.matmul(psS[:, h * 128:(h + 1) * 128], lhsT=kiTv[:, h, tsl], rhs=qTv[:, h, tsl], start=True, stop=True),
                            rd=[("kiT", tb), ("qT", tb)], wr=[pkS])
                self.op("dve", lambda e: e.tensor_tensor(out=Pmv, in0=psS[:, :].rearrange("p (h i) -> p h i", h=4), in1=causal.unsqueeze(1).broadcast_to([P, 4, 128]), op=ALU.mult),
                        rd=[pkS, "cst"], wr=["Pm"])
                for h in range(4):
                    po = psO[h // 2]
                    pko = ("ps", 5 + h // 2)
                    osl = slice((h % 2) * 256, (h % 2 + 1) * 256)
                    self.op("pe", lambda e, h=h, po=po, osl=osl, tb=tb: e.matmul(po[:, osl], lhsT=Pmv[:, h, :], rhs=vtmv[:, tb, h * 256:(h + 1) * 256], start=True, stop=False),
                            rd=["Pm", ("vtm", tb)], wr=[pko])
                    self.op("pe", lambda e, h=h, po=po, osl=osl, tsl=tsl: e.matmul(po[:, osl], lhsT=qTv[:, h, tsl], rhs=Rbv[:, h, :], start=False, stop=True),
                            rd=[("qT", tb), "Rb"], wr=[pko])
                    self.op("act", lambda e, h=h, po=po, osl=osl: e.mul(out=ycrv[:, h, :], in_=po[:, osl], mul=gpow[:, h:h + 1]), rd=[pko, "cst"], wr=[("ycr", h)])
                if tb < 3:
                    for h in range(4):
                        pu = psU[h // 2]
                        pku = ("ps", 2 + h // 2)
                        usl = slice((h % 2) * 256, (h % 2 + 1) * 256)
                        self.op("pe", lambda e, h=h, pu=pu, usl=usl, tb=tb: e.matmul(pu[:, usl], lhsT=kinvv[:, tb, h * 128:(h + 1) * 128], rhs=vtmv[:, tb, h * 256:(h + 1) * 256], start=True, stop=True),
                                rd=[("kinv", tb), ("vtm", tb)], wr=[pku])
                        if tb == 0:
                            src = Rinv[:, m, h * 256:(h + 1) * 256]
                            self.op("dve", lambda e, h=h, pu=pu, usl=usl, src=src: e.tensor_tensor(out=Qstv[:, h, :], in0=pu[:, usl], in1=src, op=ALU.add), rd=[pku, ("Rin", m)], wr=[("Qst", h)])
                        else:
                            self.op("dve", lambda e, h=h, pu=pu, usl=usl: e.scalar_tensor_tensor(out=Qstv[:, h, :], in0=Qstv[:, h, :], scalar=G128[h], in1=pu[:, usl], op0=ALU.mult, op1=ALU.add),
                                    rd=[pku, ("Qst", h)], wr=[("Qst", h)])
                        self.op("pool", lambda e, h=h: e.tensor_scalar(out=Rbv[:, h, :], in0=Qstv[:, h, :], scalar1=G128[h], scalar2=None, op0=ALU.mult), rd=[("Qst", h)], wr=["Rb"])
                self.op("dve", lambda e: e.reduce_sum(out=st[:, 0:4], in_=ycrv, axis=AX.X), rd=["ycr"], wr=["st"])
                self.op("pool", lambda e: e.tensor_tensor(out=sqb[:, :], in0=ycr[:, :], in1=ycr[:, :], op=ALU.mult), rd=["ycr"], wr=["sqb"])
                self.op("dve", lambda e: e.reduce_sum(out=st[:, 4:8], in_=sqbv, axis=AX.X), rd=["sqb"], wr=["st"])
                self.op("dve", lambda e: e.tensor_scalar(out=st[:, 8:12], in0=st[:, 0:4], scalar1=1.0 / 256, scalar2=None, op0=ALU.mult), rd=["st"], wr=["st"])
                self.op("dve", lambda e: e.tensor_tensor(out=st[:, 12:16], in0=st[:, 8:12], in1=st[:, 8:12], op=ALU.mult), rd=["st"], wr=["st"])
                self.op("dve", lambda e: e.scalar_tensor_tensor(out=st[:, 16:20], in0=st[:, 4:8], scalar=1.0 / 256, in1=st[:, 12:16], op0=ALU.mult, op1=ALU.subtract), rd=["st"], wr=["st"])
                self.op("act", lambda e: e.activation(out=st[:, 20:24], in_=st[:, 16:20], func=AF.Sqrt, bias=epsc[:, 0:1], scale=1.0), rd=["st", "epsc"], wr=["st"])
                self.op("dve", lambda e: e.reciprocal(out=st[:, 24:28], in_=st[:, 20:24]), rd=["st"], wr=["st"])
                self.op("dve", lambda e: e.tensor_tensor(out=ycrv, in0=ycrv, in1=st[:, 8:12].unsqueeze(2).broadcast_to([P, 4, 256]), op=ALU.subtract), rd=["ycr", "st"], wr=["ycr"])
                yc = yctm[tb % 2]
                yk = "yctm%d" % (tb % 2)
                self.op("pool", lambda e, yc=yc: e.tensor_tensor(out=yc[:, :].rearrange("p (h e) -> p h e", h=4), in0=ycrv, in1=st[:, 24:28].unsqueeze(2).broadcast_to([P, 4, 256]), op=ALU.mult),
                        rd=["ycr", "st"], wr=[yk])
                for ec in range(8):
                    self.op("pe", lambda e, ec=ec, yc=yc: e.transpose(out=psT[:, ec * 128:(ec + 1) * 128], in_=yc[:, ec * 128:(ec + 1) * 128], identity=ident_b), rd=[yk, "cstb"], wr=[pkT])
                self.op("act", lambda e, tsl=tsl: e.copy(out=yTv[:, 8:16, tsl], in_=psT[:, :].rearrange("p (c t) -> p c t", c=8)), rd=[pkT], wr=[("yT", 8 + tb)])
            wplan([[win_piece(l, "cg", 0, 4)], [win_piece(l, "cg", 4, 4)]])
            for half in range(2):
                Wt, wk = wtake()
                for e4 in range(4):
                    ec = half * 4 + e4
                    cg_t = cgT[ec % 2]
                    ck_ = "cgT%d" % (ec % 2)
                    proj_fm(Wt, wk, e4 * KC * 128, 40 + ec, cg_t[:, :], ck_, func=AF.Silu)
                    self.op("dve", lambda e, ec=ec, cg_t=cg_t: e.scalar_tensor_tensor(out=yTv[:, 8 + ec, :], in0=yTv[:, 8 + ec, :], scalar=vcol("g_ret", ec), in1=cg_t[:, :], op0=ALU.mult, op1=ALU.mult),
                            rd=["yT", ck_, "vec"], wr=[("yT", 100 + ec)])
            if ("yc%d_%d" % (l, m)) in self.dbg:
                self.debug_dump("yc%d_%d" % (l, m), yT[:, 8 * TS:16 * TS], [P, 8 * TS], BF16, rd=["yT"])
            pg.barrier()
            if self.stage == "B1":
                raise StopBuild()

            ar.reset(mkB)
            QA = ar.get(4 * TS, BF16)
            QAv = QA[:, :].rearrange("p (h t) -> p h t", h=4)
            QB = ar.get(4 * TS, BF16)
            QBv = QB[:, :].rearrange("p (h t) -> p h t", h=4)
            Kj = [ar.get(512, BF16) for _ in range(3)]
            Vj = [ar.get(520, BF16) for _ in range(3)]
            Pt = [ar.get(512, BF16) for _ in range(3)]
            tS = [ar.get(512, F32), ar.get(512, F32)]
            tS2 = [ar.get(512, F32), ar.get(512, F32)]
            Mq = [ar.get(512, F32), ar.get(512, F32)]
            Grow = ar.get(512, F32)
            yat = ar.get(512, F32)
            yat3 = yat[:, :].rearrange("p (q e) -> p q e", q=4)
            yat2 = ar.get(512, F32)
            yat23 = yat2[:, :].rearrange("p (q e) -> p q e", q=4)
            yab = ar.get(4 * 512, BF16)
            yabv = yab[:, :].rearrange("p (q c) -> p q c", q=4)
            ybb = ar.get(4 * 512, BF16)
            ybbv = ybb[:, :].rearrange("p (q c) -> p q c", q=4)
            bK = [ar.get(64, F32), ar.get(64, F32)]
            bKb = [ar.get(16, F32), ar.get(16, F32)]
            rr = ar.get(16, F32)
            for ks in range(3):
                self.op("pool", lambda e, ks=ks: e.memset(Vj[ks][:, :], 1.0), rd=[], wr=["Vj%d" % ks])
            psrr["lo"], psrr["hi"] = 0, 3
            wplan([[win_piece(l, "aq", 0, 4)], [win_piece(l, "bq", 0, 4)]])
            Wt, wk = wtake()
            for h in range(4):
                proj_fm(Wt, wk, h * KC * 128, 8 + h, QAv[:, h, :], ("QA", h))
            Wt, wk = wtake()
            for h in range(4):
                proj_fm(Wt, wk, h * KC * 128, 12 + h, QBv[:, h, :], ("QB", h))
            psG, pkG = PSb[7], ("ps", 7)
            for qb in range(4):
                self.op("pe", lambda e, qb=qb: e.transpose(out=psG[0:4, qb * 128:(qb + 1) * 128], in_=Gloc[:, (4 * m + qb) * 4:(4 * m + qb) * 4 + 4], identity=ident_f),
                        rd=[("Gloc", m), "cst"], wr=[pkG])
            self.op("dve", lambda e: e.tensor_scalar(out=Grow[0:4, :], in0=psG[0:4, :], scalar1=-1.0 / SCALE_B, scalar2=None, op0=ALU.mult), rd=[pkG], wr=["Grow"])

            jobs = [("A", h, mp) for h in range(4) for mp in range(2)] + [("B", h, 0) for h in range(4)]
            segs = [(m2, r) for m2 in range(m + 1) for r in range(4)]
            nsteps = len(segs) * 4
            NKV = 3
            cnt = dict(p=0, t=0, kvdma=0)
            seglist = [(ji, m2, r) for ji in range(len(jobs)) for (m2, r) in segs]
            steps = [(ji, sidx, kb) for ji in range(len(jobs)) for sidx in range(len(segs)) for kb in range(4)]
            stinfo = {}

            def jobinfo(ji):
                kind, h, mp = jobs[ji]
                jset = ji % 2
                d = dict(kind=kind, h=h, mp=mp, X=PSb[3 + 2 * jset], Y=PSb[4 + 2 * jset], kX=("ps", 3 + 2 * jset), kY=("ps", 4 + 2 * jset))
                d["X3"] = d["X"][:, 0:390].rearrange("p (q e) -> p q e", q=3)
                if kind == "A":
                    d.update(prs=slice(64 * mp, 64 * mp + 64), Qv=QAv, qkey=("QA", h), scale=SCALE_A)
                else:
                    d.update(prs=slice(0, 128), Qv=QBv, qkey=("QB", h), scale=SCALE_B, mq=Mq[h % 2], mqk="Mq%d" % (h % 2),
                             bk_=bK[h % 2], bkk="bK%d" % (h % 2), bkb=bKb[h % 2], bkbk="bKb%d" % (h % 2))
                return d

            jinfo = [jobinfo(ji) for ji in range(len(jobs))]

            def kv_dma(k):
                ji, m2, r = seglist[k]
                J = jinfo[ji]
                h, mp = J["h"], J["mp"]
                ks = k % NKV
                kj, vj = Kj[ks], Vj[ks]
                kjk, vjk = "Kj%d" % ks, "Vj%d" % ks
                row0 = 128 * r
                vj3 = vj[:, :].rearrange("p (t e) -> p t e", t=4)[:, :, 0:128]
                if J["kind"] == "A":
                    self.dma(kj[J["prs"], :], gK[m2][row0 + 64 * mp:row0 + 64 * mp + 64, h * 512:(h + 1) * 512], rd=[("gK", m2)], wr=[kjk], key=kjk)
                    self.dma(vj3, gV[m2][row0:row0 + 128, h * 512:(h + 1) * 512].rearrange("p (t e) -> p t e", t=4), rd=[("gV", m2)], wr=[vjk], key=vjk)
                else:
                    self.dma(kj[:, :], gK[m2][row0:row0 + 128, 2048 + h * 512:2048 + (h + 1) * 512], rd=[("gK", m2)], wr=[kjk], key=kjk)
                    self.dma(vj3, gV[m2][row0:row0 + 128, 2048 + h * 512:2048 + (h + 1) * 512].rearrange("p (t e) -> p t e", t=4), rd=[("gV", m2)], wr=[vjk], key=vjk)

            def job_setup(ji):
                J = jinfo[ji]
                if J["kind"] != "B":
                    return
                h = J["h"]
                mq, mqk, bk_, bkk, bkb, bkbk = J["mq"], J["mqk"], J["bk_"], J["bkk"], J["bkb"], J["bkbk"]
                self.op("pe", lambda e, h=h: e.matmul(psG[:, :], lhsT=sel4[0:4, h * 128:(h + 1) * 128], rhs=Grow[0:4, :], start=True, stop=True), rd=["Grow", "cst"], wr=[pkG])
                self.op("dve", lambda e, mq=mq: e.tensor_copy(out=mq[:, :], in_=psG[:, :]), rd=[pkG], wr=[mqk])
                Gk4 = Gk[:, :].rearrange("p (c h) -> p c h", h=4)
                self.op("dve", lambda e, h=h, bk_=bk_: e.tensor_scalar(out=bk_[:, :], in0=Gk4[:, :, h], scalar1=offq[:, m * 4 + h:m * 4 + h + 1], scalar2=None, op0=ALU.subtract),
                        rd=["Gk", "offq"], wr=[bkk])
                bk3 = bk_[:, :].rearrange("p (r c) -> p r c", r=4)
                bkb3 = bkb[:, :].rearrange("p (r c) -> p r c", r=4)
                self.op("dve", lambda e, bk3=bk3, bkb3=bkb3: e.tensor_tensor(out=bkb3, in0=bk3[:, :, 4 * m:4 * m + 4], in1=beta_c.unsqueeze(2).broadcast_to([P, 4, 4]), op=ALU.add),
                        rd=[bkk, "cst"], wr=[bkbk])

            def front(i):
                ji, sidx, kb = steps[i]
                J = jinfo[ji]
                kind, h, scale, prs, Qv, qkey = J["kind"], J["h"], J["scale"], J["prs"], J["Qv"], J["qkey"]
                m2, r = segs[sidx]
                k = ji * len(segs) + sidx
                if sidx == 0 and kb == 0:
                    job_setup(ji)
                if kb == 0:
                    while cnt["kvdma"] <= min(k + 1, len(seglist) - 1):
                        kv_dma(cnt["kvdma"])
                        cnt["kvdma"] += 1
                ks = k % NKV
                kj, vj = Kj[ks], Vj[ks]
                kjk, vjk = "Kj%d" % ks, "Vj%d" % ks
                diag = (m2 == m)
                psS_, pkS_ = next_ps()
                self.op("pe", lambda e, psS_=psS_, kj=kj, kb=kb, Qv=Qv, h=h, prs=prs: e.matmul(psS_[:, :], lhsT=kj[prs, kb * 128:(kb + 1) * 128], rhs=Qv[prs, h, :], start=True, stop=True),
                        rd=[kjk, qkey], wr=[pkS_])
                pt = Pt[cnt["p"] % 3]
                ptk = "Pt%d" % (cnt["p"] % 3)
                cnt["p"] += 1
                stinfo[i] = (pt, ptk, vj, vjk)
                if kind == "A" and not diag:
                    self.op("act", lambda e, pt=pt, psS_=psS_, sc_=scale: e.activation(out=pt[:, :], in_=psS_[:, :], func=AF.Exp, scale=sc_), rd=[pkS_], wr=[ptk])
                    return
                ts_ = tS[cnt["t"] % 2]
                tsk = "tS%d" % (cnt["t"] % 2)
                ts2 = tS2[cnt["t"] % 2]
                ts2k = "tS2%d" % (cnt["t"] % 2)
                cnt["t"] += 1
                if kind == "A":
                    self.op("dve", lambda e, ts_=ts_, psS_=psS_, kb=kb, r=r: e.scalar_tensor_tensor(out=ts_[:, :], in0=diagA[:, kb, :], scalar=alpha_c[:, r:r + 1], in1=psS_[:, :], op0=ALU.mult, op1=ALU.add),
                            rd=[pkS_, "cstb", "cst"], wr=[tsk])
                    self.op("act", lambda e, pt=pt, ts_=ts_, r=r, sc_=scale: e.activation(out=pt[:, :], in_=ts_[:, :], func=AF.Exp, bias=beta_c[:, r:r + 1], scale=sc_), rd=[tsk, "cst"], wr=[ptk])
                elif not diag:
                    mq, mqk, bk_, bkk = J["mq"], J["mqk"], J["bk_"], J["bkk"]
                    self.op("dve", lambda e, ts_=ts_, psS_=psS_, mq=mq: e.tensor_tensor(out=ts_[:, :], in0=psS_[:, :], in1=mq[:, :], op=ALU.add), rd=[pkS_, mqk], wr=[tsk])
                    bcol = bk_[:, r * 16 + m2 * 4 + kb:r * 16 + m2 * 4 + kb + 1]
                    self.op("act", lambda e, pt=pt, ts_=ts_, bcol=bcol, sc_=scale: e.activation(out=pt[:, :], in_=ts_[:, :], func=AF.Exp, bias=bcol, scale=sc_), rd=[tsk, bkk], wr=[ptk])
                else:
                    mq, mqk, bkb, bkbk = J["mq"], J["mqk"], J["bkb"], J["bkbk"]
                    self.op("dve", lambda e, ts_=ts_, psS_=psS_, kb=kb, r=r: e.scalar_tensor_tensor(out=ts_[:, :], in0=diagB[:, kb, :], scalar=alpha_c[:, r:r + 1], in1=psS_[:, :], op0=ALU.mult, op1=ALU.add),
                            rd=[pkS_, "cstb", "cst"], wr=[tsk])
                    self.op("pool", lambda e, ts_=ts_, ts2=ts2, mq=mq: e.tensor_tensor(out=ts2[:, :], in0=ts_[:, :], in1=mq[:, :], op=ALU.add), rd=[tsk, mqk], wr=[ts2k])
                    bcol = bkb[:, r * 4 + kb:r * 4 + kb + 1]
                    self.op("act", lambda e, pt=pt, ts2=ts2, bcol=bcol, sc_=scale: e.activation(out=pt[:, :], in_=ts2[:, :], func=AF.Exp, bias=bcol, scale=sc_), rd=[ts2k, bkbk], wr=[ptk])

            def back(i):
                ji, sidx, kb = steps[i]
                J = jinfo[ji]
                X, Y, kX, kY = J["X"], J["Y"], J["kX"], J["kY"]
                pt, ptk, vj, vjk = stinfo.pop(i)
                si = sidx * 4 + kb
                first = (si == 0)
                last = (si == nsteps - 1)
                for qb in range(4):
                    if qb < 3:
                        oreg = X[:, qb * 130:qb * 130 + 129]
                        ok = kX
                        st_ = first and qb == 0
                    else:
                        oreg = Y[:, 0:129]
                        ok = kY
                        st_ = first
                    self.op("pe", lambda e, oreg=oreg, pt=pt, qb=qb, vj=vj, kb=kb, st_=st_, last=last: e.matmul(oreg, lhsT=pt[:, qb * 128:(qb + 1) * 128], rhs=vj[:, kb * 130:kb * 130 + 129],
                                                                                                              start=st_, stop=last, skip_group_check=True),
                            rd=[ptk, vjk], wr=[ok])
                if last:
                    evac(ji)

            def evac(ji):
                J = jinfo[ji]
                kind, h, mp, X, Y, kX, kY, X3 = J["kind"], J["h"], J["mp"], J["X"], J["Y"], J["kX"], J["kY"], J["X3"]
                self.op("dve", lambda e, X3=X3: e.reciprocal(out=rr[:, 0:3].unsqueeze(2), in_=X3[:, :, 128:129]), rd=[kX], wr=["rr"])
                self.op("dve", lambda e, Y=Y: e.reciprocal(out=rr[:, 3:4], in_=Y[:, 128:129]), rd=[kY], wr=["rr"])
                if kind == "A" and mp == 0:
                    self.op("dve", lambda e, X3=X3: e.tensor_tensor(out=yat3[:, 0:3, :], in0=X3[:, :, 0:128], in1=rr[:, 0:3].unsqueeze(2).broadcast_to([P, 3, 128]), op=ALU.mult), rd=[kX, "rr"], wr=["yat"])
                    self.op("dve", lambda e, Y=Y: e.tensor_scalar(out=yat3[:, 3, :], in0=Y[:, 0:128], scalar1=rr[:, 3:4], scalar2=None, op0=ALU.mult), rd=[kY, "rr"], wr=["yat"])
                elif kind == "A":
                    self.op("dve", lambda e: e.tensor_scalar(out=rr[:, 4:8], in0=rr[:, 0:4], scalar1=lamv[:, 5:6], scalar2=None, op0=ALU.mult), rd=["rr", "lamv"], wr=["rr"])
                    self.op("dve", lambda e, X3=X3: e.tensor_tensor(out=yat23[:, 0:3, :], in0=X3[:, :, 0:128], in1=rr[:, 4:7].unsqueeze(2).broadcast_to([P, 3, 128]), op=ALU.mult), rd=[kX, "rr"], wr=["yat2"])
                    self.op("dve", lambda e, Y=Y: e.tensor_scalar(out=yat23[:, 3, :], in0=Y[:, 0:128], scalar1=rr[:, 7:8], scalar2=None, op0=ALU.mult), rd=[kY, "rr"], wr=["yat2"])
                    self.op("pool", lambda e: e.tensor_tensor(out=yat[:, :], in0=yat[:, :], in1=yat2[:, :], op=ALU.add), rd=["yat", "yat2"], wr=["yat"])
                    self.op("pool", lambda e: e.tensor_tensor(out=yat2[:, :], in0=yat[:, :], in1=yat[:, :], op=ALU.mult), rd=["yat"], wr=["yat2"])
                    self.op("dve", lambda e: e.reduce_sum(out=rr[:, 8:12], in_=yat23, axis=AX.X), rd=["yat2"], wr=["rr"])
                    self.op("act", lambda e: e.activation(out=rr[:, 8:12], in_=rr[:, 8:12], func=AF.Sqrt, bias=epsc[:, 0:1], scale=1.0 / 128), rd=["rr", "epsc"], wr=["rr"])
                    self.op("dve", lambda e: e.reciprocal(out=rr[:, 12:16], in_=rr[:, 8:12]), rd=["rr"], wr=["rr"])
                    self.op("dve", lambda e, h=h: e.tensor_tensor(out=yabv[:, :, h * 128:(h + 1) * 128], in0=yat3, in1=rr[:, 12:16].unsqueeze(2).broadcast_to([P, 4, 128]), op=ALU.mult), rd=["yat", "rr"], wr=[("yab", h)])
                else:
                    self.op("dve", lambda e, X3=X3, h=h: e.tensor_tensor(out=ybbv[:, 0:3, h * 128:(h + 1) * 128], in0=X3[:, :, 0:128], in1=rr[:, 0:3].unsqueeze(2).broadcast_to([P, 3, 128]), op=ALU.mult),
                            rd=[kX, "rr"], wr=[("ybb", h)])
                    self.op("dve", lambda e, Y=Y, h=h: e.tensor_scalar(out=ybbv[:, 3, h * 128:(h + 1) * 128], in0=Y[:, 0:128], scalar1=rr[:, 3:4], scalar2=None, op0=ALU.mult), rd=[kY, "rr"], wr=[("ybb", h)])

            LA = 2
            for i in range(len(steps) + LA):
                if i < len(steps):
                    front(i)
                if i - LA >= 0:
                    back(i - LA)
            psTa = [PSb[0][:, :].bitcast(BF16), PSb[1][:, :].bitcast(BF16)]
            for c in range(8):
                src = yabv if c < 4 else ybbv
                skey = "yab" if c < 4 else "ybb"
                h = c % 4
                pT = psTa[c % 2]
                pkT2 = ("ps", c % 2)
                for qb in range(4):
                    self.op("pe", lambda e, src=src, h=h, qb=qb, pT=pT: e.transpose(out=pT[:, qb * 128:(qb + 1) * 128], in_=src[:, qb, h * 128:(h + 1) * 128], identity=ident_b), rd=[skey, "cstb"], wr=[pkT2])
                if c < 4:
                    self.op("dve", lambda e, c=c, pT=pT, h=h: e.tensor_scalar(out=yTv[:, c, :], in0=pT[:, 0:512], scalar1=gdf[:, h:h + 1], scalar2=None, op0=ALU.mult), rd=[pkT2, "gdf"], wr=[("yT", c)])
                else:
                    self.op("act", lambda e, c=c, pT=pT: e.copy(out=yTv[:, c, :], in_=pT[:, 0:512]), rd=[pkT2], wr=[("yT", c)])
            if ("yab%d_%d" % (l, m)) in self.dbg:
                self.debug_dump("yab%d_%d" % (l, m), yT[:, 0:8 * TS], [P, 8 * TS], BF16, rd=["yT"])
            pg.barrier()
            if self.stage == "B2":
                raise StopBuild()

            ar.reset(mkB)
            mT = ar.get(8 * TS, BF16)
            mTv = mT[:, :].rearrange("p (c t) -> p c t", c=8)
            g3 = [ar.get(TS, BF16) for _ in range(3)]
            mt1 = ar.get(TS, F32)
            mt2 = ar.get(TS, F32)
            psrr["lo"], psrr["hi"] = 0, 6
            pieces = []
            o_gt = WIN_IDX["gt"][0]
            for dc in range(8):
                pieces.append([(wb["w_in"][l, :, o_gt + dc * 3072:o_gt + (dc + 1) * 3072], 3072, ("wb_w_in", l)),
                               wpiece("w_pa", l, dc * 512, 512), wpiece("w_pb", l, dc * 512, 512), wpiece("w_pc", l, dc * 1024, 1024)])
            wplan(pieces)
            wplan([[wpiece("w_out", l, 0, 4096)], [wpiece("w_out", l, 4096, 4096)]])
            for dc in range(8):
                Wt, wk = wtake()
                for wh in range(3):
                    proj_fm(Wt, wk, wh * 1024, 16 + dc * 3 + wh, g3[wh][:, :], "g3_%d" % wh, func=AF.Sigmoid)
                psa, pka = proj_fm(Wt, wk, 3072, 0, None, None, rhsv=yTv, rhs_key="yT", nk=4, rhs_c0=0)
                self.op("dve", lambda e, psa=psa: e.tensor_tensor(out=mt1[:, :], in0=g3[0][:, :], in1=psa[:, :], op=ALU.mult), rd=[pka, "g3_0"], wr=["mt1"])
                psb_, pkb = proj_fm(Wt, wk, 3584, 0, None, None, rhsv=yTv, rhs_key="yT", nk=4, rhs_c0=4)
                self.op("dve", lambda e, psb_=psb_: e.tensor_tensor(out=mt2[:, :], in0=g3[1][:, :], in1=psb_[:, :], op=ALU.mult), rd=[pkb, "g3_1"], wr=["mt2"])
                self.op("pool", lambda e: e.tensor_tensor(out=mt1[:, :], in0=mt1[:, :], in1=mt2[:, :], op=ALU.add), rd=["mt1", "mt2"], wr=["mt1"])
                psc, pkc = proj_fm(Wt, wk, 4096, 0, None, None, rhsv=yTv, rhs_key="yT", nk=8, rhs_c0=8)
                self.op("dve", lambda e, psc=psc: e.tensor_tensor(out=mt2[:, :], in0=g3[2][:, :], in1=psc[:, :], op=ALU.mult), rd=[pkc, "g3_2"], wr=["mt2"])
                self.op("pool", lambda e, dc=dc: e.tensor_tensor(out=mTv[:, dc, :], in0=mt1[:, :], in1=mt2[:, :], op=ALU.add), rd=["mt1", "mt2"], wr=[("mT", dc)])
            residual_ln(l, t, mTv, "mT", 8, "w_out_planned", 16, 0)
            pg.barrier()

        def residual_ln(l, t, rhsv, rhs_key, nk, wname, gt0, lnidx):
            tsl = slice(t * TS, (t + 1) * TS)
            xt = xsv[:, :, tsl]
            xall = [("xs", t * 8 + c) for c in range(8)]
            self.op("pool", lambda e: e.tensor_scalar(out=xt, in0=xt, scalar1=ALPHA, scalar2=None, op0=ALU.mult), rd=xall, wr=xall)
            pbanks = [6, 7]
            Wt = wk = None
            for dc in range(8):
                if nk == 8:
                    if dc % 4 == 0:
                        Wt, wk = wtake()
                    woff = (dc % 4) * nk * 128
                else:
                    Wt, wk = wtake()
                    woff = 0
                bi = pbanks[dc % 2]
                ps, pk = PSb[bi], ("ps", bi)
                for kc in range(nk):
                    self.op("pe", lambda e, kc=kc, ps=ps, Wt=Wt, woff=woff: e.matmul(ps[:, :], lhsT=Wt[:, woff + kc * 128:woff + (kc + 1) * 128], rhs=rhsv[:, kc, :], start=(kc == 0), stop=(kc == nk - 1)),
                            rd=[wk, rhs_key], wr=[pk])
                self.op("dve", lambda e, dc=dc, ps=ps: e.scalar_tensor_tensor(out=xsv[:, dc, tsl], in0=ps[:, :], scalar=modT[:, gt0 + dc:gt0 + dc + 1], in1=xsv[:, dc, tsl], op0=ALU.mult, op1=ALU.add),
                        rd=[pk, "modT", ("xs", t * 8 + dc)], wr=[("xs", t * 8 + dc)])
            mk = ar.mark()
            xb = ar.get(8 * TS, BF16)
            xbv = xb[:, :].rearrange("p (c t) -> p c t", c=8)
            sq = ar.get(8 * TS, BF16)
            sqv = sq[:, :].rearrange("p (c t) -> p c t", c=8)
            mean = ar.get(TS, F32)
            msq = ar.get(TS, F32)
            rstd = ar.get(TS, F32)
            tl = [ar.get(TS, F32), ar.get(TS, F32)]
            for kc in range(KC):
                self.op("act", lambda e, kc=kc: e.copy(out=xbv[:, kc, :], in_=xsv[:, kc, tsl]), rd=[("xs", t * 8 + kc)], wr=[("xb", kc)])
                self.op("act", lambda e, kc=kc: e.activation(out=sqv[:, kc, :], in_=xsv[:, kc, tsl], func=AF.Square), rd=[("xs", t * 8 + kc)], wr=[("sq", kc)])
            ps1, pk1 = PSb[4], ("ps", 4)
            ps2, pk2 = PSb[5], ("ps", 5)
            for kc in range(KC):
                self.op("pe", lambda e, kc=kc: e.matmul(ps1[:, :], lhsT=ones_b, rhs=xbv[:, kc, :], start=(kc == 0), stop=(kc == KC - 1)), rd=[("xb", kc), "cstb"], wr=[pk1])
            for kc in range(KC):
                self.op("pe", lambda e, kc=kc: e.matmul(ps2[:, :], lhsT=ones_b, rhs=sqv[:, kc, :], start=(kc == 0), stop=(kc == KC - 1)), rd=[("sq", kc), "cstb"], wr=[pk2])
            self.op("act", lambda e: e.mul(out=mean[:, :], in_=ps1[:, :], mul=1.0 / D), rd=[pk1], wr=["mean"])
            self.op("pool", lambda e: e.tensor_tensor(out=msq[:, :], in0=mean[:, :], in1=mean[:, :], op=ALU.mult), rd=["mean"], wr=["msq"])
            self.op("dve", lambda e: e.scalar_tensor_tensor(out=msq[:, :], in0=ps2[:, :], scalar=1.0 / D, in1=msq[:, :], op0=ALU.mult, op1=ALU.subtract), rd=[pk2, "msq"], wr=["msq"])
            self.op("act", lambda e: e.activation(out=rstd[:, :], in_=msq[:, :], func=AF.Sqrt, bias=epsc[:, 0:1], scale=1.0), rd=["msq", "epsc"], wr=["rstd"])
            self.op("dve", lambda e: e.reciprocal(out=rstd[:, :], in_=rstd[:, :]), rd=["rstd"], wr=["rstd"])
            og, _ = VEC["ln_g"]
            ob, _ = VEC["ln_b"]
            for kc in range(KC):
                tt = tl[kc % 2]
                tk = "tl%d" % (kc % 2)
                self.op("dve", lambda e, kc=kc, tt=tt: e.tensor_tensor(out=tt[:, :], in0=xsv[:, kc, tsl], in1=mean[:, :], op=ALU.subtract), rd=[("xs", t * 8 + kc), "mean"], wr=[tk])
                self.op("pool", lambda e, tt=tt: e.tensor_tensor(out=tt[:, :], in0=tt[:, :], in1=rstd[:, :], op=ALU.mult), rd=[tk, "rstd"], wr=[tk])
                gcol = vec[:, og + lnidx * 8 + kc:og + lnidx * 8 + kc + 1]
                bcol = vec[:, ob + lnidx * 8 + kc:ob + lnidx * 8 + kc + 1]
                self.op("dve", lambda e, kc=kc, tt=tt, gcol=gcol, bcol=bcol: e.tensor_scalar(out=xsv[:, kc, tsl], in0=tt[:, :], scalar1=gcol, scalar2=bcol, op0=ALU.mult, op1=ALU.add),
                        rd=[tk, "vec"], wr=[("xs", t * 8 + kc)])
            ar.reset(mk)

        def exchange2(l):
            ar.reset()
            hh = ar.get(64, BF16)
            hhv = hh[:, :].rearrange("p (c m t) -> p c m t", c=8, m=4)
            for m in range(4):
                for kc in range(KC):
                    src = xsv[:, kc, m * TS + TS - 2:m * TS + TS]
                    self.op("dve", lambda e, src=src, kc=kc, m=m: e.tensor_scalar(out=hhv[:, kc, m, :], in0=src, scalar1=modT[:, 32 + kc:33 + kc], scalar2=modT[:, 24 + kc:25 + kc], op0=ALU.mult, op1=ALU.add),
                            rd=[("xs", m * 8 + kc), "modT"], wr=["hh"])
            self.dma(hls[:, :], hh[:, :], rd=["hh"], wr=["hls"], key="hls")
            agather(hls, hlg, "hls", "hlg")
            hall = ar.get(256, BF16)
            hallv = hall[:, :].rearrange("p (r c m t) -> p r c m t", r=4, c=8, m=4)
            self.dma(hall[:, :].rearrange("p (r x) -> p r x", r=4), hlg[:, :].rearrange("(r p) x -> p r x", p=P), rd=["hlg"], wr=["hall"], key="hall")
            hsf = ar.get(64, F32)
            hsfv = hsf[:, :].rearrange("p (c m t) -> p c m t", c=8, m=4)
            self.op("dve", lambda e: e.tensor_scalar(out=hsfv, in0=hallv[:, 0], scalar1=selA_c[:, 0:1], scalar2=None, op0=ALU.mult), rd=["hall", "cst"], wr=["hsf"])
            for r in range(1, 3):
                self.op("dve", lambda e, r=r: e.scalar_tensor_tensor(out=hsfv, in0=hallv[:, r], scalar=selA_c[:, r:r + 1], in1=hsfv, op0=ALU.mult, op1=ALU.add), rd=["hall", "cst", "hsf"], wr=["hsf"])
            self.op("dve", lambda e: e.scalar_tensor_tensor(out=hsfv[:, :, 1:4, :], in0=hallv[:, 3, :, 0:3, :], scalar=selB_c[:, 3:4], in1=hsfv[:, :, 1:4, :], op0=ALU.mult, op1=ALU.add),
                    rd=["hall", "cst", "hsf"], wr=["hsf"])
            self.op("dve", lambda e: e.tensor_copy(out=halo[:, :], in_=hsf[:, :]), rd=["hsf"], wr=["halo"])
            pg.barrier()

        def phaseC_tile(l, t):
            m = t
            ar.reset()
            aT = ar.get(NFC * TS, BF16)
            aTv = aT[:, :].rearrange("p (c t) -> p c t", c=NFC)
            ub = [ar.get(TS + 2, F32), ar.get(TS + 2, F32)]
            cv_ = [ar.get(TS, F32), ar.get(TS, F32)]
            ge = [ar.get(TS, F32), ar.get(TS, F32)]
            halov = halo[:, :].rearrange("p (c m t) -> p c m t", c=8, m=4)
            make_hT(t, 1)
            psrr["lo"], psrr["hi"] = 0, 4
            wplan([[wpiece("w_up", l, fc * 2048, 4096)] for fc in range(0, NFC, 2)])
            wplan([[wpiece("w_down", l, dc * NFC * 128, NFC * 128)] for dc in range(8)])
            ow, _ = VEC["w_conv"]
            obc, _ = VEC["b_conv"]
            psH, pkH = PSb[4], ("ps", 4)
            Wt = wk = None
            for fc in range(NFC):
                if fc % 2 == 0:
                    Wt, wk = wtake()
                wo = (fc % 2) * 2048
                psu, pku = next_ps()
                psg, pkg = next_ps()
                for kc in range(KC):
                    self.op("pe", lambda e, kc=kc, psu=psu, Wt=Wt, wo=wo: e.matmul(psu[:, :], lhsT=Wt[:, wo + kc * 128:wo + (kc + 1) * 128], rhs=hTv[:, kc, :], start=(kc == 0), stop=(kc == KC - 1)),
                            rd=[wk, "hT"], wr=[pku])
                for kc in range(KC):
                    self.op("pe", lambda e, kc=kc, Wt=Wt, wo=wo, fc=fc: e.matmul(psH[:, fc * 2:fc * 2 + 2], lhsT=Wt[:, wo + kc * 128:wo + (kc + 1) * 128], rhs=halov[:, kc, m, :], start=(kc == 0), stop=(kc == KC - 1)),
                            rd=[wk, "halo"], wr=[pkH])
                for kc in range(KC):
                    self.op("pe", lambda e, kc=kc, psg=psg, Wt=Wt, wo=wo: e.matmul(psg[:, :], lhsT=Wt[:, wo + 1024 + kc * 128:wo + 1024 + (kc + 1) * 128], rhs=hTv[:, kc, :], start=(kc == 0), stop=(kc == KC - 1)),
                            rd=[wk, "hT"], wr=[pkg])
                u = ub[fc % 2]
                uk = "ub%d" % (fc % 2)
                c_ = cv_[fc % 2]
                ck_ = "cv%d" % (fc % 2)
                g_ = ge[fc % 2]
                gk_ = "ge%d" % (fc % 2)
                self.op("act", lambda e, u=u, psu=psu: e.copy(out=u[:, 2:TS + 2], in_=psu[:, :]), rd=[pku], wr=[(uk, 1)])
                self.op("dve", lambda e, u=u, fc=fc: e.tensor_copy(out=u[:, 0:2], in_=psH[:, fc * 2:fc * 2 + 2]), rd=[pkH], wr=[(uk, 0)])
                w0 = vec[:, ow + 0 * NFC + fc:ow + 0 * NFC + fc + 1]
                w1 = vec[:, ow + 1 * NFC + fc:ow + 1 * NFC + fc + 1]
                w2 = vec[:, ow + 2 * NFC + fc:ow + 2 * NFC + fc + 1]
                bc = vec[:, obc + fc:obc + fc + 1]
                self.op("dve", lambda e, u=u, c_=c_, w0=w0, bc=bc: e.tensor_scalar(out=c_[:, :], in0=u[:, 0:TS], scalar1=w0, scalar2=bc, op0=ALU.mult, op1=ALU.add), rd=[uk, "vec"], wr=[ck_])
                self.op("dve", lambda e, u=u, c_=c_, w1=w1: e.scalar_tensor_tensor(out=c_[:, :], in0=u[:, 1:TS + 1], scalar=w1, in1=c_[:, :], op0=ALU.mult, op1=ALU.add), rd=[uk, "vec", ck_], wr=[ck_])
                self.op("dve", lambda e, u=u, c_=c_, w2=w2: e.scalar_tensor_tensor(out=c_[:, :], in0=u[:, 2:TS + 2], scalar=w2, in1=c_[:, :], op0=ALU.mult, op1=ALU.add), rd=[uk, "vec", ck_], wr=[ck_])
                self.op("act", lambda e, c_=c_, g_=g_: e.activation(out=g_[:, :], in_=c_[:, :], func=AF.Gelu), rd=[ck_], wr=[gk_])
                self.op("dve", lambda e, g_=g_, psg=psg, fc=fc: e.tensor_tensor(out=aTv[:, fc, :], in0=g_[:, :], in1=psg[:, :], op=ALU.mult), rd=[gk_, pkg], wr=[("aT", fc)])
            residual_ln(l, t, aTv, "aT", NFC, "w_down", 40, 1)
            pg.barrier()

        def stop(tag):
            if self.stage == tag:
                raise StopBuild()

        try:
            for l in range(L):
                layer_prelude(l)
                for t in range(NSEG):
                    phaseA(l, t)
                stop("A")
                exchange1()
                stop("X1")
                self.cast(l + 1, CG0, 0)
                phaseB_prelude(l)
                stop("P")
                for t in range(NSEG):
                    phaseB_tile(l, t)
                    if t == 1:
                        self.cast(l + 1, CG1, 1)
                self.cast(l + 1, CG2, 2)
                stop("B")
                exchange2(l)
                stop("X2")
                for t in range(NSEG):
                    phaseC_tile(l, t)
        except StopBuild:
            pg.barrier()

        ar.reset()
        yo = [ar.get(D, F32), ar.get(D, F32)]
        for tb in range(NT // P):
            yt = yo[tb % 2]
            yk = "yo%d" % (tb % 2)
            for k2 in range(2):
                bi = (tb * 2 + k2) % 4
                ps = PSb[bi]
                pkey = ("ps", bi)
                for c in range(4):
                    kc = k2 * 4 + c
                    self.op("pe", lambda e, ps=ps, c=c, kc=kc, tb=tb: e.transpose(out=ps[:, c * P:(c + 1) * P], in_=xsv[:, kc, tb * P:(tb + 1) * P], identity=ident_f),
                            rd=[("xs", (tb // 4) * 8 + kc), "cst"], wr=[pkey])
                if k2 == 0:
                    self.op("dve", lambda e, yt=yt, ps=ps: e.tensor_copy(out=yt[:, 0:512], in_=ps[:, :]), rd=[pkey], wr=[(yk, 0)])
                else:
                    self.op("act", lambda e, yt=yt, ps=ps: e.copy(out=yt[:, 512:1024], in_=ps[:, :]), rd=[pkey], wr=[(yk, 1)])
            self.dma(y_out[tb * P:(tb + 1) * P, :], yt[:, :], rd=[yk], wr=["y"], key="yout")

        pg.resolve()
        sems = {}
        for e in Prog.ENGS:
            sems[("eng", e)] = self.es.enter_context(nc.semaphore("s_" + e))
        for k in self.dma_keys:
            sems[("dma", k)] = self.es.enter_context(nc.semaphore("d_" + k))
        with nc.Block() as block:
            pg.emit(nc, block, sems)
        self.es.close()
        return nc


def make_in_maps(inputs, L=DEPTH):
    hl = host_layout(inputs, L)
    x = np.asarray(inputs["x"], np.float32)
    c = np.asarray(inputs["c"], np.float32)
    maps = []
    for r in range(8):
        b, j = r // 4, r % 4
        toks = np.concatenate([np.arange((j + 4 * m) * TS, (j + 4 * m + 1) * TS) for m in range(NSEG)])
        cst, cstb = host_constants(j)
        m = {"x": np.ascontiguousarray(x[b][toks]), "cT": np.ascontiguousarray(c[b].reshape(KC, P).T), "cst": cst, "cstb": cstb,
             "vec": hl["vec"], "rows": hl["rows"], "bro": hl["bro"]}
        for n in WSHAPES:
            m[n] = hl[n]
        maps.append(m)
    return maps


def kernel(**inputs):
    inputs = {k: np.asarray(v) for k, v in inputs.items()}
    b = Builder()
    nc = b.build()
    maps = make_in_maps(inputs)
    res = run_bass_kernel_spmd(nc, maps, core_ids=list(range(8)))
    out = np.zeros((2, 8192, D), np.float32)
    for r in range(8):
        bb, j = r // 4, r % 4
        y = res.results[r]["y"]
        for m in range(NSEG):
            g = j + 4 * m
            out[bb, g * TS:(g + 1) * TS] = y[m * TS:(m + 1) * TS]
    return out
```

```python
import math
from contextlib import ExitStack

import numpy as np
import ml_dtypes

import concourse.bass as bass
import concourse.mybir as mybir
from concourse.bass_utils import run_bass_kernel_spmd

F32 = mybir.dt.float32
BF16 = mybir.dt.bfloat16
AF = mybir.ActivationFunctionType
ALU = mybir.AluOpType
AX = mybir.AxisListType

P = 128
D = 1024
KC = 8
DEPTH = 4
NT = 2048
TS = 512
NSEG = 4
DFF = 2816
NFC = 22
LN_EPS = 1e-5
ALPHA = (2 * DEPTH) ** 0.25
AQ, AK, AV, BQ, BK, BV, BFc, CQ, CK, CV, CG, GT, DIN = 0, 512, 1024, 1536, 2048, 2560, 3072, 3076, 3588, 4100, 5124, 6148, 9220
SCALE_A = 64 ** -0.5
SCALE_B = 128 ** -0.5
NEG = -30000.0
GAM = [1.0 - 2.0 ** (-5.0 - h) for h in range(4)]
G128 = [g ** 128 for g in GAM]
G512 = [g ** 512 for g in GAM]
KVW = 2048 + 2048 + 2080 + 2080
SMW = 64 + 4096


class Prog:
    ENGS = ("pe", "act", "dve", "pool", "sp")

    def __init__(self):
        self.ops = []

    def add(self, eng, fn, rd=(), wr=(), dma=None, inc=16):
        self.ops.append(dict(eng=eng, fn=fn, rd=list(rd), wr=list(wr), dma=dma, inc=inc))

    def barrier(self):
        self.ops.append(dict(barrier=True))

    @staticmethod
    def _split(key):
        if isinstance(key, tuple):
            return key[0], key[1]
        return key, None

    def resolve(self):
        state = {}
        ops = self.ops
        last_on_eng = {}
        last_dma = {}
        pend = {e: set() for e in self.ENGS}

        def touch(key):
            name, idx = self._split(key)
            d = state.setdefault(name, {})
            if idx is None:
                d.setdefault(None, dict(w=None, r={}))
                return list(d.values())
            d.setdefault(idx, dict(w=None, r={}))
            res = [d[idx]]
            if None in d:
                res.append(d[None])
            return res

        for i, op in enumerate(ops):
            if op.get("barrier"):
                allp = set(last_on_eng.values()) | set(last_dma.values())
                for e in self.ENGS:
                    pend[e] |= allp
                continue
            deps = set()
            for k in op["rd"]:
                for st in touch(k):
                    if st["w"] is not None:
                        deps.add(st["w"])
            for k in op["wr"]:
                for st in touch(k):
                    if st["w"] is not None:
                        deps.add(st["w"])
                    deps.update(st["r"].values())
            for k in op["rd"]:
                for st in touch(k):
                    st["r"][op["eng"] if not op["dma"] else ("dma", op["dma"])] = i
            for k in op["wr"]:
                name, idx = self._split(k)
                for st in touch(k):
                    st["w"] = i
                    st["r"] = {}
            deps |= pend[op["eng"]]
            pend[op["eng"]] = set()
            deps.discard(i)
            op["deps"] = deps
            if op["dma"]:
                last_dma[op["dma"]] = i
            else:
                last_on_eng[op["eng"]] = i
        for op in ops:
            if op.get("barrier"):
                continue
            op["signal"] = bool(op["dma"])
        for op in ops:
            if op.get("barrier"):
                continue
            for p in op["deps"]:
                po = ops[p]
                if not po["dma"] and po["eng"] == "pe" and op["eng"] == "pe" and not op["dma"]:
                    continue
                po["signal"] = True
        cnt = {}
        dma_hist = {}
        for i, op in enumerate(ops):
            if op.get("barrier"):
                continue
            if op["dma"]:
                k = ("dma", op["dma"])
                cnt[k] = cnt.get(k, 0) + op["inc"]
                op["sigval"] = cnt[k]
                dma_hist.setdefault(op["dma"], []).append((i, cnt[k]))
            elif op["signal"]:
                k = ("eng", op["eng"])
                cnt[k] = cnt.get(k, 0) + 1
                op["sigval"] = cnt[k]
        self.dma_hist = dma_hist
        self.final = {k[1]: v for k, v in cnt.items() if k[0] == "dma"}

    def emit(self, nc, block, sems):
        ops = self.ops
        import bisect
        hist_idx = {k: [h[0] for h in v] for k, v in self.dma_hist.items()}

        def run(engname, e):
            waited = {}
            for i, op in enumerate(ops):
                if op.get("barrier") or op["eng"] != engname:
                    continue
                need = {}
                for p in op["deps"]:
                    po = ops[p]
                    if po["dma"]:
                        key = ("dma", po["dma"])
                        hi = hist_idx[po["dma"]]
                        pos = bisect.bisect_left(hi, i) - 1
                        val = self.dma_hist[po["dma"]][pos][1]
                    else:
                        if po["eng"] == "pe" and engname == "pe" and not op["dma"]:
                            continue
                        key = ("eng", po["eng"])
                        val = po["sigval"]
                    if val > need.get(key, 0):
                        need[key] = val
                for key, val in need.items():
                    if waited.get(key, 0) >= val:
                        continue
                    e.wait_ge(sems[key], val)
                    waited[key] = val
                ins = op["fn"](e)
                if op["dma"]:
                    ins.then_inc(sems[("dma", op["dma"])], op["inc"])
                elif op["signal"]:
                    ins.then_inc(sems[("eng", engname)], 1)
            if engname == "sp":
                for k, v in self.final.items():
                    if waited.get(("dma", k), 0) < v:
                        e.wait_ge(sems[("dma", k)], v)

        block.tensor(lambda e: run("pe", e))
        block.scalar(lambda e: run("act", e))
        block.vector(lambda e: run("dve", e))
        block.gpsimd(lambda e: run("pool", e))
        block.sync(lambda e: run("sp", e))


def pack_cols(w, blocks):
    K = w.shape[0]
    kc = K // P
    parts = []
    for c0, wd in blocks:
        blk = w[:, c0:c0 + wd].reshape(kc, P, wd).transpose(1, 0, 2).reshape(P, kc * wd)
        parts.append(blk)
    return np.ascontiguousarray(np.concatenate(parts, axis=1))


def win_blocks():
    blocks = []
    index = {}
    off = 0

    def addg(name, c0, width, n):
        nonlocal off
        index[name] = (off, width, n)
        for i in range(n):
            blocks.append((c0 + i * width, width))
            off += KC * width

    addg("ak", AK, 128, 4)
    addg("bk", BK, 128, 4)
    addg("av", AV, 512, 1)
    addg("bv", BV, 512, 1)
    addg("ck", CK, 512, 1)
    addg("cv", CV, 512, 2)
    addg("aq", AQ, 128, 4)
    addg("bq", BQ, 128, 4)
    addg("cq", CQ, 512, 1)
    addg("cg", CG, 128, 8)
    index["gt"] = (off, 128, 24)
    for dc in range(8):
        for wh in range(3):
            blocks.append((GT + wh * 1024 + dc * 128, 128))
            off += KC * 128
    index["bf"] = (off, 4, 1)
    blocks.append((BFc, 4))
    off += KC * 4
    return blocks, index, off


WIN_BLOCKS, WIN_IDX, WIN_TOT = win_blocks()
ROW_OFF = {}
_o = 0
for _n, _c0, _w in (("av", AV, 512), ("bv", BV, 512), ("bf", BFc, 4), ("ck", CK, 512), ("cv", CV, 1024), ("cq", CQ, 512)):
    ROW_OFF[_n] = (_o, _c0, _w)
    _o += _w
ROW_TOT = _o + (_o % 2)
VEC = {}
_o = 0
for _n, _w in (("b_ada", 48), ("b_fm", 48), ("ln_g", 16), ("ln_b", 16), ("w_conv", 66), ("b_conv", 22), ("g_diff", 4), ("g_ret", 8)):
    VEC[_n] = (_o, _w)
    _o += _w
VEC_TOT = _o
BRO_TOT = 256
CST = {}
_o = 0
for _n, _w in (("ident", 128), ("tri", 128), ("ones", 128), ("causal", 128), ("cos", 16 * 64), ("sin", 16 * 64), ("kdec", 4), ("gpow", 4),
               ("alpha", 4), ("beta", 4), ("sel", 4), ("selA", 4), ("selB", 4), ("sel4", 512)):
    CST[_n] = (_o, _w)
    _o += _w
CST_TOT = _o
CSTB = {"ident": (0, 128), "diagA": (128, 2048), "diagB": (2176, 2048), "ones": (4224, 128), "sel4": (4352, 512), "alphaI": (4864, 512)}
CSTB_TOT = 5376


def host_constants(j):
    cst = np.zeros((P, CST_TOT), np.float32)
    cstb = np.zeros((P, CSTB_TOT), np.float32)

    def put(name, arr):
        o, w = CST[name]
        cst[:, o:o + w] = arr.reshape(P, w)

    i = np.arange(P)
    put("ident", np.eye(P, dtype=np.float32))
    put("tri", (i[:, None] <= i[None, :]).astype(np.float32))
    put("ones", np.ones((P, P), np.float32))
    put("causal", (i[:, None] <= i[None, :]).astype(np.float32))
    half = 64
    inv = (1.0 / (np.float32(10000.0) ** np.linspace(0.0, 1.0, half, dtype=np.float32))).astype(np.float32)
    cos = np.zeros((P, 16, half), np.float32)
    sin = np.zeros((P, 16, half), np.float32)
    for m in range(NSEG):
        g = j + 4 * m
        for tb in range(4):
            pos = (g * TS + tb * P + i).astype(np.float32)
            ang = (pos[:, None] * inv[None, :]).astype(np.float32)
            cos[:, 4 * m + tb, :] = np.cos(ang)
            sin[:, 4 * m + tb, :] = np.sin(ang)
    put("cos", cos)
    put("sin", sin)
    kdec = np.stack([(128.0 ** -0.5) * (GAM[h] ** (-(i + 1.0))) for h in range(4)], 1)
    gpow = np.stack([GAM[h] ** (i + 1.0) for h in range(4)], 1)
    put("kdec", kdec.astype(np.float32))
    put("gpow", gpow.astype(np.float32))
    r = np.arange(4)
    put("alpha", np.broadcast_to((r >= j).astype(np.float32), (P, 4)))
    put("beta", np.broadcast_to(np.where(r > j, NEG, 0.0).astype(np.float32), (P, 4)))
    put("sel", np.broadcast_to((r == j).astype(np.float32), (P, 4)))
    put("selA", np.broadcast_to((r == j - 1).astype(np.float32), (P, 4)))
    put("selB", np.broadcast_to(((r == 3) & (j == 0)).astype(np.float32), (P, 4)))
    s4 = np.zeros((P, 4, P), np.float32)
    for h in range(4):
        s4[h, h, :] = 1.0
    put("sel4", s4)
    o, w = CSTB["ident"]
    cstb[:, o:o + w] = np.eye(P)
    kk = i[:, None]
    q = np.arange(TS)[None, :]
    dA = np.zeros((P, 4, TS), np.float32)
    dB = np.zeros((P, 4, TS), np.float32)
    for kb in range(4):
        k = kb * P + kk
        dA[:, kb, :] = np.where((k // 64) <= (q // 64), 0.0, NEG)
        dB[:, kb, :] = np.where(k <= q, 0.0, NEG)
    o, w = CSTB["diagA"]
    cstb[:, o:o + w] = dA.reshape(P, w)
    o, w = CSTB["diagB"]
    cstb[:, o:o + w] = dB.reshape(P, w)
    o, w = CSTB["ones"]
    cstb[:, o:o + w] = 1.0
    o, w = CSTB["sel4"]
    cstb[:, o:o + w] = s4.reshape(P, 512)
    aI = np.zeros((P, 4, P), np.float32)
    for r_ in range(4):
        if r_ >= j:
            aI[:, r_, :] = np.eye(P, dtype=np.float32)
    o, w = CSTB["alphaI"]
    cstb[:, o:o + w] = aI.reshape(P, 512)
    return cst, cstb.astype(ml_dtypes.bfloat16)


_HOSTW_CACHE = {}


def host_layout(inputs, L=DEPTH):
    out = {}
    w_ada = inputs["w_ada"]
    out["w_ada"] = np.stack([pack_cols(w_ada[l], [(c * 128, 128) for c in range(48)]) for l in range(L)])
    w_in = inputs["w_in"]
    out["w_in"] = np.stack([pack_cols(w_in[l], WIN_BLOCKS) for l in range(L)])
    out["w_pa"] = np.stack([pack_cols(inputs["w_pa"][l], [(c * 128, 128) for c in range(8)]) for l in range(L)])
    out["w_pb"] = np.stack([pack_cols(inputs["w_pb"][l], [(c * 128, 128) for c in range(8)]) for l in range(L)])
    out["w_pc"] = np.stack([pack_cols(inputs["w_pc"][l], [(c * 128, 128) for c in range(8)]) for l in range(L)])
    out["w_out"] = np.stack([pack_cols(inputs["w_out"][l], [(c * 128, 128) for c in range(8)]) for l in range(L)])
    ub = []
    for fc in range(NFC):
        ub.append((fc * 128, 128))
        ub.append((DFF + fc * 128, 128))
    out["w_up"] = np.stack([pack_cols(inputs["w_up"][l], ub) for l in range(L)])
    out["w_down"] = np.stack([pack_cols(inputs["w_down"][l], [(c * 128, 128) for c in range(8)]) for l in range(L)])
    vec = np.zeros((L, P, VEC_TOT), np.float32)
    for l in range(L):
        def putv(name, arr):
            o, w = VEC[name]
            vec[l, :, o:o + w] = arr
        putv("b_ada", inputs["b_ada"][l].reshape(48, P).T)
        b_in = inputs["b_in"][l]
        fm = []
        for c0 in (AK, BK, AQ, BQ):
            for h in range(4):
                fm.append(b_in[c0 + h * 128:c0 + (h + 1) * 128])
        for dc in range(8):
            for wh in range(3):
                c0 = GT + wh * 1024 + dc * 128
                fm.append(b_in[c0:c0 + 128])
        for ec in range(8):
            fm.append(b_in[CG + ec * 128:CG + (ec + 1) * 128])
        putv("b_fm", np.stack(fm, 1))
        putv("g_diff", inputs["g_diff"][l].reshape(4, P).T)
        putv("g_ret", inputs["g_ret"][l].reshape(8, P).T)
        putv("ln_g", inputs["ln_g"][l].reshape(2, 8, P).transpose(2, 0, 1).reshape(P, 16))
        putv("ln_b", inputs["ln_b"][l].reshape(2, 8, P).transpose(2, 0, 1).reshape(P, 16))
        putv("w_conv", inputs["w_conv"][l].reshape(3, NFC, P).transpose(2, 0, 1).reshape(P, 66))
        putv("b_conv", inputs["b_conv"][l].reshape(NFC, P).T)
    out["vec"] = vec
    rows = np.zeros((L, 1, ROW_TOT), np.float32)
    for l in range(L):
        for n, (o, c0, w) in ROW_OFF.items():
            rows[l, 0, o:o + w] = inputs["b_in"][l][c0:c0 + w]
    out["rows"] = rows
    bro = np.zeros((L, 1, BRO_TOT), np.float32)
    for l in range(L):
        bro[l, 0, 0:64] = inputs["lam_q1"][l]
        bro[l, 0, 64:128] = inputs["lam_k1"][l]
        bro[l, 0, 128:192] = inputs["lam_q2"][l]
        bro[l, 0, 192:256] = inputs["lam_k2"][l]
    out["bro"] = bro
    return out


WSHAPES = {
    "w_ada": 48 * KC * 128, "w_in": WIN_TOT, "w_pa": 8 * 4 * 128, "w_pb": 8 * 4 * 128, "w_pc": 8 * 8 * 128,
    "w_out": 8 * 8 * 128, "w_up": 44 * 8 * 128, "w_down": 8 * NFC * 128,
}


class StopBuild(Exception):
    pass


class Builder:
    def __init__(self, nlayers=DEPTH, stage="full", dbg=()):
        self.nlayers = nlayers
        self.stage = stage
        self.dbg = set(dbg)
        self.nc = bass.Bass("TRN2", target_bir_lowering=False)
        self.pg = Prog()
        self.es = ExitStack()
        self.dbg_out = {}
        self.dma_keys = []

    def sb(self, name, nfree, dt):
        t = self.es.enter_context(self.nc.sbuf_tensor("sb_" + name, [P, nfree], dt))
        return t

    def dma(self, out, in_, rd, wr, key, q="sp"):
        if key not in self.dma_keys:
            self.dma_keys.append(key)
        self.pg.add(q, lambda e, o=out, i=in_: e.dma_start(out=o, in_=i), rd=rd, wr=wr, dma=key)

    def op(self, eng, fn, rd, wr):
        self.pg.add(eng, fn, rd=rd, wr=wr)

    def debug_dump(self, name, ap_sb, shape, dt, rd):
        if name not in self.dbg:
            return
        t = self.nc.dram_tensor("dbg_" + name, list(shape), dt, kind="ExternalOutput").ap()
        self.dbg_out[name] = t
        self.dma(t, ap_sb, rd=rd, wr=["dbg_" + name], key="dbg_" + name)

    def build(self):
        nc = self.nc
        pg = self.pg
        L = self.nlayers
        dram = nc.dram_tensor
        x_in = dram("x", [NT, D], F32, kind="ExternalInput").ap()
        cT_in = dram("cT", [P, KC], F32, kind="ExternalInput").ap()
        wsrc = {n: dram(n, [L, P, WSHAPES[n]], F32, kind="ExternalInput").ap() for n in WSHAPES}
        vec_in = dram("vec", [L, P, VEC_TOT], F32, kind="ExternalInput").ap()
        rows_in = dram("rows", [L, 1, ROW_TOT], F32, kind="ExternalInput").ap()
        bro_in = dram("bro", [L, 1, BRO_TOT], F32, kind="ExternalInput").ap()
        cst_in = dram("cst", [P, CST_TOT], F32, kind="ExternalInput").ap()
        cstb_in = dram("cstb", [P, CSTB_TOT], BF16, kind="ExternalInput").ap()
        y_out = dram("y", [NT, D], F32, kind="ExternalOutput").ap()
        wb = {n: dram("wb_" + n, [L, P, WSHAPES[n]], BF16).ap() for n in WSHAPES}
        kK = [dram("kK%d" % m, [P, 4096], BF16) for m in range(NSEG)]
        kV = [dram("kV%d" % m, [P, 4096], BF16) for m in range(NSEG)]
        sL = [dram("sL%d" % m, [P, 1024], F32) for m in range(NSEG)]
        sG = dram("sG", [P, 64], F32)
        gK = [dram("gK%d" % m, [4 * P, 4096], BF16) for m in range(NSEG)]
        gV = [dram("gV%d" % m, [4 * P, 4096], BF16) for m in range(NSEG)]
        gL = [dram("gL%d" % m, [4 * P, 1024], F32) for m in range(NSEG)]
        gG = dram("gG", [4 * P, 64], F32)
        hls = dram("hls", [P, 64], BF16)
        hlg = dram("hlg", [4 * P, 64], BF16)
        self.wb = wb

        sb = self.sb
        xs = sb("xs", KC * NT, F32)
        xsv = xs[:, :].rearrange("p (k t) -> p k t", k=KC)
        cst = sb("cst", CST_TOT, F32)
        cstb = sb("cstb", CSTB_TOT, BF16)
        vec = sb("vec", VEC_TOT, F32)
        modT = sb("modT", 48, F32)
        hT = sb("hT", KC * TS, BF16)
        hTv = hT[:, :].rearrange("p (k t) -> p k t", k=KC)
        W = [sb("W0", 5120, BF16), sb("W1", 5120, BF16)]
        Wbf = sb("Wbf", KC * 4, BF16)
        brow = sb("brow", ROW_TOT, BF16)
        Gloc = sb("Gloc", 64, F32)
        cT = sb("cTs", KC, F32)
        cact = sb("cact", KC, BF16)
        lamv = sb("lamv", 8, F32)
        gdf = sb("gdf", 4, F32)
        epsc = sb("epsc", 2, F32)
        Gk = sb("Gk", 256, F32)
        off_rm = sb("off_rm", 64, F32)
        offq = sb("offq", 16, F32)
        Rin = sb("Rin", 4096, F32)
        halo = sb("halo", 64, BF16)
        PT2 = [self.es.enter_context(nc.psum_tensor(f"pt{i}", [P, 1024], F32)) for i in range(4)]
        PSb = []
        for i in range(4):
            PSb.append(PT2[i][:, 0:512])
            PSb.append(PT2[i][:, 512:1024])

        def cs(name):
            o, w = CST[name]
            return cst[:, o:o + w]

        def csb(name):
            o, w = CSTB[name]
            return cstb[:, o:o + w]

        ident_f = cs("ident")
        ident_b = csb("ident")
        ones_b = csb("ones")

        CH = 8192

        def cast(l, names, gi):
            if l >= L:
                return
            for n in names:
                tot = WSHAPES[n]
                for c0 in range(0, tot, CH):
                    c1 = min(tot, c0 + CH)
                    self.dma(wb[n][l, :, c0:c1], wsrc[n][l, :, c0:c1], rd=[], wr=[("wb_" + n, l)], key="cast%d_%d" % (l % 2, gi), q="pool")

        CG0, CG1, CG2 = ["w_ada", "w_in"], ["w_pa", "w_pb", "w_pc", "w_out"], ["w_up", "w_down"]
        self.cast = cast
        cast(0, CG0, 0)
        cast(0, CG1, 1)
        cast(0, CG2, 2)

        self.dma(cst[:, :], cst_in[:, :], rd=[], wr=["cst"], key="cst")
        self.dma(cstb[:, :], cstb_in[:, :], rd=[], wr=["cstb"], key="cstb")
        self.dma(cT[:, :], cT_in[:, :], rd=[], wr=["cT"], key="cst")
        self.op("act", lambda e: e.activation(out=cact[:, :], in_=cT[:, :], func=AF.Silu), rd=["cT"], wr=["cact"])
        self.op("pool", lambda e: e.memset(epsc[:, :], LN_EPS), rd=[], wr=["epsc"])
        self.op("pool", lambda e: e.memset(halo[:, :], 0.0), rd=[], wr=["halo"])

        wstate = dict(n=0, queue=[], loaded=0)

        def wplan(lst):
            wstate["queue"].extend(lst)

        def _wload(idx):
            slot = idx % 2
            pieces = wstate["queue"][idx]
            o = 0
            for pi, (src, n, rdk) in enumerate(pieces):
                self.dma(W[slot][:, o:o + n], src, rd=[rdk], wr=[("W%d" % slot, pi)] if len(pieces) > 1 else ["W%d" % slot], key="W%d" % slot)
                o += n

        def wtake():
            idx = wstate["n"]
            while wstate["loaded"] <= min(idx + 1, len(wstate["queue"]) - 1):
                _wload(wstate["loaded"])
                wstate["loaded"] += 1
            wstate["n"] += 1
            return W[idx % 2], "W%d" % (idx % 2)

        def win_piece(l, name, i=0, n=1):
            off, width, nb = WIN_IDX[name]
            o = off + i * KC * width
            return (wb["w_in"][l, :, o:o + n * KC * width], n * KC * width, ("wb_w_in", l))

        def wpiece(wn, l, o, n):
            return (wb[wn][l, :, o:o + n], n, ("wb_" + wn, l))

        ARENA_ELEMS = 34 * 1024
        arena_t = sb("arena", ARENA_ELEMS, BF16)

        class Arena:
            def __init__(s):
                s.off = 0
                s.peak = 0

            def get(s, n, dt):
                nb = n * (2 if dt == F32 else 1)
                nb = (nb + 31) // 32 * 32
                assert s.off + nb <= ARENA_ELEMS, ("arena overflow", s.off, nb)
                v = arena_t[:, s.off:s.off + nb]
                s.off += nb
                s.peak = max(s.peak, s.off)
                if dt == F32:
                    return v.bitcast(F32)[:, 0:n]
                return v[:, 0:n]

            def mark(s):
                return s.off

            def reset(s, mk=0):
                s.off = mk

        ar = Arena()
        self.arena = ar
        RG = [[0, 1, 2, 3], [4, 5, 6, 7]]
        tri = cs("tri")
        onesf = cs("ones")
        causal = cs("causal")
        cosv = cs("cos").rearrange("p (t d) -> p t d", t=16)
        sinv = cs("sin").rearrange("p (t d) -> p t d", t=16)
        kdec = cs("kdec")
        gpow = cs("gpow")
        alpha_c = cs("alpha")
        beta_c = cs("beta")
        sel_c = cs("sel")
        selA_c = cs("selA")
        selB_c = cs("selB")
        sel4 = cs("sel4")
        diagA = csb("diagA").rearrange("p (k q) -> p k q", k=4)
        diagB = csb("diagB").rearrange("p (k q) -> p k q", k=4)

        def agather(src, dst, skey, dkey):
            if "cc" not in self.dma_keys:
                self.dma_keys.append("cc")
            pg.add("pool", lambda e, src=src, dst=dst: e.collective_compute("AllGather", ALU.bypass, replica_groups=RG, ins=[src.ap().opt()], outs=[dst.ap().opt()]),
                   rd=[skey], wr=[dkey], dma="cc", inc=1)

        def vcol(name, j=0, n=1):
            o, w = VEC[name]
            return vec[:, o + j:o + j + n]

        ar.reset()
        xin_ts = [ar.get(D, F32), ar.get(D, F32)]
        for tb in range(NT // P):
            xin_t = xin_ts[tb % 2]
            xk = "xin_t%d" % (tb % 2)
            self.dma(xin_t, x_in[tb * P:(tb + 1) * P, :], rd=[], wr=[xk], key=xk)
            for k2 in range(2):
                bi = (tb * 2 + k2) % 4
                ps = PSb[bi]
                pkey = ("ps", bi)
                for c in range(4):
                    kc = k2 * 4 + c
                    self.op("pe", lambda e, ps=ps, c=c, kc=kc, xin_t=xin_t: e.transpose(out=ps[:, c * P:(c + 1) * P], in_=xin_t[:, kc * P:(kc + 1) * P], identity=ident_f),
                            rd=[xk, "cst"], wr=[pkey])
                outv = xsv[:, k2 * 4:(k2 + 1) * 4, tb * P:(tb + 1) * P]
                inv = ps[:, :].rearrange("p (c t) -> p c t", c=4)
                xkeys = [("xs", (tb // 4) * 8 + k2 * 4 + c) for c in range(4)]
                if k2 == 0:
                    self.op("dve", lambda e, o=outv, i=inv: e.tensor_copy(out=o, in_=i), rd=[pkey], wr=xkeys)
                else:
                    self.op("act", lambda e, o=outv, i=inv: e.copy(out=o, in_=i), rd=[pkey], wr=xkeys)
        pg.barrier()

        def layer_prelude(l):
            pg.barrier()
            ar.reset()
            brf = ar.get(1024, F32)
            brb = ar.get(1024, BF16)
            lamt = ar.get(256, F32)
            lamp = ar.get(128, F32)
            self.dma(vec[:, :], vec_in[l, :, :], rd=[], wr=["vec"], key="vec")
            self.op("pool", lambda e: e.memset(brow[:, :], 0.0), rd=[], wr=["brow"])
            for c0 in range(0, ROW_TOT, 1024):
                c1 = min(ROW_TOT, c0 + 1024)
                n = c1 - c0
                self.dma(brf[0:1, 0:n], rows_in[l, :, c0:c1], rd=[], wr=["brf"], key="brf")
                self.dma(brf[32:33, 0:n], rows_in[l, :, c0:c1], rd=[], wr=["brf"], key="brf")
                self.op("dve", lambda e, c0=c0, c1=c1, n=n: e.tensor_copy(out=brow[0:1, c0:c1], in_=brf[0:1, 0:n]), rd=["brf"], wr=["brow"])
                self.op("dve", lambda e, n=n: e.tensor_copy(out=brb[32:33, 0:n], in_=brf[32:33, 0:n]), rd=["brf"], wr=["brb"])
                self.op("dve", lambda e, c0=c0, c1=c1, n=n: e.tensor_tensor(out=brow[32:33, c0:c1], in0=brf[32:33, 0:n], in1=brb[32:33, 0:n], op=ALU.subtract), rd=["brf", "brb"], wr=["brow"])
            lam_init = 0.8 - 0.6 * math.exp(-0.3 * l)
            self.dma(lamt[:, :], bro_in[l, :, :].partition_broadcast(P)[:, 0, :], rd=[], wr=["lamt"], key="lamt")
            lt3 = lamt[:, :].rearrange("p (a b d) -> p a b d", a=2, b=2)
            lp3 = lamp[:, :].rearrange("p (a d) -> p a d", a=2)
            self.op("dve", lambda e: e.tensor_tensor(out=lp3, in0=lt3[:, :, 0, :], in1=lt3[:, :, 1, :], op=ALU.mult), rd=["lamt"], wr=["lamp"])
            self.op("dve", lambda e: e.reduce_sum(out=lamv[:, 0:2], in_=lp3, axis=AX.X), rd=["lamp"], wr=["lamv"])
            self.op("act", lambda e: e.activation(out=lamv[:, 2:4], in_=lamv[:, 0:2], func=AF.Exp), rd=["lamv"], wr=["lamv"])
            self.op("dve", lambda e: e.tensor_tensor(out=lamv[:, 4:5], in0=lamv[:, 3:4], in1=lamv[:, 2:3], op=ALU.subtract), rd=["lamv"], wr=["lamv"])
            self.op("dve", lambda e: e.tensor_scalar(out=lamv[:, 5:6], in0=lamv[:, 4:5], scalar1=-lam_init, scalar2=None, op0=ALU.add), rd=["lamv"], wr=["lamv"])
            self.op("dve", lambda e: e.tensor_scalar(out=gdf[:, :], in0=vcol("g_diff", 0, 4), scalar1=1.0 - lam_init, scalar2=None, op0=ALU.mult), rd=["vec"], wr=["gdf"])
            ps = PSb[7]
            wplan([[wpiece("w_ada", l, g * 4096, 4096)] for g in range(12)])
            for g in range(12):
                Wt, wk = wtake()
                for jj in range(4):
                    j = g * 4 + jj
                    for kc in range(KC):
                        self.op("pe", lambda e, Wt=Wt, jj=jj, kc=kc, j=j: e.matmul(ps[:, j:j + 1], lhsT=Wt[:, (jj * KC + kc) * 128:(jj * KC + kc + 1) * 128],
                                                                                   rhs=cact[:, kc:kc + 1], start=(kc == 0), stop=(kc == KC - 1)),
                                rd=[wk, "cact"], wr=[("ps", 7)])
            o, w = VEC["b_ada"]
            self.op("dve", lambda e: e.tensor_tensor(out=modT[:, :], in0=ps[:, 0:48], in1=vec[:, o:o + w], op=ALU.add), rd=[("ps", 7), "vec"], wr=["modT"])
            self.op("dve", lambda e: e.tensor_scalar(out=modT[:, 8:16], in0=modT[:, 8:16], scalar1=1.0, scalar2=None, op0=ALU.add), rd=["modT"], wr=["modT"])
            self.op("dve", lambda e: e.tensor_scalar(out=modT[:, 32:40], in0=modT[:, 32:40], scalar1=1.0, scalar2=None, op0=ALU.add), rd=["modT"], wr=["modT"])
            self.debug_dump("modT%d" % l, modT[:, :], [P, 48], F32, rd=["modT"])
            pg.barrier()

        def make_hT(t, which):
            sh0 = 0 if which == 0 else 24
            sc0 = 8 if which == 0 else 32
            for kc in range(KC):
                src = xsv[:, kc, t * TS:(t + 1) * TS]
                dst = hTv[:, kc, :]
                if kc % 2 == 0:
                    self.op("dve", lambda e, s=src, d=dst, kc=kc: e.tensor_scalar(out=d, in0=s, scalar1=modT[:, sc0 + kc:sc0 + kc + 1], scalar2=modT[:, sh0 + kc:sh0 + kc + 1],
                                                                                  op0=ALU.mult, op1=ALU.add), rd=[("xs", t * 8 + kc), "modT"], wr=[("hT", kc)])
                else:
                    self.op("act", lambda e, s=src, d=dst, kc=kc: e.activation(out=d, in_=s, func=AF.Identity, bias=modT[:, sh0 + kc:sh0 + kc + 1], scale=modT[:, sc0 + kc:sc0 + kc + 1]),
                            rd=[("xs", t * 8 + kc), "modT"], wr=[("hT", kc)])

        psrr = dict(i=0, lo=0, hi=4)

        def next_ps():
            i = psrr["lo"] + psrr["i"] % (psrr["hi"] - psrr["lo"])
            psrr["i"] += 1
            return PSb[i], ("ps", i)

        def proj_fm(Wt, wk, woff, bias_col, dst, dst_key, func=None, rhsv=None, rhs_key="hT", nk=KC, rhs_c0=0):
            if rhsv is None:
                rhsv = hTv
            ps, pk = next_ps()
            for kc in range(nk):
                self.op("pe", lambda e, kc=kc, ps=ps: e.matmul(ps[:, :], lhsT=Wt[:, woff + kc * 128:woff + (kc + 1) * 128], rhs=rhsv[:, rhs_c0 + kc, :],
                                                                start=(kc == 0), stop=(kc == nk - 1)), rd=[wk, rhs_key], wr=[pk])
            if dst is None:
                return ps, pk
            f = AF.Identity if func is None else func
            bcol = vcol("b_fm", bias_col)
            self.op("act", lambda e, ps=ps: e.activation(out=dst, in_=ps[:, :], func=f, bias=bcol, scale=1.0), rd=[pk, "vec"], wr=[dst_key])

        def proj_tm(Wt, wk, woff, ncols, tb, rowname, rowoff, ps, pk, pcol=0):
            ro = ROW_OFF[rowname][0] + rowoff
            for kc in range(KC):
                self.op("pe", lambda e, kc=kc: e.matmul(ps[:, pcol:pcol + ncols], lhsT=hTv[:, kc, tb * P:(tb + 1) * P], rhs=Wt[:, woff + kc * ncols:woff + (kc + 1) * ncols],
                                                         start=(kc == 0), stop=False), rd=[wk, "hT"], wr=[pk])
            self.op("pe", lambda e: e.matmul(ps[:, pcol:pcol + ncols], lhsT=ones_b[:, :], rhs=brow[:, ro:ro + ncols], start=False, stop=True),
                    rd=["brow", "cstb"], wr=[pk])

        def rotary(ps, pk, tbg, dst, dst_key, dec, rt1, rt2, krot):
            pv = ps[:, :].rearrange("p (h s d) -> p h s d", h=4, s=2)
            t1 = pv[:, :, 0, :]
            t2 = pv[:, :, 1, :]
            cb = cosv[:, tbg, :].unsqueeze(1).broadcast_to([P, 4, 64])
            sbv = sinv[:, tbg, :].unsqueeze(1).broadcast_to([P, 4, 64])
            a = rt1[:, :].rearrange("p (h d) -> p h d", h=4)
            b = rt2[:, :].rearrange("p (h d) -> p h d", h=4)
            kr = krot[:, :].rearrange("p (h s d) -> p h s d", h=4, s=2)
            self.op("dve", lambda e: e.tensor_tensor(out=a, in0=t1, in1=cb, op=ALU.mult), rd=[pk, "cst"], wr=["rt1"])
            self.op("dve", lambda e: e.tensor_tensor(out=b, in0=t2, in1=sbv, op=ALU.mult), rd=[pk, "cst"], wr=["rt2"])
            self.op("dve", lambda e: e.tensor_tensor(out=kr[:, :, 0, :], in0=a, in1=b, op=ALU.subtract), rd=["rt1", "rt2"], wr=[("krot", 0)])
            self.op("dve", lambda e: e.tensor_tensor(out=a, in0=t1, in1=sbv, op=ALU.mult), rd=[pk, "cst"], wr=["rt1"])
            self.op("dve", lambda e: e.tensor_tensor(out=b, in0=t2, in1=cb, op=ALU.mult), rd=[pk, "cst"], wr=["rt2"])
            self.op("dve", lambda e: e.tensor_tensor(out=kr[:, :, 1, :], in0=a, in1=b, op=ALU.add), rd=["rt1", "rt2"], wr=[("krot", 1)])
            kv3 = krot[:, :].rearrange("p (h d) -> p h d", h=4)
            d3 = dst.rearrange("p (h d) -> p h d", h=4)
            if dec is not None:
                db = dec.unsqueeze(2).broadcast_to([P, 4, 128])
                self.op("dve", lambda e: e.tensor_tensor(out=d3, in0=kv3, in1=db, op=ALU.mult), rd=["krot", "cst"], wr=[dst_key])
            else:
                self.op("act", lambda e: e.copy(out=d3, in_=kv3), rd=["krot"], wr=[dst_key])

        def phaseA(l, t):
            m = t
            ar.reset()
            kTt = ar.get(4 * TS, BF16)
            kTv = kTt[:, :].rearrange("p (h t) -> p h t", h=4)
            kT2 = ar.get(4 * TS, BF16)
            kT2v = kT2[:, :].rearrange("p (h t) -> p h t", h=4)
            Vaug = [ar.get(2048, BF16), ar.get(2048, BF16)]
            Vaugv = [v[:, :].rearrange("p (h t e) -> p h t e", h=4, t=4) for v in Vaug]
            nl = ar.get(16, F32)
            ex = ar.get(16, F32)
            rt1 = ar.get(256, F32)
            rt2 = ar.get(256, F32)
            krot = ar.get(512, F32)
            kinv = ar.get(4 * 512, BF16)
            kinvv = kinv[:, :].rearrange("p (t c) -> p t c", t=4)
            vtm = ar.get(4 * 1024, BF16)
            vtmv = vtm[:, :].rearrange("p (t c) -> p t c", t=4)
            Qst = ar.get(1024, F32)
            Qstv = Qst[:, :].rearrange("p (h e) -> p h e", h=4)
            Lst = ar.get(1024, F32)
            Lstv = Lst[:, :].rearrange("p (h e) -> p h e", h=4)
            vkeys = ["VaugA", "VaugB"]
            psrr["lo"], psrr["hi"] = 0, 4
            make_hT(t, 0)
            if ("hT%d_%d" % (l, t)) in self.dbg:
                self.debug_dump("hT%d_%d" % (l, t), hT[:, :], [P, KC * TS], BF16, rd=["hT"])
            wplan([[win_piece(l, "ak", 0, 4)], [win_piece(l, "bk", 0, 4)], [win_piece(l, "av")], [win_piece(l, "bv")],
                   [win_piece(l, "ck")], [win_piece(l, "cv", 0, 1)], [win_piece(l, "cv", 1, 1)]])
            if t == 0:
                self.dma(Wbf[:, :], win_piece(l, "bf")[0], rd=[("wb_w_in", l)], wr=["Wbf"], key="Wbf")
            Wt, wk = wtake()
            for h in range(4):
                proj_fm(Wt, wk, h * KC * 128, 0 + h, kTv[:, h, :], ("kTt", h))
            self.dma(kK[m][:, 0:2048], kTt[:, :], rd=["kTt"], wr=[("kK", m)], key="kvst")
            Wt, wk = wtake()
            for h in range(4):
                proj_fm(Wt, wk, h * KC * 128, 4 + h, kT2v[:, h, :], ("kT2", h))
            self.dma(kK[m][:, 2048:4096], kT2[:, :], rd=["kT2"], wr=[("kK", m)], key="kvst")
            agather(kK[m], gK[m], ("kK", m), ("gK", m))
            for vi, nm in enumerate(("av", "bv")):
                Wt, wk = wtake()
                vkey = vkeys[vi]
                for tb in range(4):
                    ps, pk = next_ps()
                    proj_tm(Wt, wk, 0, 512, tb, nm, 0, ps, pk)
                    outv = Vaugv[vi][:, :, tb, 0:128]
                    inv = ps[:, :].rearrange("p (h e) -> p h e", h=4)
                    if tb % 2 == 0:
                        self.op("dve", lambda e, o=outv, i=inv: e.tensor_copy(out=o, in_=i), rd=[pk], wr=[(vkey, tb)])
                    else:
                        self.op("act", lambda e, o=outv, i=inv: e.copy(out=o, in_=i), rd=[pk], wr=[(vkey, tb)])
                self.dma(kV[m][:, vi * 2048:(vi + 1) * 2048], Vaug[vi][:, :], rd=[vkey], wr=[("kV", m)], key="kvst")
            agather(kV[m], gV[m], ("kV", m), ("gV", m))
            ps, pk = PSb[4], ("ps", 4)
            ro = ROW_OFF["bf"][0]
            for tb in range(4):
                for kc in range(KC):
                    self.op("pe", lambda e, kc=kc, tb=tb: e.matmul(ps[:, tb * 4:tb * 4 + 4], lhsT=hTv[:, kc, tb * P:(tb + 1) * P], rhs=Wbf[:, kc * 4:kc * 4 + 4],
                                                                    start=(kc == 0), stop=False), rd=["Wbf", "hT"], wr=[pk])
                self.op("pe", lambda e, tb=tb: e.matmul(ps[:, tb * 4:tb * 4 + 4], lhsT=ones_b[:, :], rhs=brow[:, ro:ro + 4], start=False, stop=True),
                        rd=["brow", "cstb"], wr=[pk])
            self.op("act", lambda e: e.activation(out=ex[:, :], in_=ps[:, 0:16], func=AF.Exp, scale=-1.0), rd=[pk], wr=["ex"])
            self.op("act", lambda e: e.activation(out=nl[:, :], in_=ex[:, :], func=AF.Ln, bias=1.0, scale=1.0), rd=["ex"], wr=["nl"])
            ps2, pk2 = PSb[5], ("ps", 5)
            for tb in range(4):
                for tb2 in range(tb + 1):
                    lh = tri if tb2 == tb else onesf
                    self.op("pe", lambda e, tb=tb, tb2=tb2, lh=lh: e.matmul(ps2[:, tb * 4:tb * 4 + 4], lhsT=lh, rhs=nl[:, tb2 * 4:tb2 * 4 + 4], start=(tb2 == 0), stop=(tb2 == tb)),
                            rd=["nl", "cst"], wr=[pk2])
            self.op("dve", lambda e: e.tensor_copy(out=Gloc[:, m * 16:(m + 1) * 16], in_=ps2[:, 0:16]), rd=[pk2], wr=[("Gloc", m)])
            Wck, wkck = wtake()
            for tb in range(4):
                psk, pkk = next_ps()
                proj_tm(Wck, wkck, 0, 512, tb, "ck", 0, psk, pkk)
                rotary(psk, pkk, 4 * m + tb, kinvv[:, tb, :], ("kinv", tb), kdec, rt1, rt2, krot)
            for half in range(2):
                Wcv, wkcv = wtake()
                for tb in range(4):
                    psv, pkv = next_ps()
                    proj_tm(Wcv, wkcv, 0, 512, tb, "cv", half * 512, psv, pkv)
                    outv = vtmv[:, tb, half * 512:(half + 1) * 512]
                    if tb % 2 == 0:
                        self.op("act", lambda e, o=outv, ps=psv: e.copy(out=o, in_=ps[:, :]), rd=[pkv], wr=[("vtm", tb)])
                    else:
                        self.op("dve", lambda e, o=outv, ps=psv: e.tensor_copy(out=o, in_=ps[:, :]), rd=[pkv], wr=[("vtm", tb)])
            for tb in range(4):
                for hp in range(2):
                    psu, pku = PSb[6 + hp], ("ps", 6 + hp)
                    for hh in range(2):
                        h = hp * 2 + hh
                        self.op("pe", lambda e, tb=tb, h=h, hh=hh, psu=psu: e.matmul(psu[:, hh * 256:(hh + 1) * 256], lhsT=kinvv[:, tb, h * 128:(h + 1) * 128],
                                                                                      rhs=vtmv[:, tb, h * 256:(h + 1) * 256], start=True, stop=True),
                                rd=[("kinv", tb), ("vtm", tb)], wr=[pku])
                        if tb == 0:
                            self.op("dve", lambda e, h=h, hh=hh, psu=psu: e.tensor_copy(out=Qstv[:, h, :], in_=psu[:, hh * 256:(hh + 1) * 256]), rd=[pku], wr=[("Qst", h)])
                        else:
                            self.op("dve", lambda e, h=h, hh=hh, psu=psu: e.scalar_tensor_tensor(out=Qstv[:, h, :], in0=Qstv[:, h, :], scalar=G128[h], in1=psu[:, hh * 256:(hh + 1) * 256],
                                                                                                  op0=ALU.mult, op1=ALU.add), rd=[pku, ("Qst", h)], wr=[("Qst", h)])
            for h in range(4):
                self.op("act", lambda e, h=h: e.mul(out=Lstv[:, h, :], in_=Qstv[:, h, :], mul=G128[h]), rd=[("Qst", h)], wr=["Lst"])
            self.dma(sL[m][:, :], Lst[:, :], rd=["Lst"], wr=[("sL", m)], key="smst")
            agather(sL[m], gL[m], ("sL", m), ("gL", m))
            pg.barrier()

        def exchange1():
            self.dma(sG[:, :], Gloc[:, :], rd=["Gloc"], wr=["sG"], key="smst")
            agather(sG, gG, "sG", "gG")

        def phaseB_prelude(l):
            pg.barrier()
            ar.reset()
            Tb = ar.get(256, F32)
            Tbv = Tb[:, :].rearrange("p (r c) -> p r c", r=4)
            Pg = ar.get(1024, F32)
            Lg = [ar.get(1024, F32), ar.get(1024, F32)]
            Gkv = Gk[:, :].rearrange("p (r c) -> p r c", r=4)
            self.dma(Gkv, gG[:, :].rearrange("(r p) c -> p r c", p=P), rd=["gG"], wr=["Gk"], key="Gk")
            for r in range(4):
                self.dma(Tbv[:, r, :], gG[r * P + 127:r * P + 128, :].partition_broadcast(P)[:, 0, :], rd=["gG"], wr=["Tb"], key="Tb")
            offv = off_rm[:, :].rearrange("p (r m h) -> p r m h", r=4, m=4)
            self.op("dve", lambda e: e.memset(off_rm[:, :], 0.0), rd=[], wr=["off"])
            for g in range(15):
                r, m = g % 4, g // 4
                r2, m2 = (g + 1) % 4, (g + 1) // 4
                self.op("dve", lambda e, r=r, m=m, r2=r2, m2=m2: e.tensor_tensor(out=offv[:, r2, m2, :], in0=offv[:, r, m, :], in1=Tbv[:, r, (4 * m + 3) * 4:(4 * m + 3) * 4 + 4], op=ALU.add),
                        rd=["off", "Tb"], wr=["off"])
            Gk5 = Gk[:, :].rearrange("p (r m t h) -> p r m t h", r=4, m=4, t=4)
            for r in range(4):
                self.op("dve", lambda e, r=r: e.tensor_tensor(out=Gk5[:, r], in0=Gk5[:, r], in1=offv[:, r].unsqueeze(2).broadcast_to([P, 4, 4, 4]), op=ALU.add),
                        rd=["Gk", "off"], wr=["Gk"])
            offqv = offq[:, :].rearrange("p (m h) -> p m h", m=4)
            self.op("dve", lambda e: e.tensor_scalar(out=offqv, in0=offv[:, 0], scalar1=sel_c[:, 0:1], scalar2=None, op0=ALU.mult), rd=["off", "cst"], wr=["offq"])
            for r in range(1, 4):
                self.op("dve", lambda e, r=r: e.scalar_tensor_tensor(out=offqv, in0=offv[:, r], scalar=sel_c[:, r:r + 1], in1=offqv, op0=ALU.mult, op1=ALU.add),
                        rd=["off", "cst", "offq"], wr=["offq"])
            Rinv = Rin[:, :].rearrange("p (m c) -> p m c", m=4)
            Pgv = Pg[:, :].rearrange("p (h e) -> p h e", h=4)
            self.op("pool", lambda e: e.memset(Pg[:, :], 0.0), rd=[], wr=["Pg"])
            for g in range(16):
                r, m = g % 4, g // 4
                if r == 0:
                    self.op("dve", lambda e, m=m, r=r: e.tensor_scalar(out=Rinv[:, m, :], in0=Pg[:, :], scalar1=sel_c[:, r:r + 1], scalar2=None, op0=ALU.mult),
                            rd=["Pg", "cst"], wr=[("Rin", m)])
                else:
                    self.op("dve", lambda e, m=m, r=r: e.scalar_tensor_tensor(out=Rinv[:, m, :], in0=Pg[:, :], scalar=sel_c[:, r:r + 1], in1=Rinv[:, m, :], op0=ALU.mult, op1=ALU.add),
                            rd=["Pg", "cst", ("Rin", m)], wr=[("Rin", m)])
                if g < 15:
                    lg = Lg[g % 2]
                    lk = "Lg%d" % (g % 2)
                    self.dma(lg[:, :], gL[m][r * P:(r + 1) * P, :], rd=[("gL", m)], wr=[lk], key=lk)
                    lgv = lg[:, :].rearrange("p (h e) -> p h e", h=4)
                    for h in range(4):
                        self.op("dve", lambda e, h=h, lgv=lgv: e.scalar_tensor_tensor(out=Pgv[:, h, :], in0=Pgv[:, h, :], scalar=G512[h], in1=lgv[:, h, :], op0=ALU.mult, op1=ALU.add),
                                rd=["Pg", lk], wr=["Pg"])
            pg.barrier()

        def phaseB_tile(l, m):
            t = m
            ar.reset()
            yT = ar.get(16 * TS, BF16)
            yTv = yT[:, :].rearrange("p (c t) -> p c t", c=16)
            mkB = ar.mark()
            make_hT(t, 0)
            qT = ar.get(4 * TS, BF16)
            qTv = qT[:, :].rearrange("p (h t) -> p h t", h=4)
            kiT = ar.get(4 * TS, BF16)
            kiTv = kiT[:, :].rearrange("p (h t) -> p h t", h=4)
            kinv = ar.get(4 * 512, BF16)
            kinvv = kinv[:, :].rearrange("p (t c) -> p t c", t=4)
            vtm = ar.get(4 * 1024, BF16)
            vtmv = vtm[:, :].rearrange("p (t c) -> p t c", t=4)
            qrot = [ar.get(512, BF16), ar.get(512, BF16)]
            yctm = [ar.get(1024, BF16), ar.get(1024, BF16)]
            cgT = [ar.get(TS, BF16), ar.get(TS, BF16)]
            Qst = ar.get(1024, F32)
            Qstv = Qst[:, :].rearrange("p (h e) -> p h e", h=4)
            Rb = ar.get(1024, BF16)
            Rbv = Rb[:, :].rearrange("p (h e) -> p h e", h=4)
            ycr = ar.get(1024, F32)
            ycrv = ycr[:, :].rearrange("p (h e) -> p h e", h=4)
            sqb = ar.get(1024, F32)
            sqbv = sqb[:, :].rearrange("p (h e) -> p h e", h=4)
            Pm = ar.get(512, BF16)
            Pmv = Pm[:, :].rearrange("p (h i) -> p h i", h=4)
            st = ar.get(32, F32)
            rt1 = ar.get(256, F32)
            rt2 = ar.get(256, F32)
            krot = ar.get(512, F32)
            psrr["lo"], psrr["hi"] = 0, 2
            wplan([[win_piece(l, "cq")], [win_piece(l, "ck")], [win_piece(l, "cv", 0, 1)], [win_piece(l, "cv", 1, 1)]])
            psT = PSb[7][:, :].bitcast(BF16)
            pkT = ("ps", 7)
            Wt, wk = wtake()
            for tb in range(4):
                ps, pk = next_ps()
                proj_tm(Wt, wk, 0, 512, tb, "cq", 0, ps, pk)
                qr = qrot[tb % 2]
                qk = "qrot%d" % (tb % 2)
                rotary(ps, pk, 4 * m + tb, qr[:, :], qk, None, rt1, rt2, krot)
                for h in range(4):
                    self.op("pe", lambda e, h=h, qr=qr: e.transpose(out=psT[:, h * 128:(h + 1) * 128], in_=qr[:, h * 128:(h + 1) * 128], identity=ident_b),
                            rd=[qk, "cstb"], wr=[pkT])
                self.op("act", lambda e, tb=tb: e.copy(out=qTv[:, :, tb * P:(tb + 1) * P], in_=psT[:, 0:512].rearrange("p (h t) -> p h t", h=4)), rd=[pkT], wr=[("qT", tb)])
            Wt, wk = wtake()
            for tb in range(4):
                ps, pk = next_ps()
                proj_tm(Wt, wk, 0, 512, tb, "ck", 0, ps, pk)
                rotary(ps, pk, 4 * m + tb, kinvv[:, tb, :], ("kinv", tb), kdec, rt1, rt2, krot)
                for h in range(4):
                    self.op("pe", lambda e, h=h, tb=tb: e.transpose(out=psT[:, 512 + h * 128:512 + (h + 1) * 128], in_=kinvv[:, tb, h * 128:(h + 1) * 128], identity=ident_b),
                            rd=[("kinv", tb), "cstb"], wr=[pkT])
                self.op("act", lambda e, tb=tb: e.copy(out=kiTv[:, :, tb * P:(tb + 1) * P], in_=psT[:, 512:1024].rearrange("p (h t) -> p h t", h=4)), rd=[pkT], wr=[("kiT", tb)])
            for half in range(2):
                Wcv, wkcv = wtake()
                for tb in range(4):
                    psv, pkv = next_ps()
                    proj_tm(Wcv, wkcv, 0, 512, tb, "cv", half * 512, psv, pkv)
                    outv = vtmv[:, tb, half * 512:(half + 1) * 512]
                    if tb % 2 == 0:
                        self.op("act", lambda e, o=outv, ps=psv: e.copy(out=o, in_=ps[:, :]), rd=[pkv], wr=[("vtm", tb)])
                    else:
                        self.op("dve", lambda e, o=outv, ps=psv: e.tensor_copy(out=o, in_=ps[:, :]), rd=[pkv], wr=[("vtm", tb)])
            Rinv = Rin[:, :].rearrange("p (m c) -> p m c", m=4)
            self.op("act", lambda e: e.copy(out=Rb[:, :], in_=Rinv[:, m, :]), rd=[("Rin", m)], wr=["Rb"])
            psS, pkS = PSb[4], ("ps", 4)
            psU = [PSb[2], PSb[3]]
            psO = [PSb[5], PSb[6]]
            for tb in range(4):
                tsl = slice(tb * P, (tb + 1) * P)
                for h in range(4):
                    self.op("pe", lambda e, h=h, tsl=tsl: e.matmul(psS[:, h * 128:(h + 1) * 128], lhsT=kiTv[:, h, tsl], rhs=qTv[:, h, tsl], start=True, stop=True),
                            rd=[("kiT", tb), ("qT", tb)], wr=[pkS])
                self.op("dve", lambda e: e.tensor_tensor(out=Pmv, in0=psS[:, :].rearrange("p (h i) -> p h i", h=4), in1=causal.unsqueeze(1).broadcast_to([P, 4, 128]), op=ALU.mult),
                        rd=[pkS, "cst"], wr=["Pm"])
                for h in range(4):
                    po = psO[h // 2]
                    pko = ("ps", 5 + h // 2)
                    osl = slice((h % 2) * 256, (h % 2 + 1) * 256)
                    self.op("pe", lambda e, h=h, po=po, osl=osl, tb=tb: e.matmul(po[:, osl], lhsT=Pmv[:, h, :], rhs=vtmv[:, tb, h * 256:(h + 1) * 256], start=True, stop=False),
                            rd=["Pm", ("vtm", tb)], wr=[pko])
                    self.op("pe", lambda e, h=h, po=po, osl=osl, tsl=tsl: e.matmul(po[:, osl], lhsT=qTv[:, h, tsl], rhs=Rbv[:, h, :], start=False, stop=True),
                            rd=[("qT", tb), "Rb"], wr=[pko])
                    self.op("act", lambda e, h=h, po=po, osl=osl: e.mul(out=ycrv[:, h, :], in_=po[:, osl], mul=gpow[:, h:h + 1]), rd=[pko, "cst"], wr=[("ycr", h)])
                if tb < 3:
                    for h in range(4):
                        pu = psU[h // 2]
                        pku = ("ps", 2 + h // 2)
                        usl = slice((h % 2) * 256, (h % 2 + 1) * 256)
                        self.op("pe", lambda e, h=h, pu=pu, usl=usl, tb=tb: e.matmul(pu[:, usl], lhsT=kinvv[:, tb, h * 128:(h + 1) * 128], rhs=vtmv[:, tb, h * 256:(h + 1) * 256], start=True, stop=True),
                                rd=[("kinv", tb), ("vtm", tb)], wr=[pku])
                        if tb == 0:
                            src = Rinv[:, m, h * 256:(h + 1) * 256]
                            self.op("dve", lambda e, h=h, pu=pu, usl=usl, src=src: e.tensor_tensor(out=Qstv[:, h, :], in0=pu[:, usl], in1=src, op=ALU.add), rd=[pku, ("Rin", m)], wr=[("Qst", h)])
                        else:
                            self.op("dve", lambda e, h=h, pu=pu, usl=usl: e.scalar_tensor_tensor(out=Qstv[:, h, :], in0=Qstv[:, h, :], scalar=G128[h], in1=pu[:, usl], op0=ALU.mult, op1=ALU.add),
                                    rd=[pku, ("Qst", h)], wr=[("Qst", h)])
                        self.op("act", lambda e, h=h: e.mul(out=Rbv[:, h, :], in_=Qstv[:, h, :], mul=G128[h]), rd=[("Qst", h)], wr=["Rb"])
                self.op("dve", lambda e: e.reduce_sum(out=st[:, 0:4], in_=ycrv, axis=AX.X), rd=["ycr"], wr=["st"])
                self.op("act", lambda e: e.activation(out=sqb[:, :], in_=ycr[:, :], func=AF.Square), rd=["ycr"], wr=["sqb"])
                self.op("dve", lambda e: e.reduce_sum(out=st[:, 4:8], in_=sqbv, axis=AX.X), rd=["sqb"], wr=["st"])
                self.op("dve", lambda e: e.tensor_scalar(out=st[:, 8:12], in0=st[:, 0:4], scalar1=1.0 / 256, scalar2=None, op0=ALU.mult), rd=["st"], wr=["st"])
                self.op("dve", lambda e: e.tensor_tensor(out=st[:, 12:16], in0=st[:, 8:12], in1=st[:, 8:12], op=ALU.mult), rd=["st"], wr=["st"])
                self.op("dve", lambda e: e.scalar_tensor_tensor(out=st[:, 16:20], in0=st[:, 4:8], scalar=1.0 / 256, in1=st[:, 12:16], op0=ALU.mult, op1=ALU.subtract), rd=["st"], wr=["st"])
                self.op("act", lambda e: e.activation(out=st[:, 20:24], in_=st[:, 16:20], func=AF.Sqrt, bias=epsc[:, 0:1], scale=1.0), rd=["st", "epsc"], wr=["st"])
                self.op("dve", lambda e: e.reciprocal(out=st[:, 24:28], in_=st[:, 20:24]), rd=["st"], wr=["st"])
                self.op("dve", lambda e: e.tensor_tensor(out=ycrv, in0=ycrv, in1=st[:, 8:12].unsqueeze(2).broadcast_to([P, 4, 256]), op=ALU.subtract), rd=["ycr", "st"], wr=["ycr"])
                yc = yctm[tb % 2]
                yk = "yctm%d" % (tb % 2)
                self.op("dve", lambda e, yc=yc: e.tensor_tensor(out=yc[:, :].rearrange("p (h e) -> p h e", h=4), in0=ycrv, in1=st[:, 24:28].unsqueeze(2).broadcast_to([P, 4, 256]), op=ALU.mult),
                        rd=["ycr", "st"], wr=[yk])
                for ec in range(8):
                    self.op("pe", lambda e, ec=ec, yc=yc: e.transpose(out=psT[:, ec * 128:(ec + 1) * 128], in_=yc[:, ec * 128:(ec + 1) * 128], identity=ident_b), rd=[yk, "cstb"], wr=[pkT])
                self.op("act", lambda e, tsl=tsl: e.copy(out=yTv[:, 8:16, tsl], in_=psT[:, :].rearrange("p (c t) -> p c t", c=8)), rd=[pkT], wr=[("yT", 8 + tb)])
            wplan([[win_piece(l, "cg", 0, 4)], [win_piece(l, "cg", 4, 4)]])
            for half in range(2):
                Wt, wk = wtake()
                for e4 in range(4):
                    ec = half * 4 + e4
                    cg_t = cgT[ec % 2]
                    ck_ = "cgT%d" % (ec % 2)
                    proj_fm(Wt, wk, e4 * KC * 128, 40 + ec, cg_t[:, :], ck_, func=AF.Silu)
                    self.op("dve", lambda e, ec=ec, cg_t=cg_t: e.scalar_tensor_tensor(out=yTv[:, 8 + ec, :], in0=yTv[:, 8 + ec, :], scalar=vcol("g_ret", ec), in1=cg_t[:, :], op0=ALU.mult, op1=ALU.mult),
                            rd=["yT", ck_, "vec"], wr=[("yT", 100 + ec)])
            if ("yc%d_%d" % (l, m)) in self.dbg:
                self.debug_dump("yc%d_%d" % (l, m), yT[:, 8 * TS:16 * TS], [P, 8 * TS], BF16, rd=["yT"])
            pg.barrier()
            if self.stage == "B1":
                raise StopBuild()

            ar.reset(mkB)
            QA = [ar.get(4 * TS, BF16), ar.get(4 * TS, BF16)]
            QAv = [q_[:, :].rearrange("p (h t) -> p h t", h=4) for q_ in QA]
            QB = ar.get(4 * TS, BF16)
            QBv = QB[:, :].rearrange("p (h t) -> p h t", h=4)
            NKV = 3
            Kj = [ar.get(512, BF16) for _ in range(NKV)]
            Vj = [ar.get(512, BF16) for _ in range(NKV)]
            Pt2 = [ar.get(1024, BF16) for _ in range(3)]
            acc = [[ar.get(1024, F32), ar.get(1024, F32)], [ar.get(1024, F32), ar.get(1024, F32)]]
            accb = ar.get(512, BF16)
            alphaI = csb("alphaI").rearrange("p (r c) -> p r c", r=4)
            Growb = ar.get(512, BF16)
            yat = ar.get(512, F32)
            yat2 = ar.get(512, F32)
            rbc = ar.get(512, F32)
            sqb2 = ar.get(512, BF16)
            bK = [ar.get(64, F32), ar.get(64, F32)]
            bKb = [ar.get(16, F32), ar.get(16, F32)]
            sel4b = csb("sel4")
            psrr["lo"], psrr["hi"] = 0, 3
            self.op("pool", lambda e: e.memset(QA[0][64:128, :], 0.0), rd=[], wr=["QA0z"])
            self.op("pool", lambda e: e.memset(QA[1][0:64, :], 0.0), rd=[], wr=["QA1z"])
            wplan([[win_piece(l, "aq", 0, 4)], [win_piece(l, "bq", 0, 4)]])
            Wt, wk = wtake()
            for h in range(4):
                ps, pk = proj_fm(Wt, wk, h * KC * 128, 0, None, None)
                self.op("act", lambda e, ps=ps, h=h: e.activation(out=QAv[0][0:64, h, :], in_=ps[0:64, :], func=AF.Identity, bias=vcol("b_fm", 8 + h)[0:64, :], scale=1.0), rd=[pk, "vec", "QA0z"], wr=[("QA0", h)])
                self.op("act", lambda e, ps=ps, h=h: e.activation(out=QAv[1][64:128, h, :], in_=ps[64:128, :], func=AF.Identity, bias=vcol("b_fm", 8 + h)[64:128, :], scale=1.0), rd=[pk, "vec", "QA1z"], wr=[("QA1", h)])
            Wt, wk = wtake()
            for h in range(4):
                proj_fm(Wt, wk, h * KC * 128, 12 + h, QBv[:, h, :], ("QB", h))
            psG, pkG = PSb[7], ("ps", 7)
            for qb in range(4):
                self.op("pe", lambda e, qb=qb: e.transpose(out=psG[0:4, qb * 128:(qb + 1) * 128], in_=Gloc[:, (4 * m + qb) * 4:(4 * m + qb) * 4 + 4], identity=ident_f),
                        rd=[("Gloc", m), "cst"], wr=[pkG])
            self.op("pool", lambda e: e.memset(Growb[:, :], 0.0), rd=[], wr=["Growb"])
            self.op("dve", lambda e: e.tensor_scalar(out=Growb[0:4, :], in0=psG[0:4, :], scalar1=-1.0 / SCALE_B, scalar2=None, op0=ALU.mult), rd=[pkG, "Growb"], wr=["Growb"])

            jobs = [("A", h, mp) for h in range(4) for mp in range(2)] + [("B", h, 0) for h in range(4)]
            segs = [(m2, r) for m2 in range(m + 1) for r in range(4)]
            nsteps = len(segs) * 4
            cnt = dict(p=0, t=0, kvdma=0)
            seglist = [(ji, m2, r) for ji in range(len(jobs)) for (m2, r) in segs]
            steps = [(ji, sidx, kb) for ji in range(len(jobs)) for sidx in range(len(segs)) for kb in range(4)]
            stinfo = {}

            def jobinfo(ji):
                kind, h, mp = jobs[ji]
                jset = ji % 2
                d = dict(kind=kind, h=h, mp=mp, X=PSb[4 + jset], kX=("ps", 4 + jset), acc=acc[jset], acck="acc%d" % jset)
                if kind == "A":
                    d.update(Qv=QAv[mp], qkey=("QA%d" % mp, h), scale=SCALE_A)
                else:
                    d.update(Qv=QBv, qkey=("QB", h), scale=SCALE_B, bk_=bK[h % 2], bkk="bK%d" % (h % 2), bkb=bKb[h % 2], bkbk="bKb%d" % (h % 2))
                return d

            jinfo = [jobinfo(ji) for ji in range(len(jobs))]

            def kv_dma(k):
                ji, m2, r = seglist[k]
                J = jinfo[ji]
                h = J["h"]
                ks = k % NKV
                kj, vj = Kj[ks], Vj[ks]
                kjk, vjk = "Kj%d" % ks, "Vj%d" % ks
                row0 = 128 * r
                c0 = 0 if J["kind"] == "A" else 2048
                self.dma(kj[:, :], gK[m2][row0:row0 + 128, c0 + h * 512:c0 + (h + 1) * 512], rd=[("gK", m2)], wr=[kjk], key=kjk)
                self.dma(vj[:, :], gV[m2][row0:row0 + 128, c0 + h * 512:c0 + (h + 1) * 512], rd=[("gV", m2)], wr=[vjk], key=vjk)

            def job_setup(ji):
                J = jinfo[ji]
                if J["kind"] != "B":
                    return
                h = J["h"]
                bk_, bkk, bkb, bkbk = J["bk_"], J["bkk"], J["bkb"], J["bkbk"]
                Gk4 = Gk[:, :].rearrange("p (c h) -> p c h", h=4)
                self.op("dve", lambda e, h=h, bk_=bk_: e.tensor_scalar(out=bk_[:, :], in0=Gk4[:, :, h], scalar1=offq[:, m * 4 + h:m * 4 + h + 1], scalar2=None, op0=ALU.subtract),
                        rd=["Gk", "offq"], wr=[bkk])
                bk3 = bk_[:, :].rearrange("p (r c) -> p r c", r=4)
                bkb3 = bkb[:, :].rearrange("p (r c) -> p r c", r=4)
                self.op("dve", lambda e, bk3=bk3, bkb3=bkb3: e.tensor_tensor(out=bkb3, in0=bk3[:, :, 4 * m:4 * m + 4], in1=beta_c.unsqueeze(2).broadcast_to([P, 4, 4]), op=ALU.add),
                        rd=[bkk, "cst"], wr=[bkbk])

            pairs = [(ji, sidx, pi) for ji in range(len(jobs)) for sidx in range(len(segs)) for pi in range(2)]
            npairs_job = len(segs) * 2
            SP = [PT2[0], PT2[1]]
            SPk = [[("ps", 0), ("ps", 1)], [("ps", 2), ("ps", 3)]]

            def front(i):
                ji, sidx, pi = pairs[i]
                J = jinfo[ji]
                kind, h, scale, Qv, qkey = J["kind"], J["h"], J["scale"], J["Qv"], J["qkey"]
                m2, r = segs[sidx]
                k = ji * len(segs) + sidx
                if sidx == 0 and pi == 0:
                    job_setup(ji)
                if pi == 0:
                    while cnt["kvdma"] <= min(k + 1, len(seglist) - 1):
                        kv_dma(cnt["kvdma"])
                        cnt["kvdma"] += 1
                ks = k % NKV
                kj, vj = Kj[ks], Vj[ks]
                kjk, vjk = "Kj%d" % ks, "Vj%d" % ks
                diag = (m2 == m)
                isB = (kind == "B")
                sp = SP[i % 2]
                spk = SPk[i % 2]
                pt = Pt2[cnt["p"] % 3]
                ptk = "Pt%d" % (cnt["p"] % 3)
                cnt["p"] += 1
                stinfo[i] = (pt, ptk, vj, vjk)
                for hf in range(2):
                    kb = 2 * pi + hf
                    psS_ = sp[:, hf * 512:(hf + 1) * 512]
                    pkS_ = spk[hf]
                    self.op("pe", lambda e, psS_=psS_, kj=kj, kb=kb, Qv=Qv, h=h, st1=(not isB and not diag): e.matmul(psS_, lhsT=kj[:, kb * 128:(kb + 1) * 128], rhs=Qv[:, h, :], start=True, stop=st1),
                            rd=[kjk, qkey], wr=[pkS_])
                    if isB:
                        self.op("pe", lambda e, psS_=psS_, h=h, st2=(not diag): e.matmul(psS_, lhsT=sel4b[:, h * 128:(h + 1) * 128], rhs=Growb[:, :], start=False, stop=st2),
                                rd=["Growb", "cstb"], wr=[pkS_])
                    if diag:
                        dg = diagA if kind == "A" else diagB
                        self.op("pe", lambda e, psS_=psS_, kb=kb, r=r, dg=dg: e.matmul(psS_, lhsT=alphaI[:, r, :], rhs=dg[:, kb, :], start=False, stop=True), rd=["cstb"], wr=[pkS_])
                if kind == "A":
                    if diag:
                        self.op("act", lambda e, pt=pt, sp=sp, r=r, sc_=scale: e.activation(out=pt[:, :], in_=sp[:, :], func=AF.Exp, bias=beta_c[:, r:r + 1], scale=sc_), rd=spk + ["cst"], wr=[ptk])
                    else:
                        self.op("act", lambda e, pt=pt, sp=sp, sc_=scale: e.activation(out=pt[:, :], in_=sp[:, :], func=AF.Exp, scale=sc_), rd=spk, wr=[ptk])
                else:
                    for hf in range(2):
                        kb = 2 * pi + hf
                        psS_ = sp[:, hf * 512:(hf + 1) * 512]
                        if diag:
                            bkb, bkbk = J["bkb"], J["bkbk"]
                            bcol = bkb[:, r * 4 + kb:r * 4 + kb + 1]
                            bkey = bkbk
                        else:
                            bk_, bkk = J["bk_"], J["bkk"]
                            bcol = bk_[:, r * 16 + m2 * 4 + kb:r * 16 + m2 * 4 + kb + 1]
                            bkey = bkk
                        self.op("act", lambda e, pt=pt, psS_=psS_, bcol=bcol, sc_=scale, hf=hf: e.activation(out=pt[:, hf * 512:(hf + 1) * 512], in_=psS_, func=AF.Exp, bias=bcol, scale=sc_),
                                rd=[spk[hf], bkey], wr=[(ptk, hf)])

            def back(i):
                ji, sidx, pi = pairs[i]
                J = jinfo[ji]
                X, kX, ac, ack = J["X"], J["kX"], J["acc"], J["acck"]
                pt, ptk, vj, vjk = stinfo.pop(i)
                pj = sidx * 2 + pi
                for hf in range(2):
                    kb = 2 * pi + hf
                    first = (pj == 0 and hf == 0)
                    last = (pj == npairs_job - 1 and hf == 1)
                    self.op("pe", lambda e, X=X, pt=pt, vj=vj, kb=kb, hf=hf, first=first, last=last: e.matmul(X, lhsT=vj[:, kb * 128:(kb + 1) * 128], rhs=pt[:, hf * 512:(hf + 1) * 512], start=first, stop=last),
                            rd=[ptk, vjk], wr=[kX])
                acp = ac[pj % 2]
                ackp = (ack, pj % 2)
                if pj < 2:
                    self.op("dve", lambda e, acp=acp, pt=pt: e.tensor_copy(out=acp[:, :], in_=pt[:, :]), rd=[ptk], wr=[ackp])
                else:
                    self.op("dve", lambda e, acp=acp, pt=pt: e.tensor_tensor(out=acp[:, :], in0=pt[:, :], in1=acp[:, :], op=ALU.add), rd=[ptk, ackp], wr=[ackp])
                if pj == npairs_job - 1:
                    evac(ji)

            psL, pkL = PSb[6], ("ps", 6)
            psQ, pkQ = PSb[6], ("ps", 6)

            def evac(ji):
                J = jinfo[ji]
                kind, h, mp, X, kX, ac, ack = J["kind"], J["h"], J["mp"], J["X"], J["kX"], J["acc"], J["acck"]
                self.op("pool", lambda e, ac=ac: e.tensor_tensor(out=ac[0][:, :], in0=ac[0][:, :], in1=ac[1][:, :], op=ALU.add), rd=[ack], wr=[ack])
                self.op("pool", lambda e, ac=ac: e.tensor_tensor(out=accb[:, :], in0=ac[0][:, 0:512], in1=ac[0][:, 512:1024], op=ALU.add), rd=[ack], wr=["accb"])
                self.op("pe", lambda e: e.matmul(psL, lhsT=ones_b, rhs=accb[:, :], start=True, stop=True), rd=["accb", "cstb"], wr=[pkL])
                self.op("dve", lambda e: e.reciprocal(out=rbc[:, :], in_=psL[:, :]), rd=[pkL], wr=["rbc"])
                if kind == "A" and mp == 0:
                    self.op("dve", lambda e, X=X: e.tensor_tensor(out=yat[:, :], in0=X[:, :], in1=rbc[:, :], op=ALU.mult), rd=[kX, "rbc"], wr=["yat"])
                elif kind == "A":
                    self.op("dve", lambda e, X=X: e.scalar_tensor_tensor(out=yat2[:, :], in0=X[:, :], scalar=lamv[:, 5:6], in1=rbc[:, :], op0=ALU.mult, op1=ALU.mult), rd=[kX, "rbc", "lamv"], wr=["yat2"])
                    self.op("pool", lambda e: e.tensor_tensor(out=yat[:, :], in0=yat[:, :], in1=yat2[:, :], op=ALU.add), rd=["yat", "yat2"], wr=["yat"])
                    self.op("act", lambda e: e.activation(out=sqb2[:, :], in_=yat[:, :], func=AF.Square), rd=["yat"], wr=["sqb2"])
                    self.op("pe", lambda e: e.matmul(psQ[:, :], lhsT=ones_b, rhs=sqb2[:, :], start=True, stop=True), rd=["sqb2", "cstb"], wr=[pkQ])
                    self.op("act", lambda e: e.activation(out=yat2[:, :], in_=psQ[:, :], func=AF.Sqrt, bias=epsc[:, 0:1], scale=1.0 / 128), rd=[pkQ, "epsc"], wr=["yat2"])
                    self.op("dve", lambda e: e.reciprocal(out=yat2[:, :], in_=yat2[:, :]), rd=["yat2"], wr=["yat2"])
                    self.op("dve", lambda e, h=h: e.scalar_tensor_tensor(out=yTv[:, h, :], in0=yat[:, :], scalar=gdf[:, h:h + 1], in1=yat2[:, :], op0=ALU.mult, op1=ALU.mult), rd=["yat", "yat2", "gdf"], wr=[("yT", h)])
                else:
                    self.op("dve", lambda e, X=X, h=h: e.tensor_tensor(out=yTv[:, 4 + h, :], in0=X[:, :], in1=rbc[:, :], op=ALU.mult), rd=[kX, "rbc"], wr=[("yT", 4 + h)])

            LA = 1
            for i in range(len(pairs) + LA):
                if i < len(pairs):
                    front(i)
                if i - LA >= 0:
                    back(i - LA)
            if ("yab%d_%d" % (l, m)) in self.dbg:
                self.debug_dump("yab%d_%d" % (l, m), yT[:, 0:8 * TS], [P, 8 * TS], BF16, rd=["yT"])
            pg.barrier()
            if self.stage == "B2":
                raise StopBuild()

            ar.reset(mkB)
            mT = ar.get(8 * TS, BF16)
            mTv = mT[:, :].rearrange("p (c t) -> p c t", c=8)
            g3 = [ar.get(TS, BF16) for _ in range(3)]
            mt1 = ar.get(TS, F32)
            mt2 = ar.get(TS, F32)
            psrr["lo"], psrr["hi"] = 0, 6
            pieces = []
            o_gt = WIN_IDX["gt"][0]
            for dc in range(8):
                pieces.append([(wb["w_in"][l, :, o_gt + dc * 3072:o_gt + (dc + 1) * 3072], 3072, ("wb_w_in", l)),
                               wpiece("w_pa", l, dc * 512, 512), wpiece("w_pb", l, dc * 512, 512), wpiece("w_pc", l, dc * 1024, 1024)])
            wplan(pieces)
            wplan([[wpiece("w_out", l, 0, 4096)], [wpiece("w_out", l, 4096, 4096)]])
            for dc in range(8):
                Wt, wk = wtake()
                for wh in range(3):
                    proj_fm(Wt, wk, wh * 1024, 16 + dc * 3 + wh, g3[wh][:, :], "g3_%d" % wh, func=AF.Sigmoid)
                psa, pka = proj_fm(Wt, wk, 3072, 0, None, None, rhsv=yTv, rhs_key="yT", nk=4, rhs_c0=0)
                self.op("dve", lambda e, psa=psa: e.tensor_tensor(out=mt1[:, :], in0=g3[0][:, :], in1=psa[:, :], op=ALU.mult), rd=[pka, "g3_0"], wr=["mt1"])
                psb_, pkb = proj_fm(Wt, wk, 3584, 0, None, None, rhsv=yTv, rhs_key="yT", nk=4, rhs_c0=4)
                self.op("dve", lambda e, psb_=psb_: e.tensor_tensor(out=mt2[:, :], in0=g3[1][:, :], in1=psb_[:, :], op=ALU.mult), rd=[pkb, "g3_1"], wr=["mt2"])
                self.op("pool", lambda e: e.tensor_tensor(out=mt1[:, :], in0=mt1[:, :], in1=mt2[:, :], op=ALU.add), rd=["mt1", "mt2"], wr=["mt1"])
                psc, pkc = proj_fm(Wt, wk, 4096, 0, None, None, rhsv=yTv, rhs_key="yT", nk=8, rhs_c0=8)
                self.op("dve", lambda e, psc=psc: e.tensor_tensor(out=mt2[:, :], in0=g3[2][:, :], in1=psc[:, :], op=ALU.mult), rd=[pkc, "g3_2"], wr=["mt2"])
                self.op("pool", lambda e, dc=dc: e.tensor_tensor(out=mTv[:, dc, :], in0=mt1[:, :], in1=mt2[:, :], op=ALU.add), rd=["mt1", "mt2"], wr=[("mT", dc)])
            residual_ln(l, t, mTv, "mT", 8, "w_out_planned", 16, 0)
            pg.barrier()

        def residual_ln(l, t, rhsv, rhs_key, nk, wname, gt0, lnidx):
            tsl = slice(t * TS, (t + 1) * TS)
            xt = xsv[:, :, tsl]
            xall = [("xs", t * 8 + c) for c in range(8)]
            for kc in range(KC):
                self.op("act", lambda e, kc=kc: e.mul(out=xsv[:, kc, tsl], in_=xsv[:, kc, tsl], mul=ALPHA), rd=[("xs", t * 8 + kc)], wr=[("xs", t * 8 + kc)])
            pbanks = [6, 7]
            Wt = wk = None
            for dc in range(8):
                if nk == 8:
                    if dc % 4 == 0:
                        Wt, wk = wtake()
                    woff = (dc % 4) * nk * 128
                else:
                    Wt, wk = wtake()
                    woff = 0
                bi = pbanks[dc % 2]
                ps, pk = PSb[bi], ("ps", bi)
                for kc in range(nk):
                    self.op("pe", lambda e, kc=kc, ps=ps, Wt=Wt, woff=woff: e.matmul(ps[:, :], lhsT=Wt[:, woff + kc * 128:woff + (kc + 1) * 128], rhs=rhsv[:, kc, :], start=(kc == 0), stop=(kc == nk - 1)),
                            rd=[wk, rhs_key], wr=[pk])
                self.op("dve", lambda e, dc=dc, ps=ps: e.scalar_tensor_tensor(out=xsv[:, dc, tsl], in0=ps[:, :], scalar=modT[:, gt0 + dc:gt0 + dc + 1], in1=xsv[:, dc, tsl], op0=ALU.mult, op1=ALU.add),
                        rd=[pk, "modT", ("xs", t * 8 + dc)], wr=[("xs", t * 8 + dc)])
            mk = ar.mark()
            xb = ar.get(8 * TS, BF16)
            xbv = xb[:, :].rearrange("p (c t) -> p c t", c=8)
            sq = ar.get(8 * TS, BF16)
            sqv = sq[:, :].rearrange("p (c t) -> p c t", c=8)
            mean = ar.get(TS, F32)
            msq = ar.get(TS, F32)
            rstd = ar.get(TS, F32)
            tl = [ar.get(TS, F32), ar.get(TS, F32)]
            for kc in range(KC):
                self.op("act", lambda e, kc=kc: e.copy(out=xbv[:, kc, :], in_=xsv[:, kc, tsl]), rd=[("xs", t * 8 + kc)], wr=[("xb", kc)])
                self.op("act", lambda e, kc=kc: e.activation(out=sqv[:, kc, :], in_=xsv[:, kc, tsl], func=AF.Square), rd=[("xs", t * 8 + kc)], wr=[("sq", kc)])
            ps1, pk1 = PSb[4], ("ps", 4)
            ps2, pk2 = PSb[5], ("ps", 5)
            for kc in range(KC):
                self.op("pe", lambda e, kc=kc: e.matmul(ps1[:, :], lhsT=ones_b, rhs=xbv[:, kc, :], start=(kc == 0), stop=(kc == KC - 1)), rd=[("xb", kc), "cstb"], wr=[pk1])
            for kc in range(KC):
                self.op("pe", lambda e, kc=kc: e.matmul(ps2[:, :], lhsT=ones_b, rhs=sqv[:, kc, :], start=(kc == 0), stop=(kc == KC - 1)), rd=[("sq", kc), "cstb"], wr=[pk2])
            self.op("act", lambda e: e.mul(out=mean[:, :], in_=ps1[:, :], mul=1.0 / D), rd=[pk1], wr=["mean"])
            self.op("pool", lambda e: e.tensor_tensor(out=msq[:, :], in0=mean[:, :], in1=mean[:, :], op=ALU.mult), rd=["mean"], wr=["msq"])
            self.op("dve", lambda e: e.scalar_tensor_tensor(out=msq[:, :], in0=ps2[:, :], scalar=1.0 / D, in1=msq[:, :], op0=ALU.mult, op1=ALU.subtract), rd=[pk2, "msq"], wr=["msq"])
            self.op("act", lambda e: e.activation(out=rstd[:, :], in_=msq[:, :], func=AF.Sqrt, bias=epsc[:, 0:1], scale=1.0), rd=["msq", "epsc"], wr=["rstd"])
            self.op("dve", lambda e: e.reciprocal(out=rstd[:, :], in_=rstd[:, :]), rd=["rstd"], wr=["rstd"])
            og, _ = VEC["ln_g"]
            ob, _ = VEC["ln_b"]
            for kc in range(KC):
                tt = tl[kc % 2]
                tk = "tl%d" % (kc % 2)
                self.op("dve", lambda e, kc=kc, tt=tt: e.tensor_tensor(out=tt[:, :], in0=xsv[:, kc, tsl], in1=mean[:, :], op=ALU.subtract), rd=[("xs", t * 8 + kc), "mean"], wr=[tk])
                self.op("pool", lambda e, tt=tt: e.tensor_tensor(out=tt[:, :], in0=tt[:, :], in1=rstd[:, :], op=ALU.mult), rd=[tk, "rstd"], wr=[tk])
                gcol = vec[:, og + lnidx * 8 + kc:og + lnidx * 8 + kc + 1]
                bcol = vec[:, ob + lnidx * 8 + kc:ob + lnidx * 8 + kc + 1]
                self.op("dve", lambda e, kc=kc, tt=tt, gcol=gcol, bcol=bcol: e.tensor_scalar(out=xsv[:, kc, tsl], in0=tt[:, :], scalar1=gcol, scalar2=bcol, op0=ALU.mult, op1=ALU.add),
                        rd=[tk, "vec"], wr=[("xs", t * 8 + kc)])
            ar.reset(mk)

        def exchange2(l):
            ar.reset()
            hh = ar.get(64, BF16)
            hhv = hh[:, :].rearrange("p (c m t) -> p c m t", c=8, m=4)
            for m in range(4):
                for kc in range(KC):
                    src = xsv[:, kc, m * TS + TS - 2:m * TS + TS]
                    self.op("dve", lambda e, src=src, kc=kc, m=m: e.tensor_scalar(out=hhv[:, kc, m, :], in0=src, scalar1=modT[:, 32 + kc:33 + kc], scalar2=modT[:, 24 + kc:25 + kc], op0=ALU.mult, op1=ALU.add),
                            rd=[("xs", m * 8 + kc), "modT"], wr=["hh"])
            self.dma(hls[:, :], hh[:, :], rd=["hh"], wr=["hls"], key="hls")
            agather(hls, hlg, "hls", "hlg")
            hall = ar.get(256, BF16)
            hallv = hall[:, :].rearrange("p (r c m t) -> p r c m t", r=4, c=8, m=4)
            self.dma(hall[:, :].rearrange("p (r x) -> p r x", r=4), hlg[:, :].rearrange("(r p) x -> p r x", p=P), rd=["hlg"], wr=["hall"], key="hall")
            hsf = ar.get(64, F32)
            hsfv = hsf[:, :].rearrange("p (c m t) -> p c m t", c=8, m=4)
            self.op("dve", lambda e: e.tensor_scalar(out=hsfv, in0=hallv[:, 0], scalar1=selA_c[:, 0:1], scalar2=None, op0=ALU.mult), rd=["hall", "cst"], wr=["hsf"])
            for r in range(1, 3):
                self.op("dve", lambda e, r=r: e.scalar_tensor_tensor(out=hsfv, in0=hallv[:, r], scalar=selA_c[:, r:r + 1], in1=hsfv, op0=ALU.mult, op1=ALU.add), rd=["hall", "cst", "hsf"], wr=["hsf"])
            self.op("dve", lambda e: e.scalar_tensor_tensor(out=hsfv[:, :, 1:4, :], in0=hallv[:, 3, :, 0:3, :], scalar=selB_c[:, 3:4], in1=hsfv[:, :, 1:4, :], op0=ALU.mult, op1=ALU.add),
                    rd=["hall", "cst", "hsf"], wr=["hsf"])
            self.op("dve", lambda e: e.tensor_copy(out=halo[:, :], in_=hsf[:, :]), rd=["hsf"], wr=["halo"])
            pg.barrier()

        def phaseC_tile(l, t):
            m = t
            ar.reset()
            aT = ar.get(NFC * TS, BF16)
            aTv = aT[:, :].rearrange("p (c t) -> p c t", c=NFC)
            ub = [ar.get(TS + 2, F32), ar.get(TS + 2, F32)]
            cv_ = [ar.get(TS, F32), ar.get(TS, F32)]
            ge = [ar.get(TS, F32), ar.get(TS, F32)]
            halov = halo[:, :].rearrange("p (c m t) -> p c m t", c=8, m=4)
            make_hT(t, 1)
            psrr["lo"], psrr["hi"] = 0, 4
            wplan([[wpiece("w_up", l, fc * 2048, 4096)] for fc in range(0, NFC, 2)])
            wplan([[wpiece("w_down", l, dc * NFC * 128, NFC * 128)] for dc in range(8)])
            ow, _ = VEC["w_conv"]
            obc, _ = VEC["b_conv"]
            psH, pkH = PSb[4], ("ps", 4)
            Wt = wk = None
            for fc in range(NFC):
                if fc % 2 == 0:
                    Wt, wk = wtake()
                wo = (fc % 2) * 2048
                psu, pku = next_ps()
                psg, pkg = next_ps()
                for kc in range(KC):
                    self.op("pe", lambda e, kc=kc, psu=psu, Wt=Wt, wo=wo: e.matmul(psu[:, :], lhsT=Wt[:, wo + kc * 128:wo + (kc + 1) * 128], rhs=hTv[:, kc, :], start=(kc == 0), stop=(kc == KC - 1)),
                            rd=[wk, "hT"], wr=[pku])
                for kc in range(KC):
                    self.op("pe", lambda e, kc=kc, Wt=Wt, wo=wo, fc=fc: e.matmul(psH[:, fc * 2:fc * 2 + 2], lhsT=Wt[:, wo + kc * 128:wo + (kc + 1) * 128], rhs=halov[:, kc, m, :], start=(kc == 0), stop=(kc == KC - 1)),
                            rd=[wk, "halo"], wr=[pkH])
                for kc in range(KC):
                    self.op("pe", lambda e, kc=kc, psg=psg, Wt=Wt, wo=wo: e.matmul(psg[:, :], lhsT=Wt[:, wo + 1024 + kc * 128:wo + 1024 + (kc + 1) * 128], rhs=hTv[:, kc, :], start=(kc == 0), stop=(kc == KC - 1)),
                            rd=[wk, "hT"], wr=[pkg])
                u = ub[fc % 2]
                uk = "ub%d" % (fc % 2)
                c_ = cv_[fc % 2]
                ck_ = "cv%d" % (fc % 2)
                g_ = ge[fc % 2]
                gk_ = "ge%d" % (fc % 2)
                self.op("act", lambda e, u=u, psu=psu: e.copy(out=u[:, 2:TS + 2], in_=psu[:, :]), rd=[pku], wr=[(uk, 1)])
                self.op("dve", lambda e, u=u, fc=fc: e.tensor_copy(out=u[:, 0:2], in_=psH[:, fc * 2:fc * 2 + 2]), rd=[pkH], wr=[(uk, 0)])
                w0 = vec[:, ow + 0 * NFC + fc:ow + 0 * NFC + fc + 1]
                w1 = vec[:, ow + 1 * NFC + fc:ow + 1 * NFC + fc + 1]
                w2 = vec[:, ow + 2 * NFC + fc:ow + 2 * NFC + fc + 1]
                bc = vec[:, obc + fc:obc + fc + 1]
                self.op("dve", lambda e, u=u, c_=c_, w0=w0, bc=bc: e.tensor_scalar(out=c_[:, :], in0=u[:, 0:TS], scalar1=w0, scalar2=bc, op0=ALU.mult, op1=ALU.add), rd=[uk, "vec"], wr=[ck_])
                self.op("dve", lambda e, u=u, c_=c_, w1=w1: e.scalar_tensor_tensor(out=c_[:, :], in0=u[:, 1:TS + 1], scalar=w1, in1=c_[:, :], op0=ALU.mult, op1=ALU.add), rd=[uk, "vec", ck_], wr=[ck_])
                self.op("dve", lambda e, u=u, c_=c_, w2=w2: e.scalar_tensor_tensor(out=c_[:, :], in0=u[:, 2:TS + 2], scalar=w2, in1=c_[:, :], op0=ALU.mult, op1=ALU.add), rd=[uk, "vec", ck_], wr=[ck_])
                self.op("act", lambda e, c_=c_, g_=g_: e.activation(out=g_[:, :], in_=c_[:, :], func=AF.Gelu), rd=[ck_], wr=[gk_])
                self.op("dve", lambda e, g_=g_, psg=psg, fc=fc: e.tensor_tensor(out=aTv[:, fc, :], in0=g_[:, :], in1=psg[:, :], op=ALU.mult), rd=[gk_, pkg], wr=[("aT", fc)])
            residual_ln(l, t, aTv, "aT", NFC, "w_down", 40, 1)
            pg.barrier()

        def stop(tag):
            if self.stage == tag:
                raise StopBuild()

        try:
            for l in range(L):
                layer_prelude(l)
                for t in range(NSEG):
                    phaseA(l, t)
                stop("A")
                exchange1()
                stop("X1")
                self.cast(l + 1, CG0, 0)
                phaseB_prelude(l)
                stop("P")
                for t in range(NSEG):
                    phaseB_tile(l, t)
                    if t == 1:
                        self.cast(l + 1, CG1, 1)
                self.cast(l + 1, CG2, 2)
                stop("B")
                exchange2(l)
                stop("X2")
                for t in range(NSEG):
                    phaseC_tile(l, t)
        except StopBuild:
            pg.barrier()

        pg.barrier()
        ar.reset()
        yo = [ar.get(D, F32), ar.get(D, F32)]
        for tb in range(NT // P):
            yt = yo[tb % 2]
            yk = "yo%d" % (tb % 2)
            for k2 in range(2):
                bi = (tb * 2 + k2) % 4
                ps = PSb[bi]
                pkey = ("ps", bi)
                for c in range(4):
                    kc = k2 * 4 + c
                    self.op("pe", lambda e, ps=ps, c=c, kc=kc, tb=tb: e.transpose(out=ps[:, c * P:(c + 1) * P], in_=xsv[:, kc, tb * P:(tb + 1) * P], identity=ident_f),
                            rd=[("xs", (tb // 4) * 8 + kc), "cst"], wr=[pkey])
                if k2 == 0:
                    self.op("dve", lambda e, yt=yt, ps=ps: e.tensor_copy(out=yt[:, 0:512], in_=ps[:, :]), rd=[pkey], wr=[(yk, 0)])
                else:
                    self.op("act", lambda e, yt=yt, ps=ps: e.copy(out=yt[:, 512:1024], in_=ps[:, :]), rd=[pkey], wr=[(yk, 1)])
            self.dma(y_out[tb * P:(tb + 1) * P, :], yt[:, :], rd=[yk], wr=["y"], key="yout")

        pg.resolve()
        sems = {}
        for e in Prog.ENGS:
            sems[("eng", e)] = self.es.enter_context(nc.semaphore("s_" + e))
        for k in self.dma_keys:
            sems[("dma", k)] = self.es.enter_context(nc.semaphore("d_" + k))
        with nc.Block() as block:
            pg.emit(nc, block, sems)
        self.es.close()
        return nc


def make_in_maps(inputs, L=DEPTH):
    hl = host_layout(inputs, L)
    x = np.asarray(inputs["x"], np.float32)
    c = np.asarray(inputs["c"], np.float32)
    maps = []
    for r in range(8):
        b, j = r // 4, r % 4
        toks = np.concatenate([np.arange((j + 4 * m) * TS, (j + 4 * m + 1) * TS) for m in range(NSEG)])
        cst, cstb = host_constants(j)
        m = {"x": np.ascontiguousarray(x[b][toks]), "cT": np.ascontiguousarray(c[b].reshape(KC, P).T), "cst": cst, "cstb": cstb,
             "vec": hl["vec"], "rows": hl["rows"], "bro": hl["bro"]}
        for n in WSHAPES:
            m[n] = hl[n]
        maps.append(m)
    return maps


def kernel(**inputs):
    inputs = {k: np.asarray(v) for k, v in inputs.items()}
    b = Builder()
    nc = b.build()
    maps = make_in_maps(inputs)
    res = run_bass_kernel_spmd(nc, maps, core_ids=list(range(8)))
    out = np.zeros((2, 8192, D), np.float32)
    for r in range(8):
        bb, j = r // 4, r % 4
        y = res.results[r]["y"]
        for m in range(NSEG):
            g = j + 4 * m
            out[bb, g * TS:(g + 1) * TS] = y[m * TS:(m + 1) * TS]
    return out
```
